# Optimizing a Trainium2 kernel written in Bass

```python
import math
import jax
import jax.numpy as jnp
from jax import lax
import numpy as np

D_MODEL = 1024
BATCH = 8
SEQ = 4096
DEPTH = 1

PLE_DIM = 256
GMLP_WIDTH = 1024
GMLP_GROUPS = 8
GMLP_CHUNK = 128
MOBA_HEADS = 8
MOBA_HEAD_DIM = 128
MOBA_WIDTH = MOBA_HEADS * MOBA_HEAD_DIM
MOBA_BLOCK = 256
MOBA_TOPK = 3
MOBA_QCHUNK = 32
REL_BUCKETS = 32
REL_MAX_DISTANCE = 1024
PEER_HEADS = 8
PEER_N_KEYS = 128
PEER_N_EXPERTS = PEER_N_KEYS * PEER_N_KEYS
PEER_TOPK = 16
PEER_KEY_DIM = 256
PEER_KEY_HALF = PEER_KEY_DIM // 2
PEER_TOK_CHUNK = 128
IN_WIDTHS = (GMLP_WIDTH, GMLP_WIDTH, MOBA_WIDTH, MOBA_WIDTH, MOBA_WIDTH, D_MODEL, D_MODEL)
IN_TOTAL = sum(IN_WIDTHS)
LN_EPS = 1e-5
NEG_INF = -1e30

kernel_name = 'hybrid_gmlp_moba_peer_block'


def layer_norm(x, g, b):
    xf = x.astype(jnp.float32)
    mu = jnp.mean(xf, axis=-1, keepdims=True)
    var = jnp.mean(jnp.square(xf - mu), axis=-1, keepdims=True)
    y = (xf - mu) * lax.rsqrt(var + LN_EPS) * g.astype(jnp.float32) + b.astype(jnp.float32)
    return y.astype(x.dtype)


def t5_bucket(dist):
    n = jnp.maximum(dist, 0)
    max_exact = REL_BUCKETS // 2
    nf = jnp.maximum(n, max_exact).astype(jnp.float32)
    large = max_exact + (jnp.log(nf / max_exact) / math.log(REL_MAX_DISTANCE / max_exact)
                         * (REL_BUCKETS - max_exact)).astype(jnp.int32)
    large = jnp.minimum(large, REL_BUCKETS - 1)
    return jnp.where(n < max_exact, n, large)


def gmlp_spatial_gating(u, v, ln_g, ln_b, w_s, b_s):
    bsz, seq, width = v.shape
    v = layer_norm(v, ln_g, ln_b)
    vc = v.reshape(bsz, seq // GMLP_CHUNK, GMLP_CHUNK, GMLP_GROUPS, width // GMLP_GROUPS)
    w_causal = jnp.tril(w_s)
    s = jnp.einsum('gts,bnsgc->bntgc', w_causal, vc) + b_s.T[None, None, :, :, None]
    return u * s.reshape(bsz, seq, width)


def moba_attention(q, k, v, rel_bias):
    bsz, seq, nh, hd = q.shape
    nb = -(-seq // MOBA_BLOCK)
    s_pad = nb * MOBA_BLOCK
    pad = ((0, 0), (0, s_pad - seq), (0, 0), (0, 0))
    q, k, v = (jnp.pad(t, pad).transpose(0, 2, 1, 3) for t in (q, k, v))
    kb = k.reshape(bsz, nh, nb, MOBA_BLOCK, hd)
    vb = v.reshape(bsz, nh, nb, MOBA_BLOCK, hd)
    scale = hd ** -0.5

    k_mean = jnp.mean(kb.astype(jnp.float32), axis=3).astype(q.dtype)
    gate = jnp.einsum('bhtd,bhnd->bhtn', q, k_mean).astype(jnp.float32)
    q_blk = jnp.arange(s_pad) // MOBA_BLOCK
    past = jnp.arange(nb)[None, :] < q_blk[:, None]
    gate = jnp.where(past, gate, NEG_INF)
    n_sel = min(MOBA_TOPK, nb)
    _, sel = lax.top_k(gate, n_sel)
    sel_valid = jnp.arange(n_sel)[None, :] < q_blk[:, None]

    n_chunks = s_pad // MOBA_QCHUNK
    q_c = q.reshape(bsz, nh, n_chunks, MOBA_QCHUNK, hd).transpose(2, 0, 1, 3, 4)
    sel_c = sel.reshape(bsz, nh, n_chunks, MOBA_QCHUNK, n_sel).transpose(2, 0, 1, 3, 4)
    valid_c = sel_valid.reshape(n_chunks, MOBA_QCHUNK, n_sel)
    b_idx = jnp.arange(bsz)[:, None, None]
    h_idx = jnp.arange(nh)[None, :, None]
    bias_h = rel_bias.T.astype(jnp.float32)
    offs = jnp.arange(MOBA_BLOCK)

    def chunk_attend(args):
        c, qc, selc, validc = args
        t = c * MOBA_QCHUNK + jnp.arange(MOBA_QCHUNK)
        own = (c * MOBA_QCHUNK) // MOBA_BLOCK
        k_own = lax.dynamic_index_in_dim(kb, own, axis=2, keepdims=False)
        v_own = lax.dynamic_index_in_dim(vb, own, axis=2, keepdims=False)
        d_own = t[:, None] - (own * MOBA_BLOCK + offs)[None, :]
        l_own = (jnp.einsum('bhqd,bhkd->bhqk', qc, k_own).astype(jnp.float32) * scale
                 + bias_h[:, t5_bucket(d_own)])
        logits = [jnp.where(d_own >= 0, l_own, NEG_INF)]
        for r in range(n_sel):
            idx = selc[..., r]
            k_g = kb[b_idx, h_idx, idx]
            d_r = t[:, None] - (idx[..., None] * MOBA_BLOCK + offs)
            l_r = (jnp.einsum('bhqd,bhqkd->bhqk', qc, k_g).astype(jnp.float32) * scale
                   + bias_h[h_idx[..., None], t5_bucket(d_r)])
            logits.append(jnp.where(validc[None, None, :, r, None], l_r, NEG_INF))
        w = jax.nn.softmax(jnp.concatenate(logits, axis=-1), axis=-1).astype(qc.dtype)
        out = jnp.einsum('bhqk,bhkd->bhqd', w[..., :MOBA_BLOCK], v_own)
        for r in range(n_sel):
            v_g = vb[b_idx, h_idx, selc[..., r]]
            w_r = w[..., (r + 1) * MOBA_BLOCK:(r + 2) * MOBA_BLOCK]
            out = out + jnp.einsum('bhqk,bhqkd->bhqd', w_r, v_g)
        return out

    out = lax.map(chunk_attend, (jnp.arange(n_chunks), q_c, sel_c, valid_c))
    out = out.transpose(1, 0, 3, 2, 4).reshape(bsz, s_pad, nh * hd)
    return out[:, :seq]


def peer_ffn(h, w_q, sub_keys, u_tab, v_tab):
    bsz, seq, d = h.shape
    n_tok = bsz * seq
    xt = h.reshape(n_tok, d)
    q = (xt @ w_q).reshape(n_tok, PEER_HEADS, 2, PEER_KEY_HALF)
    s = jnp.einsum('thpk,pnk->thpn', q, sub_keys).astype(jnp.float32)
    top_s, top_i = lax.top_k(s, PEER_TOPK)
    n_cand = PEER_TOPK * PEER_TOPK
    cand_s = (top_s[:, :, 0, :, None] + top_s[:, :, 1, None, :]).reshape(n_tok, PEER_HEADS, n_cand)
    cand_i = (top_i[:, :, 0, :, None] * PEER_N_KEYS + top_i[:, :, 1, None, :]).reshape(n_tok, PEER_HEADS, n_cand)
    best_s, best_pos = lax.top_k(cand_s, PEER_TOPK)
    experts = jnp.take_along_axis(cand_i, best_pos, axis=-1)
    gates = jax.nn.softmax(best_s, axis=-1).astype(h.dtype)
    n_chunks = n_tok // PEER_TOK_CHUNK

    def chunk_ffn(args):
        xc, ec, gc = args
        hid = jax.nn.gelu(jnp.einsum('chkd,cd->chk', u_tab[ec], xc), approximate=False)
        return jnp.einsum('chk,chkd->cd', gc * hid, v_tab[ec])

    y = lax.map(chunk_ffn, (xt.reshape(n_chunks, PEER_TOK_CHUNK, d),
                            experts.reshape(n_chunks, PEER_TOK_CHUNK, PEER_HEADS, PEER_TOPK),
                            gates.reshape(n_chunks, PEER_TOK_CHUNK, PEER_HEADS, PEER_TOPK)))
    return y.reshape(bsz, seq, d)


def setup_inputs(seed: int = 0) -> dict:
    key = jax.random.key(seed)
    ks = jax.random.split(key, 24)
    f32 = jnp.float32
    beta = (8.0 * DEPTH) ** -0.25

    def nrm(k, shape, s):
        return jax.random.normal(k, shape, f32) * s

    v_off = 2 * GMLP_WIDTH + 2 * MOBA_WIDTH
    col_scale = jnp.ones((IN_TOTAL,), f32).at[v_off:v_off + MOBA_WIDTH].set(beta)
    return {
        'x': nrm(ks[0], (BATCH, SEQ, D_MODEL), 1.0),
        'p': nrm(ks[1], (DEPTH, BATCH, SEQ, PLE_DIM), 1.0),
        'w_in': nrm(ks[2], (DEPTH, D_MODEL, IN_TOTAL), D_MODEL ** -0.5) * col_scale,
        'b_in': nrm(ks[3], (DEPTH, IN_TOTAL), 0.02),
        'gmlp_ln_g': 1.0 + nrm(ks[4], (DEPTH, GMLP_WIDTH), 0.02),
        'gmlp_ln_b': nrm(ks[5], (DEPTH, GMLP_WIDTH), 0.02),
        'gmlp_w_s': nrm(ks[6], (DEPTH, GMLP_GROUPS, GMLP_CHUNK, GMLP_CHUNK), GMLP_CHUNK ** -0.5),
        'gmlp_b_s': 1.0 + nrm(ks[7], (DEPTH, GMLP_GROUPS, GMLP_CHUNK), 0.02),
        'w_proj_a': nrm(ks[8], (DEPTH, GMLP_WIDTH, D_MODEL), beta * GMLP_WIDTH ** -0.5),
        'w_proj_b': nrm(ks[9], (DEPTH, MOBA_WIDTH, D_MODEL), beta * MOBA_WIDTH ** -0.5),
        'w_out': nrm(ks[10], (DEPTH, D_MODEL, D_MODEL), beta * D_MODEL ** -0.5),
        'ln1_g': 1.0 + nrm(ks[11], (DEPTH, D_MODEL), 0.02),
        'ln1_b': nrm(ks[12], (DEPTH, D_MODEL), 0.02),
        'rel_bias': nrm(ks[13], (REL_BUCKETS, MOBA_HEADS), 0.5),
        'peer_w_q': nrm(ks[14], (DEPTH, D_MODEL, PEER_HEADS * PEER_KEY_DIM), D_MODEL ** -0.5),
        'peer_sub_keys': nrm(ks[15], (DEPTH, 2, PEER_N_KEYS, PEER_KEY_HALF), PEER_KEY_HALF ** -0.5),
        'peer_u': nrm(ks[16], (DEPTH, PEER_N_EXPERTS, D_MODEL), D_MODEL ** -0.5),
        'peer_v': nrm(ks[17], (DEPTH, PEER_N_EXPERTS, D_MODEL), beta * PEER_HEADS ** -0.5),
        'ple_w_proj': nrm(ks[18], (DEPTH, PLE_DIM, D_MODEL), beta * PLE_DIM ** -0.5),
        'ple_w_gate': nrm(ks[19], (DEPTH, D_MODEL, D_MODEL), D_MODEL ** -0.5),
        'ple_b_gate': nrm(ks[20], (DEPTH, D_MODEL), 0.02),
        'ln2_g': 1.0 + nrm(ks[21], (DEPTH, D_MODEL), 0.02),
        'ln2_b': nrm(ks[22], (DEPTH, D_MODEL), 0.02),
    }


def reference(x, p, w_in, b_in, gmlp_ln_g, gmlp_ln_b, gmlp_w_s, gmlp_b_s, w_proj_a, w_proj_b,
              w_out, ln1_g, ln1_b, rel_bias, peer_w_q, peer_sub_keys, peer_u, peer_v,
              ple_w_proj, ple_w_gate, ple_b_gate, ln2_g, ln2_b):
    alpha = (2.0 * DEPTH) ** 0.25
    bsz, seq, _ = x.shape
    splits = [sum(IN_WIDTHS[:j + 1]) for j in range(len(IN_WIDTHS) - 1)]
    for i in range(DEPTH):
        z = x @ w_in[i] + b_in[i]
        u_a, v_a, q_b, k_b, v_b, g_a, g_b = jnp.split(z, splits, axis=-1)
        a = gmlp_spatial_gating(jax.nn.gelu(u_a, approximate=False), jax.nn.gelu(v_a, approximate=False),
                                gmlp_ln_g[i], gmlp_ln_b[i], gmlp_w_s[i], gmlp_b_s[i])
        head_shape = (bsz, seq, MOBA_HEADS, MOBA_HEAD_DIM)
        o = moba_attention(q_b.reshape(head_shape), k_b.reshape(head_shape), v_b.reshape(head_shape), rel_bias)
        m = jax.nn.sigmoid(g_a) * (a @ w_proj_a[i]) + jax.nn.sigmoid(g_b) * (o @ w_proj_b[i])
        h = layer_norm(alpha * x + m @ w_out[i], ln1_g[i], ln1_b[i])
        f = peer_ffn(h, peer_w_q[i], peer_sub_keys[i], peer_u[i], peer_v[i])
        e = (p[i] @ ple_w_proj[i]) * jax.nn.sigmoid(h @ ple_w_gate[i] + ple_b_gate[i])
        x = layer_norm(alpha * h + f + e, ln2_g[i], ln2_b[i])
    return x
```

```python
import contextlib
import numpy as np
import concourse.bass as bass
import concourse.mybir as mybir
from concourse.bass_utils import run_bass_kernel_spmd

F32 = mybir.dt.float32
BF16 = mybir.dt.bfloat16
AF = mybir.ActivationFunctionType
ALU = mybir.AluOpType
AX = mybir.AxisListType


class Buf:
    __slots__ = ("name", "ap", "lw", "rd", "dsem", "dcount")

    def __init__(self, name, ap=None):
        self.name = name
        self.ap = ap
        self.lw = None
        self.rd = []
        self.dsem = None
        self.dcount = 0


class Slot:
    __slots__ = ("dsem", "dcount")

    def __init__(self):
        self.dsem = None
        self.dcount = 0


class _Op:
    __slots__ = ("emit", "waits", "signal", "known", "dma_buf", "dma_val")

    def __init__(self, emit):
        self.emit = emit
        self.waits = []
        self.signal = False
        self.known = None
        self.dma_buf = None
        self.dma_val = 0


class Prog:
    ENG = {"pe": "tensor", "act": "scalar", "dve": "vector", "pool": "gpsimd", "sync": "sync"}
    ALIAS = {"gpsimd": "pool", "scalar": "act", "vector": "dve", "tensor": "pe"}

    def __init__(self, nc):
        self.nc = nc
        self.ops = {k: [] for k in self.ENG}
        self.known = {k: {} for k in self.ENG}
        self.stack = contextlib.ExitStack()
        self.dbufs = []
        self.dummy = Buf("dummy")
        self.n_ops = 0
        self.free_slots = []
        self.live = []

    def sbuf(self, name, shape, dtype):
        t = self.stack.enter_context(self.nc.sbuf_tensor(name, shape, dtype))
        return Buf(name, t)

    def psum(self, name, shape, dtype):
        t = self.stack.enter_context(self.nc.psum_tensor(name, shape, dtype))
        return Buf(name, t)

    def dram_buf(self, name):
        return Buf(name)

    def view(self, name, ap):
        return Buf(name, ap)

    def _deps(self, reads, writes):
        deps = []
        for b in reads:
            if b.lw is not None:
                deps.append(b.lw)
        for b in writes:
            if b.lw is not None:
                deps.append(b.lw)
            deps.extend(b.rd)
        return deps

    def _apply_waits(self, eng, op, deps):
        known = self.known[eng]
        changed = False
        for ev in deps:
            if ev[0] == "e":
                _, e2, idx = ev
                if e2 == eng and eng == "pe":
                    continue
                key = e2
                if known.get(key, -1) >= idx:
                    continue
                if not changed:
                    known = dict(known)
                    changed = True
                if e2 != eng or True:
                    op.waits.append(ev)
                    self.ops[e2][idx].signal = True
                known[key] = idx
                k2 = self.ops[e2][idx].known
                if k2:
                    for kk, vv in k2.items():
                        if known.get(kk, -1) < vv:
                            known[kk] = vv
            else:
                _, b, val = ev
                key = ("d", id(b))
                if known.get(key, -1) >= val:
                    continue
                if not changed:
                    known = dict(known)
                    changed = True
                op.waits.append(ev)
                known[key] = val
        self.known[eng] = known
        op.known = known

    def _commit(self, ev, reads, writes):
        for b in reads:
            b.rd.append(ev)
        for b in writes:
            b.lw = ev
            b.rd = []

    def op(self, eng, emit, reads=(), writes=()):
        eng = self.ALIAS.get(eng, eng)
        o = _Op(emit)
        self._apply_waits(eng, o, self._deps(reads, writes))
        idx = len(self.ops[eng])
        self.ops[eng].append(o)
        self._commit(("e", eng, idx), reads, writes)
        self.n_ops += 1
        return o

    def dma(self, q, out_ap, in_ap, reads=(), writes=(), sem_buf=None, **kw):
        q = self.ALIAS.get(q, q)
        if sem_buf is None:
            for b in list(writes) + list(reads):
                if b.ap is not None:
                    sem_buf = b
                    break
            else:
                sem_buf = self.dummy
        o = _Op(lambda e: e.dma_start(out=out_ap, in_=in_ap, **kw))
        self._apply_waits(q, o, self._deps(reads, writes))
        if sem_buf.dsem is None:
            if self.free_slots:
                sem_buf.dsem = self.free_slots.pop()
            else:
                sem_buf.dsem = Slot()
                self.dbufs.append(sem_buf.dsem)
            self.live.append(sem_buf)
        sl = sem_buf.dsem
        sl.dcount += 16
        o.dma_buf = sl
        o.dma_val = sl.dcount
        self.ops[q].append(o)
        self._commit(("d", sl, sl.dcount), reads, writes)
        self.n_ops += 1
        return o

    def finish(self, out_bufs):
        nc = self.nc
        fin = _Op(None)
        self._apply_waits("sync", fin, self._deps(out_bufs, ()))
        self.ops["sync"].append(fin)
        st = self.stack
        esem = {}
        for k in self.ENG:
            if any(o.signal for o in self.ops[k]):
                esem[k] = st.enter_context(nc.semaphore("e_" + k))
        for i, b in enumerate(self.dbufs):
            b.dsem = st.enter_context(nc.semaphore("d_%d" % i))
        sval = {}
        for k, ops in self.ops.items():
            c = 0
            for i, o in enumerate(ops):
                if o.signal:
                    c += 1
                    sval[(k, i)] = c
        self.max_sval = max(sval.values()) if sval else 0

        def run(k, eng):
            for i, o in enumerate(self.ops[k]):
                for ev in o.waits:
                    if ev[0] == "e":
                        eng.wait_ge(esem[ev[1]], sval[(ev[1], ev[2])])
                    else:
                        eng.wait_ge(ev[1].dsem, ev[2])
                if o.emit is None:
                    continue
                ins = o.emit(eng)
                if o.dma_buf is not None:
                    ins.then_inc(o.dma_buf.dsem, 16)
                elif o.signal:
                    ins.then_inc(esem[k], 1)

        block = st.enter_context(nc.Block())

        @block.sync
        def _(e):
            run("sync", e)

        @block.tensor
        def _(e):
            run("pe", e)

        @block.scalar
        def _(e):
            run("act", e)

        @block.vector
        def _(e):
            run("dve", e)

        @block.gpsimd
        def _(e):
            run("pool", e)

        st.close()


def _esize(dt):
    return 2 if dt == BF16 else 4


class Arena:
    def __init__(self, P, words):
        self.P = P
        self.words = words
        self.t = P.stack.enter_context(P.nc.sbuf_tensor("arena", [128, words], F32))
        self.off = 0

    def reset(self):
        self.off = 0

    def alloc(self, name, shape, dtype=F32):
        n = 1
        for s in shape[1:]:
            n *= s
        words = (n * _esize(dtype) + 3) // 4
        assert self.off + words <= self.words, (name, self.off, words, self.words)
        ap = self.t[0:shape[0], self.off:self.off + words]
        self.off += words
        if dtype == BF16:
            ap = ap.bitcast(dtype)[:, 0:n]
        elif dtype != F32:
            ap = ap.bitcast(dtype)
        if len(shape) > 2:
            names = ["a%d" % i for i in range(len(shape) - 1)]
            pat = "p (" + " ".join(names) + ") -> p " + " ".join(names)
            ap = ap.rearrange(pat, **{nm: s for nm, s in zip(names, shape[1:])})
        return Buf(name, ap)


def _barrier(self):
    evs = []
    for k in ("pe", "act", "dve", "pool"):
        ops = self.ops[k]
        for i in range(len(ops) - 1, -1, -1):
            if ops[i].dma_buf is None and ops[i].emit is not None:
                evs.append(("e", k, i))
                break
    for b in self.dbufs:
        evs.append(("d", b, b.dcount))
    self.pending = {k: list(evs) for k in self.ENG}
    for b in self.live:
        if b is not self.dummy:
            self.free_slots.append(b.dsem)
            b.dsem = None
    self.live = [b for b in self.live if b is self.dummy]


Prog.barrier = _barrier
_orig_apply = Prog._apply_waits


def _apply2(self, eng, op, deps):
    pend = getattr(self, "pending", None)
    if pend and pend.get(eng):
        deps = list(deps) + pend[eng]
        pend[eng] = []
    _orig_apply(self, eng, op, deps)


Prog._apply_waits = _apply2


def MM(P, out_ap, pairs, reads, writes):
    pairs = list(pairs)

    def emit(e):
        n = len(pairs)
        ins = None
        for i, (l, r) in enumerate(pairs):
            ins = e.matmul(out_ap, l, r, start=(i == 0), stop=(i == n - 1))
        return ins
    return P.op("pe", emit, reads, writes)


def ACT(P, out, in_, func, reads, writes, bias=None, scale=None):
    kw = {}
    if bias is not None:
        kw["bias"] = bias
    if scale is not None:
        kw["scale"] = scale
    return P.op("act", lambda e: e.activation(out=out, in_=in_, func=func, **kw), reads, writes)


def TT(P, out, in0, in1, op, reads, writes, eng="dve"):
    return P.op(eng, lambda e: e.tensor_tensor(out=out, in0=in0, in1=in1, op=op), reads, writes)


def TS(P, out, in0, s1, s2, op0, op1, reads, writes, eng="dve"):
    if op1 is None:
        return P.op(eng, lambda e: e.tensor_scalar(out=out, in0=in0, scalar1=s1, scalar2=None, op0=op0), reads, writes)
    return P.op(eng, lambda e: e.tensor_scalar(out=out, in0=in0, scalar1=s1, scalar2=s2, op0=op0, op1=op1), reads, writes)


def STT(P, out, in0, scalar, in1, op0, op1, reads, writes):
    return P.op("dve", lambda e: e.scalar_tensor_tensor(out=out, in0=in0, scalar=scalar, in1=in1, op0=op0, op1=op1), reads, writes)


def CP(P, out, in_, reads, writes, eng="dve"):
    if eng == "act":
        return P.op("act", lambda e: e.copy(out=out, in_=in_), reads, writes)
    return P.op(eng, lambda e: e.tensor_copy(out=out, in_=in_), reads, writes)


def layer_norm_rows(P, t1, out_ap, out_buf, g_bc, b_bc, sm, eps_ap):
    st6 = sm["st6"]
    mv = sm["mv"]
    sd = sm["sd"]
    P.op("dve", lambda e: e.bn_stats(out=st6.ap[:, 0:6], in_=t1.ap[:, 0:512]), [t1], [st6])
    P.op("dve", lambda e: e.bn_stats(out=st6.ap[:, 6:12], in_=t1.ap[:, 512:1024]), [t1, st6], [st6])
    P.op("dve", lambda e: e.bn_aggr(out=mv.ap[:, 0:2], in_=st6.ap[:, 0:12]), [st6], [mv])
    ACT(P, sd.ap[:, 0:1], mv.ap[:, 1:2], AF.Sqrt, [mv, eps_ap], [sd], bias=eps_ap.ap[:, 0:1], scale=1.0)
    P.op("dve", lambda e: e.reciprocal(out=sd.ap[:, 1:2], in_=sd.ap[:, 0:1]), [sd], [sd])
    TS(P, t1.ap, t1.ap, mv.ap[:, 0:1], sd.ap[:, 1:2], ALU.subtract, ALU.mult, [t1, mv, sd], [t1])
    TT(P, t1.ap, t1.ap, g_bc.ap, ALU.mult, [t1, g_bc], [t1])
    TT(P, out_ap, t1.ap, b_bc.ap, ALU.add, [t1, b_bc], [out_buf] if out_buf is not t1 else [t1])


S = 4096
D = 1024
NT = S // 128
ALPHA = 2.0 ** 0.25
LN_EPS = 1e-5
QSCALE = 128.0 ** -0.5
NEG = -1e30
ARENA_WORDS = 45 * 1024
STRIP_LEN = 1792
R_LEN = 1920


def _r3(ap, p=128):
    return ap.rearrange("(c p) t -> p c t", p=p)


def build_program(phases="ACDEF", debug=False):
    nc = bass.Bass("TRN2", target_bir_lowering=False)

    def din(name, shape):
        return nc.dram_tensor(name, list(shape), F32, kind="ExternalInput").ap()

    def dscr(name, shape, dt):
        kind = "ExternalOutput" if debug else "Internal"
        return nc.dram_tensor(name, list(shape), dt, kind=kind).ap()

    T = type("T", (), {})()
    T.xT = din("xT", [D, S]); T.x = din("x", [S, D]); T.pT = din("pT", [256, S])
    T.w_in = din("w_in", [D, 7168]); T.b_in_c = din("b_in_c", [128, 56]); T.b_in = din("b_in", [7168])
    T.gln_g = din("gmlp_ln_g", [D]); T.gln_b = din("gmlp_ln_b", [D])
    T.wsT = din("wsT", [128, 8, 128]); T.bs = din("bs", [1, 8, 128]); T.tri = din("tri", [128, 128])
    T.w_pa = din("w_proj_a", [D, D]); T.w_pb = din("w_proj_b", [D, D]); T.w_out = din("w_out", [D, D])
    T.ln1_g = din("ln1_g", [D]); T.ln1_b = din("ln1_b", [D])
    T.rb_ext = din("rb_ext", [33, 8]); T.Emat = din("Emat", [33, R_LEN])
    T.sel_near = din("sel_near", [33, 16, 128]); T.sel_far = din("sel_far", [33, 16, 128])
    T.ident = din("ident", [128, 128]); T.antiid = din("antiid", [128, 128])
    T.w_q = din("peer_w_q", [D, 2048]); T.skT = din("skT", [128, 2, 128])
    T.uT = din("uT", [D, 16384]); T.vtab = din("vtab", [16384, D])
    T.w_ple = din("ple_w_proj", [256, D]); T.w_pg = din("ple_w_gate", [D, D]); T.b_pg = din("ple_b_gate", [D])
    T.ln2_g = din("ln2_g", [D]); T.ln2_b = din("ln2_b", [D])
    T.out = nc.dram_tensor("out", [S, D], F32, kind="ExternalOutput").ap()
    T.guT = dscr("s_guT", [D, S], BF16); T.qTs = dscr("s_qT", [D, S], BF16); T.kTs = dscr("s_kT", [D, S], BF16)
    T.sgaT = dscr("s_sgaT", [D, S], BF16); T.sgbT = dscr("s_sgbT", [D, S], BF16)
    T.vtok = dscr("s_vtok", [S, D], BF16); T.aT = dscr("s_aT", [D, S], BF16); T.oT = dscr("s_oT", [D, S], BF16)
    T.rrow = dscr("s_rrow", [8, R_LEN], BF16)
    T.uTb = nc.dram_tensor("s_uTb", [D, 16384], BF16, kind="Internal").ap()
    T.vtb = nc.dram_tensor("s_vtb", [16384, D], BF16, kind="Internal").ap()
    T.qpT = dscr("s_qpT", [2048, S], BF16)
    T.Cs = nc.dram_tensor("s_Cs", [NT, 128, 128, 128], BF16, kind="Internal").ap()
    T.h32 = dscr("s_h32", [S, D], F32); T.hT = dscr("s_hT", [D, S], BF16); T.f32 = dscr("s_f32", [S, D], F32)

    P = Prog(nc)
    A = Arena(P, ARENA_WORDS)
    pst = P.stack.enter_context(nc.psum_tensor("psum_all", [128, 4096], F32))
    bank = [Buf("bank%d" % i, pst[:, i * 512:(i + 1) * 512]) for i in range(8)]
    OUT = Buf("OUT")

    def bc(ap1d):
        return ap1d.partition_broadcast(128)

    def load_xT():
        xTb = [A.alloc("xTb%d" % c, [128, S], BF16) for c in range(8)]
        return xTb

    def phase_A1():
        P.barrier(); A.reset()
        xTb = load_xT()
        for c in range(8):
            P.dma("pool", xTb[c].ap, T.xT[c * 128:(c + 1) * 128, :], writes=[xTb[c]])
        wb = [A.alloc("wb%d" % i, [128, 8, 512], BF16) for i in range(2)]
        stg = [[A.alloc("stg%d_%d" % (i, j), [128, 512], BF16) for j in range(8)] for i in range(2)]
        bcol = A.alloc("bcol", [128, 56], F32)
        bqs = A.alloc("bqs", [128, 8], F32)
        P.dma("sync", bcol.ap, T.b_in_c, writes=[bcol])
        TS(P, bqs.ap, bcol.ap[:, 16:24], QSCALE, None, ALU.mult, None, [bcol], [bqs])
        groups = [("gu", 0, AF.Gelu, T.guT, None), ("q", 2048, AF.Identity, T.qTs, QSCALE),
                  ("k", 3072, AF.Identity, T.kTs, None), ("ga", 5120, AF.Sigmoid, T.sgaT, None),
                  ("gb", 6144, AF.Sigmoid, T.sgbT, None)]
        wi = si = bi = 0
        for (nm, c0, fn, dst, sc) in groups:
            for half in range(2):
                w = wb[wi % 2]; wi += 1
                P.dma("pool", w.ap, _r3(T.w_in[:, c0 + half * 512:c0 + half * 512 + 512]), writes=[w])
                for f4 in range(4):
                    fl = half * 4 + f4
                    fc = c0 // 128 + fl
                    st = stg[si % 2]; si += 1
                    if nm == "q":
                        bias_ap, bias_buf = bqs.ap[:, fl:fl + 1], bqs
                    else:
                        bias_ap, bias_buf = bcol.ap[:, fc:fc + 1], bcol
                    for tb in range(8):
                        ps = bank[bi % 4]; bi += 1
                        MM(P, ps.ap, [(w.ap[:, dc, f4 * 128:(f4 + 1) * 128], xTb[dc].ap[:, tb * 512:(tb + 1) * 512])
                                      for dc in range(8)], [w] + xTb, [ps])
                        ACT(P, st[tb].ap, ps.ap, fn, [ps, bias_buf], [st[tb]], bias=bias_ap,
                            scale=(sc if sc is not None else 1.0))
                    for tb in range(8):
                        P.dma("sync", dst[fl * 128:(fl + 1) * 128, tb * 512:(tb + 1) * 512], st[tb].ap, reads=[st[tb]])

    def phase_A2():
        P.barrier(); A.reset()
        xTb = load_xT()
        Wv = A.alloc("Wv", [128, 8, 1024], BF16)
        Wg = A.alloc("Wg", [128, 8, 1024], BF16)
        P.dma("pool", Wv.ap, _r3(T.w_in[:, 4096:5120]), writes=[Wv])
        P.dma("pool", Wg.ap, _r3(T.w_in[:, 1024:2048]), writes=[Wg])
        bvb = A.alloc("bvb", [128, 1024]); bvg = A.alloc("bvg", [128, 1024])
        lg = A.alloc("lg", [128, 1024]); lb = A.alloc("lb", [128, 1024])
        P.dma("sync", bvb.ap, bc(T.b_in[4096:5120]), writes=[bvb])
        P.dma("sync", bvg.ap, bc(T.b_in[1024:2048]), writes=[bvg])
        P.dma("sync", lg.ap, bc(T.gln_g), writes=[lg])
        P.dma("sync", lb.ap, bc(T.gln_b), writes=[lb])
        ws32 = A.alloc("ws32", [128, 8, 128]); tri = A.alloc("tri", [128, 128])
        wsb = A.alloc("wsb", [128, 8, 128], BF16)
        bs32 = A.alloc("bs32", [1, 8, 128]); ones32 = A.alloc("ones32", [1, 128])
        epsb = A.alloc("eps", [128, 1])
        P.dma("sync", ws32.ap, T.wsT, writes=[ws32])
        P.dma("sync", tri.ap, T.tri, writes=[tri])
        P.dma("sync", bs32.ap, T.bs, writes=[bs32])
        P.op("dve", lambda e: e.memset(ones32.ap, 1.0), [], [ones32])
        P.op("dve", lambda e: e.memset(epsb.ap, LN_EPS), [], [epsb])
        TT(P, wsb.ap, ws32.ap, tri.ap.unsqueeze(1).broadcast_to([128, 8, 128]), ALU.mult, [ws32, tri], [wsb])
        t1 = [A.alloc("t1_%d" % i, [128, 1024]) for i in range(2)]
        vn = [A.alloc("vn%d" % i, [128, 1024], BF16) for i in range(2)]
        vst = [A.alloc("vst%d" % i, [128, 1024], BF16) for i in range(2)]
        gub = [A.alloc("gub%d" % i, [128, 8, 512], BF16) for i in range(2)]
        ast = [[A.alloc("ast%d_%d" % (i, j), [128, 8, 128], BF16) for j in range(4)] for i in range(2)]
        sm = [{"st6": A.alloc("st6_%d" % i, [128, 12]), "mv": A.alloc("mv%d" % i, [128, 2]),
               "sd": A.alloc("sd%d" % i, [128, 2])} for i in range(2)]
        for tt in range(NT):
            tsl = slice(tt * 128, (tt + 1) * 128)
            tb, t4 = tt // 4, tt % 4
            b0 = (tt % 2) * 2
            for half in range(2):
                MM(P, bank[b0 + half].ap, [(xTb[dc].ap[:, tsl], Wv.ap[:, dc, half * 512:(half + 1) * 512]) for dc in range(8)],
                   xTb + [Wv], [bank[b0 + half]])
            vs = vst[tt % 2]
            TT(P, vs.ap, pst[:, b0 * 512:(b0 + 2) * 512], bvb.ap, ALU.add, [bank[b0], bank[b0 + 1], bvb], [vs])
            P.dma("sync", T.vtok[tsl, :], vs.ap, reads=[vs])
            for half in range(2):
                MM(P, bank[4 + half].ap, [(xTb[dc].ap[:, tsl], Wg.ap[:, dc, half * 512:(half + 1) * 512]) for dc in range(8)],
                   xTb + [Wg], [bank[4 + half]])
            t = t1[tt % 2]; v = vn[tt % 2]
            TT(P, t.ap, pst[:, 4 * 512:6 * 512], bvg.ap, ALU.add, [bank[4], bank[5], bvg], [t])
            ACT(P, t.ap, t.ap, AF.Gelu, [t], [t])
            layer_norm_rows(P, t, v.ap, v, lg, lb, sm[tt % 2], epsb)
            for g in range(8):
                bk = bank[6 + g // 4]
                oap = bk.ap[:, (g % 4) * 128:(g % 4 + 1) * 128]

                def emit(e, oap=oap, g=g, v=v):
                    e.matmul(oap, v.ap[:, g * 128:(g + 1) * 128], wsb.ap[:, g, :], start=True, stop=False)
                    return e.matmul(oap, ones32.ap[0:1, :], bs32.ap[0:1, g, :], start=False, stop=True)
                P.op("pe", emit, [v, wsb, ones32, bs32], [bk])
            if t4 == 0:
                gu = gub[tb % 2]
                P.dma("sync", gu.ap, _r3(T.guT[:, tb * 512:(tb + 1) * 512]), writes=[gu])
            gu = gub[tb % 2]
            a_s = ast[tb % 2][t4]
            TT(P, a_s.ap, pst[:, 6 * 512:8 * 512].rearrange("p (g t) -> p g t", g=8), gu.ap[:, :, t4 * 128:(t4 + 1) * 128],
               ALU.mult, [bank[6], bank[7], gu], [a_s])
            P.dma("sync", _r3(T.aT[:, tsl]), a_s.ap, reads=[a_s])

    def phase_C():
        P.barrier(); A.reset()
        RR = Buf("RR")
        rb = A.alloc("rb", [33, 8]); Em = A.alloc("Em", [33, R_LEN]); rsb = A.alloc("rsb", [8, R_LEN], BF16)
        P.dma("sync", rb.ap, T.rb_ext, writes=[rb]); P.dma("sync", Em.ap, T.Emat, writes=[Em])
        for j in range(4):
            MM(P, bank[0].ap[0:8, 0:480], [(rb.ap[0:33, :], Em.ap[0:33, j * 480:(j + 1) * 480])], [rb, Em], [bank[0]])
            CP(P, rsb.ap[:, j * 480:(j + 1) * 480], bank[0].ap[0:8, 0:480], [bank[0]], [rsb])
        P.dma("sync", T.rrow, rsb.ap, reads=[rsb], writes=[RR])
        strips = [A.alloc("strip%d" % h, [128, STRIP_LEN], BF16) for h in range(8)]
        for h in range(8):
            src = bass.AP(tensor=T.rrow.tensor, offset=h * R_LEN, ap=[[1, 128], [1, STRIP_LEN]])
            P.dma("sync", strips[h].ap, src, reads=[RR], writes=[strips[h]])
        identb = A.alloc("identb", [128, 128], BF16); ident32 = A.alloc("ident32", [128, 128])
        onesb = A.alloc("onesb", [128, 128], BF16)
        selN = A.alloc("selN", [33, 16, 128], BF16); selF = A.alloc("selF", [33, 16, 128], BF16)
        P.dma("pool", identb.ap, T.antiid, writes=[identb])
        P.dma("sync", ident32.ap, T.ident, writes=[ident32])
        P.dma("pool", selN.ap, T.sel_near, writes=[selN])
        P.dma("pool", selF.ap, T.sel_far, writes=[selF])
        P.op("dve", lambda e: e.memset(onesb.ap, 1.0), [], [onesb])
        kT = [A.alloc("kT%d" % i, [128, S], BF16) for i in range(2)]
        qT = [A.alloc("qT%d" % i, [128, S], BF16) for i in range(2)]
        vh = [A.alloc("vh%d" % i, [128, NT, 128], BF16) for i in range(2)]
        negT = [A.alloc("negT%d" % i, [33, S], BF16) for i in range(2)]
        ost = [[A.alloc("ost%d_%d" % (i, j), [128, 512], BF16) for j in range(8)] for i in range(2)]
        PT = [A.alloc("PT%d" % i, [128, 512], BF16) for i in range(3)]
        rden = [A.alloc("rden%d" % i, [128, 512]) for i in range(2)]
        kmT = [A.alloc("kmT%d" % i, [128, 16], BF16) for i in range(2)]
        km32 = [A.alloc("km32_%d" % i, [128, 16]) for i in range(2)]
        g16 = [A.alloc("g16_%d" % i, [128, 16]) for i in range(2)]
        top8 = [A.alloc("top8_%d" % i, [128, 8]) for i in range(2)]
        ngt = [A.alloc("ngt%d" % i, [128, 16]) for i in range(2)]
        for i in range(2):
            P.op("dve", lambda e, i=i: e.memset(negT[i].ap[0:33, :], 0.0), [], [negT[i]])

        def load_head(h):
            i = h % 2
            P.dma("sync", kT[i].ap, T.kTs[h * 128:(h + 1) * 128, :], writes=[kT[i]])
            P.dma("sync", qT[i].ap, T.qTs[h * 128:(h + 1) * 128, :], writes=[qT[i]])
            P.dma("sync", vh[i].ap, T.vtok[:, h * 128:(h + 1) * 128].rearrange("(n p) c -> p n c", p=128), writes=[vh[i]])

        b7g = Buf("b7g", bank[7].ap[:, 256:272])
        b7t = Buf("b7t", bank[7].ap[0:16, 0:128])

        def gate_first(h):
            i = h % 2
            P.op("dve", lambda e: e.tensor_reduce(out=km32[i].ap, in_=kT[i].ap.rearrange("p (n k) -> p n k", k=256),
                                                  axis=AX.X, op=ALU.add), [kT[i]], [km32[i]])
            CP(P, kmT[i].ap, km32[i].ap, [km32[i]], [kmT[i]])
            P.op("dve", lambda e: e.tensor_copy(out=negT[i].ap[32:33, :],
                                                in_=strips[h].ap[32:33, STRIP_LEN - 1:STRIP_LEN].broadcast_to([1, S])),
                 [strips[h]], [negT[i]])

        def gate_a(h, tt):
            i = h % 2
            qb = tt // 2
            ng = ngt[tt % 4]
            if qb <= 3:
                P.op("dve", lambda e: e.memset(ng.ap, 0.0), [], [ng])
            else:
                j = tt % 2
                MM(P, b7g.ap, [(qT[i].ap[:, tt * 128:(tt + 1) * 128], kmT[i].ap)], [qT[i], kmT[i]], [b7g])
                P.op("dve", lambda e: e.memset(g16[j].ap, NEG), [], [g16[j]])
                CP(P, g16[j].ap[:, 0:qb], b7g.ap[:, 0:qb], [b7g], [g16[j]])
                P.op("dve", lambda e: e.max(out=top8[j].ap, in_=g16[j].ap), [g16[j]], [top8[j]])
                TS(P, ng.ap, g16[j].ap, top8[j].ap[:, 2:3], None, ALU.is_ge, None, [g16[j], top8[j]], [ng])
                TS(P, ng.ap, ng.ap, 1e30, -1e30, ALU.mult, ALU.add, [ng], [ng])
                P.op("dve", lambda e: e.memset(ng.ap[:, qb:16], 0.0), [ng], [ng])

        def gate_b(h, tt):
            i = h % 2
            ng = ngt[tt % 4]
            P.op("pe", lambda e: e.transpose(out=b7t.ap, in_=ng.ap, identity=ident32.ap), [ng, ident32], [b7t])
            CP(P, negT[i].ap[0:16, tt * 128:(tt + 1) * 128], b7t.ap, [b7t], [negT[i]], eng="act")

        ngt = [A.alloc("ngtx%d" % i, [128, 16]) for i in range(4)]
        load_head(0)
        gate_first(0)
        for tt in range(NT):
            gate_a(0, tt)
            if tt >= 2:
                gate_b(0, tt - 2)
        gate_b(0, NT - 2); gate_b(0, NT - 1)

        tasks = []
        for h in range(8):
            for Q in range(8):
                for kt in range(4 * Q + 4):
                    tasks.append((h, Q, kt))
        st = {}

        def stageS(k):
            h, Q, kt = tasks[k]
            i = h % 2
            qs = slice(Q * 512, (Q + 1) * 512)
            n = kt // 2
            delta = Q * 512 - kt * 128
            far = delta >= 1024
            Sb = bank[k % 3]
            pairs = [(kT[i].ap[:, kt * 128:(kt + 1) * 128], qT[i].ap[:, qs])]
            rds = [kT[i], qT[i], negT[i]]
            if not far:
                pairs.append((identb.ap, strips[h].ap[:, delta + 384:delta + 384 + 512]))
                rds += [identb, strips[h]]
                sel = selN
            else:
                sel = selF
            pairs.append((sel.ap[0:33, n, :], negT[i].ap[0:33, qs]))
            rds.append(sel)
            MM(P, Sb.ap, pairs, rds, [Sb])
            pt = PT[k % 3]
            ACT(P, pt.ap, Sb.ap, AF.Exp, [Sb], [pt])
            st[k] = pt

        qcount = {"o": 0}

        def stageV(k):
            h, Q, kt = tasks[k]
            i = h % 2
            qs = slice(Q * 512, (Q + 1) * 512)
            nk = 4 * Q + 4
            pt = st.pop(k)
            if kt == 0:
                qcount["o"] += 1
            oi = qcount["o"] % 2
            OT = bank[3 + oi]; DEN = bank[5 + oi]
            P.op("pe", lambda e: e.matmul(OT.ap, vh[i].ap[:, kt, :], pt.ap, start=(kt == 0), stop=(kt == nk - 1)), [vh[i], pt], [OT])
            P.op("pe", lambda e: e.matmul(DEN.ap, onesb.ap, pt.ap, start=(kt == 0), stop=(kt == nk - 1)), [onesb, pt], [DEN])
            if kt == nk - 1:
                rd = rden[Q % 2]
                P.op("dve", lambda e: e.reciprocal(out=rd.ap, in_=DEN.ap), [DEN], [rd])
                TT(P, ost[i][Q].ap, OT.ap, rd.ap, ALU.mult, [OT, rd], [ost[i][Q]])
                P.dma("sync", T.oT[h * 128:(h + 1) * 128, qs], ost[i][Q].ap, reads=[ost[i][Q]])

        gate_plan = {}
        kbase = 0
        for h in range(8):
            nmain = 144
            if h + 1 < 8:
                for tt in range(NT):
                    gate_plan.setdefault(kbase + 4 + tt * 4, []).append(("a", h + 1, tt))
                    gate_plan.setdefault(kbase + 4 + tt * 4 + 10, []).append(("b", h + 1, tt))
                gate_plan.setdefault(kbase + 1, []).append(("load", h + 1, 0))
                gate_plan.setdefault(kbase + 2, []).append(("first", h + 1, 0))
            kbase += nmain
        for k in range(len(tasks) + 1):
            if k < len(tasks):
                stageS(k)
            if k >= 1:
                stageV(k - 1)
            for (kind, hh, tt) in gate_plan.get(k, []):
                if kind == "load":
                    load_head(hh)
                elif kind == "first":
                    gate_first(hh)
                elif kind == "a":
                    gate_a(hh, tt)
                else:
                    gate_b(hh, tt)

    def phase_D():
        P.barrier(); A.reset()
        WA = A.alloc("WA", [128, 8, 1024], BF16); WB = A.alloc("WB", [128, 8, 1024], BF16); WO = A.alloc("WO", [128, 8, 1024], BF16)
        P.dma("pool", WA.ap, _r3(T.w_pa), writes=[WA]); P.dma("pool", WB.ap, _r3(T.w_pb), writes=[WB])
        P.dma("pool", WO.ap, _r3(T.w_out), writes=[WO])
        lg = A.alloc("lg", [128, 1024]); lb = A.alloc("lb", [128, 1024])
        P.dma("sync", lg.ap, bc(T.ln1_g), writes=[lg]); P.dma("sync", lb.ap, bc(T.ln1_b), writes=[lb])
        identb = A.alloc("identb", [128, 128], BF16)
        P.dma("pool", identb.ap, T.ident, writes=[identb])
        epsb = A.alloc("eps", [128, 1])
        P.op("dve", lambda e: e.memset(epsb.ap, LN_EPS), [], [epsb])
        blk = {nm: [A.alloc("%s%d" % (nm, i), [128, 8, 512], BF16) for i in range(2)] for nm in ("a", "o", "ga", "gb")}
        src = {"a": T.aT, "o": T.oT, "ga": T.sgaT, "gb": T.sgbT}
        mT = [A.alloc("mT%d" % c, [128, 512], BF16) for c in range(8)]
        tmp = [A.alloc("tmp%d" % i, [128, 512]) for i in range(2)]
        tmp2 = [A.alloc("tmpb%d" % i, [128, 512]) for i in range(2)]
        xt = [A.alloc("xt%d" % i, [128, 1024]) for i in range(2)]
        t1 = [A.alloc("t1_%d" % i, [128, 1024]) for i in range(2)]
        hb = [A.alloc("hb%d" % i, [128, 1024], BF16) for i in range(2)]
        hTs = [[A.alloc("hTs%d_%d" % (i, j), [128, 8, 128], BF16) for j in range(4)] for i in range(2)]
        sm = [{"st6": A.alloc("st6_%d" % i, [128, 12]), "mv": A.alloc("mv%d" % i, [128, 2]),
               "sd": A.alloc("sd%d" % i, [128, 2])} for i in range(2)]
        bi = 0
        for tb in range(8):
            i = tb % 2
            bs_ = slice(tb * 512, (tb + 1) * 512)
            for nm in ("a", "o", "ga", "gb"):
                P.dma("sync", blk[nm][i].ap, _r3(src[nm][:, bs_]), writes=[blk[nm][i]])
            for n in range(8):
                pa = bank[bi % 4]; pb = bank[(bi + 1) % 4]; bi += 2
                MM(P, pa.ap, [(WA.ap[:, wc, n * 128:(n + 1) * 128], blk["a"][i].ap[:, wc, :]) for wc in range(8)], [WA, blk["a"][i]], [pa])
                MM(P, pb.ap, [(WB.ap[:, wc, n * 128:(n + 1) * 128], blk["o"][i].ap[:, wc, :]) for wc in range(8)], [WB, blk["o"][i]], [pb])
                ta = tmp[n % 2]; tb2 = tmp2[n % 2]
                TT(P, ta.ap, pa.ap, blk["ga"][i].ap[:, n, :], ALU.mult, [pa, blk["ga"][i]], [ta])
                TT(P, tb2.ap, pb.ap, blk["gb"][i].ap[:, n, :], ALU.mult, [pb, blk["gb"][i]], [tb2])
                TT(P, mT[n].ap, ta.ap, tb2.ap, ALU.add, [ta, tb2], [mT[n]], eng="pool")
            for t4 in range(4):
                tt = tb * 4 + t4
                j = tt % 2
                tsl = slice(tt * 128, (tt + 1) * 128)
                P.dma("sync", xt[j].ap, T.x[tsl, :], writes=[xt[j]])
                for half in range(2):
                    MM(P, bank[4 + half].ap, [(mT[wc].ap[:, t4 * 128:(t4 + 1) * 128], WO.ap[:, wc, half * 512:(half + 1) * 512]) for wc in range(8)],
                       mT + [WO], [bank[4 + half]])
                STT(P, t1[j].ap, xt[j].ap, ALPHA, pst[:, 4 * 512:6 * 512], ALU.mult, ALU.add, [xt[j], bank[4], bank[5]], [t1[j]])
                layer_norm_rows(P, t1[j], t1[j].ap, t1[j], lg, lb, sm[j], epsb)
                P.dma("sync", T.h32[tsl, :], t1[j].ap, reads=[t1[j]])
                CP(P, hb[j].ap, t1[j].ap, [t1[j]], [hb[j]], eng="act")
                psT = bank[6 + j]
                psT_ap = psT.ap.bitcast(BF16)[:, 0:1024]

                def emit(e, j=j, psT_ap=psT_ap):
                    ins = None
                    for c in range(8):
                        ins = e.transpose(out=psT_ap[:, c * 128:(c + 1) * 128], in_=hb[j].ap[:, c * 128:(c + 1) * 128], identity=identb.ap)
                    return ins
                P.op("pe", emit, [hb[j], identb], [psT])
                hs = hTs[i][t4]
                CP(P, hs.ap, psT_ap.rearrange("p (c t) -> p c t", c=8), [psT], [hs], eng="act")
                P.dma("sync", _r3(T.hT[:, tsl]), hs.ap, reads=[hs])

    def phase_F():
        P.barrier(); A.reset()
        Wp = A.alloc("Wp", [128, 2, 1024], BF16); Wg = A.alloc("Wg", [128, 8, 1024], BF16)
        P.dma("pool", Wp.ap, _r3(T.w_ple), writes=[Wp]); P.dma("pool", Wg.ap, _r3(T.w_pg), writes=[Wg])
        pTb = [A.alloc("pTb%d" % c, [128, S], BF16) for c in range(2)]
        for c in range(2):
            P.dma("pool", pTb[c].ap, T.pT[c * 128:(c + 1) * 128, :], writes=[pTb[c]])
        bg = A.alloc("bg", [128, 1024]); lg = A.alloc("lg", [128, 1024]); lb = A.alloc("lb", [128, 1024])
        P.dma("sync", bg.ap, bc(T.b_pg), writes=[bg])
        P.dma("sync", lg.ap, bc(T.ln2_g), writes=[lg]); P.dma("sync", lb.ap, bc(T.ln2_b), writes=[lb])
        epsb = A.alloc("eps", [128, 1])
        P.op("dve", lambda e: e.memset(epsb.ap, LN_EPS), [], [epsb])
        hTb = [A.alloc("hTb%d" % i, [128, 8, 512], BF16) for i in range(2)]
        ht = [A.alloc("ht%d" % i, [128, 1024]) for i in range(2)]
        ft = [A.alloc("ft%d" % i, [128, 1024]) for i in range(2)]
        t1 = [A.alloc("t1_%d" % i, [128, 1024]) for i in range(2)]
        t2 = [A.alloc("t2_%d" % i, [128, 1024]) for i in range(2)]
        sm = [{"st6": A.alloc("st6_%d" % i, [128, 12]), "mv": A.alloc("mv%d" % i, [128, 2]),
               "sd": A.alloc("sd%d" % i, [128, 2])} for i in range(2)]
        for tt in range(NT):
            tb, t4 = tt // 4, tt % 4
            j = tt % 2
            tsl = slice(tt * 128, (tt + 1) * 128)
            if t4 == 0:
                P.dma("sync", hTb[tb % 2].ap, _r3(T.hT[:, tb * 512:(tb + 1) * 512]), writes=[hTb[tb % 2]])
            hblk = hTb[tb % 2]
            P.dma("sync", ht[j].ap, T.h32[tsl, :], writes=[ht[j]])
            P.dma("sync", ft[j].ap, T.f32[tsl, :], writes=[ft[j]])
            b0 = 4 * j
            for half in range(2):
                MM(P, bank[b0 + half].ap, [(pTb[kc].ap[:, tsl], Wp.ap[:, kc, half * 512:(half + 1) * 512]) for kc in range(2)],
                   pTb + [Wp], [bank[b0 + half]])
                MM(P, bank[b0 + 2 + half].ap, [(hblk.ap[:, dc, t4 * 128:(t4 + 1) * 128], Wg.ap[:, dc, half * 512:(half + 1) * 512]) for dc in range(8)],
                   [hblk, Wg], [bank[b0 + 2 + half]])
            TT(P, t2[j].ap, pst[:, (b0 + 2) * 512:(b0 + 4) * 512], bg.ap, ALU.add, [bank[b0 + 2], bank[b0 + 3], bg], [t2[j]])
            ACT(P, t2[j].ap, t2[j].ap, AF.Sigmoid, [t2[j]], [t2[j]])
            TT(P, t2[j].ap, t2[j].ap, pst[:, b0 * 512:(b0 + 2) * 512], ALU.mult, [t2[j], bank[b0], bank[b0 + 1]], [t2[j]])
            STT(P, t1[j].ap, ht[j].ap, ALPHA, ft[j].ap, ALU.mult, ALU.add, [ht[j], ft[j]], [t1[j]])
            TT(P, t1[j].ap, t1[j].ap, t2[j].ap, ALU.add, [t1[j], t2[j]], [t1[j]], eng="pool")
            layer_norm_rows(P, t1[j], t1[j].ap, t1[j], lg, lb, sm[j], epsb)
            P.dma("sync", T.out[tsl, :], t1[j].ap, reads=[t1[j]])

    def phase_E_stub():
        P.barrier(); A.reset()
        z = A.alloc("z", [128, 1024])
        P.op("dve", lambda e: e.memset(z.ap, 0.0), [], [z])
        for tt in range(NT):
            P.dma("sync", T.f32[tt * 128:(tt + 1) * 128, :], z.ap, reads=[z])

    def phase_E0():
        for r in range(16):
            P.dma("pool", T.uTb[r * 64:(r + 1) * 64, :], T.uT[r * 64:(r + 1) * 64, :])
        for r in range(16):
            P.dma("pool", T.vtb[r * 1024:(r + 1) * 1024, :], T.vtab[r * 1024:(r + 1) * 1024, :])

    def phase_E1():
        P.barrier(); A.reset()
        hTb = [A.alloc("hTb%d" % c, [128, S], BF16) for c in range(8)]
        for c in range(8):
            P.dma("sync", hTb[c].ap, T.hT[c * 128:(c + 1) * 128, :], writes=[hTb[c]])
        wb = [A.alloc("wb%d" % i, [128, 8, 512], BF16) for i in range(2)]
        stg = [[A.alloc("stg%d_%d" % (i, j), [128, 512], BF16) for j in range(8)] for i in range(2)]
        si = bi = 0
        for cg in range(4):
            w = wb[cg % 2]
            P.dma("pool", w.ap, _r3(T.w_q[:, cg * 512:(cg + 1) * 512]), writes=[w])
            for f4 in range(4):
                fc = cg * 4 + f4
                st = stg[si % 2]; si += 1
                for tb in range(8):
                    ps = bank[bi % 4]; bi += 1
                    MM(P, ps.ap, [(w.ap[:, dc, f4 * 128:(f4 + 1) * 128], hTb[dc].ap[:, tb * 512:(tb + 1) * 512]) for dc in range(8)],
                       [w] + hTb, [ps])
                    if tb % 2 == 0:
                        CP(P, st[tb].ap, ps.ap, [ps], [st[tb]], eng="act")
                    else:
                        CP(P, st[tb].ap, ps.ap, [ps], [st[tb]], eng="dve")
                for tb in range(8):
                    P.dma("sync", T.qpT[fc * 128:(fc + 1) * 128, tb * 512:(tb + 1) * 512], st[tb].ap, reads=[st[tb]])

    def phase_E2():
        P.barrier(); A.reset()
        I32 = mybir.dt.int32
        U32 = mybir.dt.uint32
        skb = A.alloc("skb", [128, 2, 128], BF16)
        P.dma("pool", skb.ap, T.skT, writes=[skb])
        identb = A.alloc("identb", [128, 128], BF16)
        P.dma("pool", identb.ap, T.ident, writes=[identb])
        ident32 = A.alloc("ident32", [128, 128])
        P.dma("sync", ident32.ap, T.ident, writes=[ident32])
        iot_i = A.alloc("iot_i", [128, 128], I32)
        iot_f = A.alloc("iot_f", [128, 128])
        P.op("pool", lambda e: e.iota(iot_i.ap, pattern=[[1, 128]], base=0, channel_multiplier=0), [], [iot_i])
        P.op("pool", lambda e: e.tensor_copy(out=iot_f.ap, in_=iot_i.ap), [iot_i], [iot_f])
        qpb = A.alloc("qpb", [128, 16, 128], BF16)
        Ssb = A.alloc("Ssb", [128, 16, 128])
        S2 = [A.alloc("S2_%d" % i, [128, 128]) for i in range(2)]
        top = A.alloc("top", [128, 16, 16])
        idx_u = A.alloc("idx_u", [128, 8, 16], U32)
        idx_f = A.alloc("idx_f", [128, 128])
        idxT = A.alloc("idxT", [128, 128])
        cand2 = [A.alloc("cand2_%d" % i, [128, 256]) for i in range(2)]
        best = A.alloc("best", [128, 8, 16])
        eb = A.alloc("eb", [128, 8, 16])
        Z = A.alloc("Z", [128, 8]); lnZ = A.alloc("lnZ", [128, 8])
        wp = A.alloc("wp", [128, 8, 16]); taup = A.alloc("taup", [128, 8])
        X = [A.alloc("X%d" % i, [128, 16, 128]) for i in range(2)]
        cand = X[1]
        cand_ap = X[1].ap.rearrange("p a b -> p (a b)").rearrange("p (h c) -> p h c", h=8)
        Rall = A.alloc("Rall", [128, 128, 128], BF16)
        Rtm = [Buf("Rtm%d" % h, Rall.ap[:, h * 16:(h + 1) * 16, :]) for h in range(8)]
        Lsm = [A.alloc("Lsm%d" % i, [128, 128, 128], BF16) for i in range(2)]
        Rsm = A.alloc("Rsm", [128, 128, 128], BF16)
        Cst = [A.alloc("Cst%d" % i, [128, 16, 128], BF16) for i in range(2)]
        topv = top.ap.rearrange("p (h two) a -> p h two a", two=2)
        Sv = Ssb.ap.rearrange("p (h two) n -> p h two n", two=2)
        b0t = Buf("b0t", bank[0].ap)
        cnt = {"sc": 0, "tr": 0, "cp": 0, "cs": 0}

        def stage_scores(tt):
            tsl = slice(tt * 128, (tt + 1) * 128)
            P.dma("sync", qpb.ap, T.qpT[:, tsl].rearrange("(c k) t -> k c t", k=128), writes=[qpb])
            for cg in range(4):
                bk = bank[cnt["sc"] % 2]; cnt["sc"] += 1

                def emit(e, bk=bk, cg=cg):
                    ins = None
                    for cl in range(4):
                        c = cg * 4 + cl
                        ins = e.matmul(bk.ap[:, cl * 128:(cl + 1) * 128], qpb.ap[:, c, :], skb.ap[:, c % 2, :], start=True, stop=True)
                    return ins
                P.op("pe", emit, [qpb, skb], [bk])
                CP(P, Ssb.ap[:, cg * 4:(cg + 1) * 4, :], bk.ap.rearrange("p (c n) -> p c n", c=4), [bk], [Ssb], eng="act")

        def stage_route1(tt):
            for c in range(16):
                s2 = S2[c % 2]
                P.op("dve", lambda e, c=c: e.max(out=top.ap[:, c, 0:8], in_=Ssb.ap[:, c, :]), [Ssb], [top])
                P.op("dve", lambda e, c=c, s2=s2: e.match_replace(out=s2.ap, in_to_replace=top.ap[:, c, 0:8], in_values=Ssb.ap[:, c, :], imm_value=NEG),
                     [Ssb, top], [s2])
                P.op("dve", lambda e, c=c, s2=s2: e.max(out=top.ap[:, c, 8:16], in_=s2.ap), [s2], [top])
                if c % 2 == 0:
                    h = c // 2
                    P.op("dve", lambda e, c=c, h=h: e.max_index(out=idx_u.ap[:, h, 0:8], in_max=top.ap[:, c, 0:8], in_values=Ssb.ap[:, c, :]),
                         [Ssb, top], [idx_u])
                    P.op("dve", lambda e, c=c, h=h: e.max_index(out=idx_u.ap[:, h, 8:16], in_max=top.ap[:, c, 8:16], in_values=Ssb.ap[:, c, :]),
                         [Ssb, top], [idx_u])
            TT(P, cand_ap.rearrange("p h (a b) -> p h a b", a=16),
               topv[:, :, 0, :].unsqueeze(3).broadcast_to([128, 8, 16, 16]),
               topv[:, :, 1, :].unsqueeze(2).broadcast_to([128, 8, 16, 16]), ALU.add, [top], [cand])
            for h in range(8):
                c2 = cand2[h % 2]
                P.op("dve", lambda e, h=h: e.max(out=best.ap[:, h, 0:8], in_=cand_ap[:, h, :]), [cand], [best])
                P.op("dve", lambda e, h=h, c2=c2: e.match_replace(out=c2.ap, in_to_replace=best.ap[:, h, 0:8], in_values=cand_ap[:, h, :], imm_value=NEG),
                     [cand, best], [c2])
                P.op("dve", lambda e, h=h, c2=c2: e.max(out=best.ap[:, h, 8:16], in_=c2.ap), [c2], [best])
            CP(P, idx_f.ap, idx_u.ap.rearrange("p h a -> p (h a)"), [idx_u], [idx_f])

        def stage_route2(tt):
            ACT(P, eb.ap, best.ap, AF.Exp, [best], [eb])
            P.op("dve", lambda e: e.tensor_reduce(out=Z.ap, in_=eb.ap, axis=AX.X, op=ALU.add), [eb], [Z])
            ACT(P, lnZ.ap, Z.ap, AF.Ln, [Z], [lnZ])
            TT(P, wp.ap, topv[:, :, 0, :], lnZ.ap.unsqueeze(2).broadcast_to([128, 8, 16]), ALU.subtract, [top, lnZ], [wp])
            STT(P, taup.ap, best.ap[:, :, 15], -1e-5, lnZ.ap, ALU.add, ALU.subtract, [best, lnZ], [taup])

        def stage_buildL(tt):
            L = Lsm[tt % 2]
            P.op("pe", lambda e: e.transpose(out=b0t.ap[:, 0:128], in_=idx_f.ap, identity=ident32.ap), [idx_f, ident32], [b0t, bank[0]])
            CP(P, idxT.ap, b0t.ap[:, 0:128], [b0t, bank[0]], [idxT], eng="act")
            def emitL(e, L=L):
                ins = None
                for t in range(128):
                    ins = e.tensor_scalar(out=L.ap[:, t, :], in0=iot_f.ap, scalar1=idxT.ap[:, t:t + 1], scalar2=None, op0=ALU.is_equal)
                return ins
            P.op("pool", emitL, [iot_f, idxT], [L])

        def stage_buildR(tt):
            for h in range(8):
                x = X[h % 2]
                TT(P, x.ap, Sv[:, h, 1, :].unsqueeze(1).broadcast_to([128, 16, 128]),
                   wp.ap[:, h, :].unsqueeze(2).broadcast_to([128, 16, 128]), ALU.add, [Ssb, wp], [x])
                ACT(P, Rtm[h].ap, x.ap, AF.Exp, [x], [Rtm[h]])
                STT(P, Rtm[h].ap, x.ap, taup.ap[:, h:h + 1], Rtm[h].ap, ALU.is_ge, ALU.mult, [x, taup, Rtm[h]], [Rtm[h]])

        def stage_trR(tt):
            for ig in range(16):
                bk = bank[2 + cnt["tr"] % 2]; cnt["tr"] += 1
                bkv = bk.ap.bitcast(BF16)[:, 0:1024]

                def emit(e, bkv=bkv, ig=ig):
                    ins = None
                    for q in range(8):
                        i_ = ig * 8 + q
                        ins = e.transpose(out=bkv[:, q * 128:(q + 1) * 128], in_=Rall.ap[:, :, i_], identity=identb.ap)
                    return ins
                P.op("pe", emit, Rtm + [identb], [bk])
                CP(P, Rsm.ap[:, ig * 8:(ig + 1) * 8, :], bkv.rearrange("p (i t) -> p i t", i=8), [bk], [Rsm], eng="act")

        def stage_cmm(tt):
            L = Lsm[tt % 2]
            for t16 in range(8):
                cs = Cst[cnt["cs"] % 2]; cnt["cs"] += 1
                for t4 in range(4):
                    bk = bank[4 + cnt["cp"] % 4]; cnt["cp"] += 1
                    tb_ = t16 * 16 + t4 * 4

                    def emit(e, bk=bk, tb_=tb_):
                        ins = None
                        for q in range(4):
                            t = tb_ + q
                            ins = e.matmul(bk.ap[:, q * 128:(q + 1) * 128], Rsm.ap[:, :, t], L.ap[:, t, :], start=True, stop=True)
                        return ins
                    P.op("pe", emit, [Rsm, L], [bk])
                    CP(P, cs.ap[:, t4 * 4:(t4 + 1) * 4, :], bk.ap.rearrange("p (t i) -> p t i", t=4), [bk], [cs], eng="act")
                P.dma("sync", T.Cs[tt][:, t16 * 16:(t16 + 1) * 16, :], cs.ap, reads=[cs])

        stage_scores(0)
        stage_route1(0)
        stage_route2(0)
        stage_buildL(0)
        stage_buildR(0)
        for tt in range(NT):
            if tt + 1 < NT:
                stage_scores(tt + 1)
                stage_route1(tt + 1)
            stage_trR(tt)
            stage_cmm(tt)
            if tt + 1 < NT:
                stage_route2(tt + 1)
                stage_buildL(tt + 1)
                stage_buildR(tt + 1)

    def phase_E3():
        P.barrier(); A.reset()
        TB = 256
        hTb = [A.alloc("hTb%d" % i, [128, 8, TB], BF16) for i in range(2)]
        ysb = [A.alloc("ysb%d" % i, [128, 1024]) for i in range(2)]
        NG = 4
        gl = [A.alloc("glx%d" % i, [128, TB]) for i in range(NG)]
        G = [A.alloc("Gx%d" % i, [128, TB], BF16) for i in range(NG)]
        NB = 3
        Ug = [A.alloc("Ugx%d" % i, [128, 8, 256], BF16) for i in range(NB)]
        Vg = [A.alloc("Vgx%d" % i, [128, 2, 1024], BF16) for i in range(NB)]
        Cp = [A.alloc("Cp%d" % i, [128, 2, 128, 128], BF16) for i in range(2)]
        NBLK = S // TB
        tasks = [(blk, c) for blk in range(NBLK) for c in range(128)]
        state = {}

        def load_group(gi_):
            blk, cg = gi_ // 64, gi_ % 64
            if blk >= NBLK:
                return
            ug = Ug[gi_ % NB]; vg = Vg[gi_ % NB]
            P.dma("sync", ug.ap, _r3(T.uTb[:, cg * 256:(cg + 1) * 256]), writes=[ug])
            P.dma("sync", vg.ap, T.vtb[cg * 256:(cg + 1) * 256, :].rearrange("(c p) d -> p c d", p=128), writes=[vg])

        def load_block(blk):
            if blk >= NBLK:
                return
            hb = hTb[blk % 2]
            P.dma("sync", hb.ap, _r3(T.hT[:, blk * TB:(blk + 1) * TB]), writes=[hb])
            cp = Cp[blk % 2]
            for tl in range(2):
                P.dma("pool", cp.ap[:, tl], T.Cs[blk * 2 + tl], writes=[cp])

        load_block(0)
        load_group(0)
        load_group(1)

        def stage1(k):
            blk, c = tasks[k]
            hb = hTb[blk % 2]
            cp = Cp[blk % 2]
            cg, cl = c // 2, c % 2
            gi_ = blk * 64 + cg
            ug = Ug[gi_ % NB]; vg = Vg[gi_ % NB]
            hp = bank[k % 4]
            MM(P, hp.ap[:, 0:TB], [(ug.ap[:, dc, cl * 128:(cl + 1) * 128], hb.ap[:, dc, :]) for dc in range(8)], [ug, hb], [hp])
            g1 = gl[k % NG]; g2 = G[k % NG]
            ACT(P, g1.ap, hp.ap[:, 0:TB], AF.Gelu, [hp], [g1])
            TT(P, g2.ap.rearrange("p (a t) -> p a t", a=2), g1.ap.rearrange("p (a t) -> p a t", a=2), cp.ap[:, :, :, c],
               ALU.mult, [g1, cp], [g2])
            state[k] = (g2, vg, cl)

        def stage2(k):
            blk, c = tasks[k]
            g2, vg, cl = state.pop(k)
            for tl in range(2):
                for half in range(2):
                    yb = bank[4 + tl * 2 + half]
                    P.op("pe", lambda e, yb=yb, tl=tl, half=half:
                         e.matmul(yb.ap, g2.ap[:, tl * 128:(tl + 1) * 128], vg.ap[:, cl, half * 512:(half + 1) * 512],
                                  start=(c == 0), stop=(c == 127)), [g2, vg], [yb])
            if c == 127:
                for tl in range(2):
                    tt = blk * 2 + tl
                    y = ysb[tl]
                    CP(P, y.ap, pst[:, (4 + tl * 2) * 512:(6 + tl * 2) * 512], [bank[4 + tl * 2], bank[5 + tl * 2]], [y], eng=("act" if tl else "dve"))
                    P.dma("sync", T.f32[tt * 128:(tt + 1) * 128, :], y.ap, reads=[y])

        SK = 2
        for k in range(len(tasks) + SK):
            if k < len(tasks):
                stage1(k)
            if k >= SK:
                stage2(k - SK)
            if k < len(tasks):
                blk, c = tasks[k]
                if c % 2 == 1:
                    load_group(blk * 64 + c // 2 + 2)
                if c == 127:
                    load_block(blk + 1)

    def phase_E():
        phase_E0()
        phase_E1()
        phase_E2()
        phase_E3()

    if "A" in phases:
        phase_A1()
        phase_A2()
    if "C" in phases:
        phase_C()
    if "D" in phases:
        phase_D()
    if "E" in phases:
        phase_E()
    for ch, fn in (("0", phase_E0), ("1", phase_E1), ("2", phase_E2), ("3", phase_E3)):
        if ch in phases:
            fn()
    if "e" in phases:
        phase_E_stub()
    if "F" in phases:
        phase_F()
    P.barrier()
    P.finish([])
    return nc, P


def _t5_bucket_np(d):
    import math
    n = np.maximum(d, 0)
    nf = np.maximum(n, 16).astype(np.float32)
    large = 16 + (np.log(nf / np.float32(16)) / np.float32(math.log(1024 / 16)) * np.float32(16)).astype(np.int32)
    large = np.minimum(large, 31)
    return np.where(n < 16, n, large)


def _constants():
    c = {}
    c["tri"] = (np.arange(128)[:, None] <= np.arange(128)[None, :]).astype(np.float32)
    E = np.zeros((33, R_LEN), np.float32)
    y = np.arange(R_LEN)
    d = y - 511
    bk = _t5_bucket_np(d)
    for i in range(R_LEN):
        if d[i] < 0:
            E[32, i] = 1.0
        else:
            E[bk[i], i] = 1.0
    c["Emat"] = E
    sn = np.zeros((33, 16, 128), np.float32)
    sf = np.zeros((33, 16, 128), np.float32)
    for n in range(16):
        sn[n, n, :] = 1.0
        sf[n, n, :] = 1.0
        sf[32, n, :] = 1.0
    c["sel_near"] = sn
    c["sel_far"] = sf
    c["ident"] = np.eye(128, dtype=np.float32)
    c["antiid"] = np.ascontiguousarray(np.eye(128, dtype=np.float32)[::-1])
    return c


def prep_shared(inp):
    f = lambda a: np.ascontiguousarray(a, dtype=np.float32)
    sh = {}
    sh["w_in"] = f(inp["w_in"][0])
    sh["b_in_c"] = f(inp["b_in"][0].reshape(56, 128).T)
    sh["b_in"] = f(inp["b_in"][0])
    sh["gmlp_ln_g"] = f(inp["gmlp_ln_g"][0]); sh["gmlp_ln_b"] = f(inp["gmlp_ln_b"][0])
    sh["wsT"] = f(inp["gmlp_w_s"][0].transpose(2, 0, 1))
    sh["bs"] = f(inp["gmlp_b_s"][0][None])
    for k in ("w_proj_a", "w_proj_b", "w_out", "ln1_g", "ln1_b", "peer_w_q", "ple_w_proj", "ple_w_gate",
              "ple_b_gate", "ln2_g", "ln2_b"):
        sh[k] = f(inp[k][0])
    sh["rb_ext"] = f(np.concatenate([inp["rel_bias"], np.full((1, 8), NEG, np.float32)], axis=0))
    sh["skT"] = f(inp["peer_sub_keys"][0].transpose(2, 0, 1))
    sh["uT"] = f(inp["peer_u"][0].T)
    sh["vtab"] = f(inp["peer_v"][0])
    sh.update(_constants())
    return sh


def prep_core(inp, b):
    f = lambda a: np.ascontiguousarray(a, dtype=np.float32)
    return {"xT": f(inp["x"][b].T), "x": f(inp["x"][b]), "pT": f(inp["p"][0, b].T)}


PHASES = "ACDEF"


def kernel(**inputs):
    inp = {k: np.asarray(v) for k, v in inputs.items()}
    nc, P = build_program(phases=PHASES, debug=False)
    sh = prep_shared(inp)
    in_maps = [dict(sh, **prep_core(inp, b)) for b in range(8)]
    res = run_bass_kernel_spmd(nc, in_maps, core_ids=list(range(8)))
    out = np.stack([np.asarray(r["out"], dtype=np.float32) for r in res.results], axis=0)
    return out
```

```python
import contextlib
import numpy as np
import concourse.bass as bass
import concourse.mybir as mybir
from concourse.bass_utils import run_bass_kernel_spmd

F32 = mybir.dt.float32
BF16 = mybir.dt.bfloat16
AF = mybir.ActivationFunctionType
ALU = mybir.AluOpType
AX = mybir.AxisListType


class Buf:
    __slots__ = ("name", "ap", "lw", "rd", "dsem", "dcount")

    def __init__(self, name, ap=None):
        self.name = name
        self.ap = ap
        self.lw = None
        self.rd = []
        self.dsem = None
        self.dcount = 0


class Slot:
    __slots__ = ("dsem", "dcount")

    def __init__(self):
        self.dsem = None
        self.dcount = 0


class _Op:
    __slots__ = ("emit", "waits", "signal", "known", "dma_buf", "dma_val")

    def __init__(self, emit):
        self.emit = emit
        self.waits = []
        self.signal = False
        self.known = None
        self.dma_buf = None
        self.dma_val = 0


class Prog:
    ENG = {"pe": "tensor", "act": "scalar", "dve": "vector", "pool": "gpsimd", "sync": "sync"}
    ALIAS = {"gpsimd": "pool", "scalar": "act", "vector": "dve", "tensor": "pe"}

    def __init__(self, nc):
        self.nc = nc
        self.ops = {k: [] for k in self.ENG}
        self.known = {k: {} for k in self.ENG}
        self.stack = contextlib.ExitStack()
        self.dbufs = []
        self.dummy = Buf("dummy")
        self.n_ops = 0
        self.free_slots = []
        self.live = []

    def sbuf(self, name, shape, dtype):
        t = self.stack.enter_context(self.nc.sbuf_tensor(name, shape, dtype))
        return Buf(name, t)

    def psum(self, name, shape, dtype):
        t = self.stack.enter_context(self.nc.psum_tensor(name, shape, dtype))
        return Buf(name, t)

    def dram_buf(self, name):
        return Buf(name)

    def view(self, name, ap):
        return Buf(name, ap)

    def _deps(self, reads, writes):
        deps = []
        for b in reads:
            if b.lw is not None:
                deps.append(b.lw)
        for b in writes:
            if b.lw is not None:
                deps.append(b.lw)
            deps.extend(b.rd)
        return deps

    def _apply_waits(self, eng, op, deps):
        known = self.known[eng]
        changed = False
        for ev in deps:
            if ev[0] == "e":
                _, e2, idx = ev
                if e2 == eng and eng == "pe":
                    continue
                key = e2
                if known.get(key, -1) >= idx:
                    continue
                if not changed:
                    known = dict(known)
                    changed = True
                if e2 != eng or True:
                    op.waits.append(ev)
                    self.ops[e2][idx].signal = True
                known[key] = idx
                k2 = self.ops[e2][idx].known
                if k2:
                    for kk, vv in k2.items():
                        if known.get(kk, -1) < vv:
                            known[kk] = vv
            else:
                _, b, val = ev
                key = ("d", id(b))
                if known.get(key, -1) >= val:
                    continue
                if not changed:
                    known = dict(known)
                    changed = True
                op.waits.append(ev)
                known[key] = val
        self.known[eng] = known
        op.known = known

    def _commit(self, ev, reads, writes):
        for b in reads:
            b.rd.append(ev)
        for b in writes:
            b.lw = ev
            b.rd = []

    def op(self, eng, emit, reads=(), writes=()):
        eng = self.ALIAS.get(eng, eng)
        o = _Op(emit)
        self._apply_waits(eng, o, self._deps(reads, writes))
        idx = len(self.ops[eng])
        self.ops[eng].append(o)
        self._commit(("e", eng, idx), reads, writes)
        self.n_ops += 1
        return o

    def dma(self, q, out_ap, in_ap, reads=(), writes=(), sem_buf=None, **kw):
        q = self.ALIAS.get(q, q)
        if sem_buf is None:
            for b in list(writes) + list(reads):
                if b.ap is not None:
                    sem_buf = b
                    break
            else:
                sem_buf = self.dummy
        o = _Op(lambda e: e.dma_start(out=out_ap, in_=in_ap, **kw))
        self._apply_waits(q, o, self._deps(reads, writes))
        if sem_buf.dsem is None:
            if self.free_slots:
                sem_buf.dsem = self.free_slots.pop()
            else:
                sem_buf.dsem = Slot()
                self.dbufs.append(sem_buf.dsem)
            self.live.append(sem_buf)
        sl = sem_buf.dsem
        sl.dcount += 16
        o.dma_buf = sl
        o.dma_val = sl.dcount
        self.ops[q].append(o)
        self._commit(("d", sl, sl.dcount), reads, writes)
        self.n_ops += 1
        return o

    def finish(self, out_bufs):
        nc = self.nc
        fin = _Op(None)
        self._apply_waits("sync", fin, self._deps(out_bufs, ()))
        self.ops["sync"].append(fin)
        st = self.stack
        esem = {}
        for k in self.ENG:
            if any(o.signal for o in self.ops[k]):
                esem[k] = st.enter_context(nc.semaphore("e_" + k))
        for i, b in enumerate(self.dbufs):
            b.dsem = st.enter_context(nc.semaphore("d_%d" % i))
        sval = {}
        for k, ops in self.ops.items():
            c = 0
            for i, o in enumerate(ops):
                if o.signal:
                    c += 1
                    sval[(k, i)] = c
        self.max_sval = max(sval.values()) if sval else 0

        def run(k, eng):
            for i, o in enumerate(self.ops[k]):
                for ev in o.waits:
                    if ev[0] == "e":
                        eng.wait_ge(esem[ev[1]], sval[(ev[1], ev[2])])
                    else:
                        eng.wait_ge(ev[1].dsem, ev[2])
                if o.emit is None:
                    continue
                ins = o.emit(eng)
                if o.dma_buf is not None:
                    ins.then_inc(o.dma_buf.dsem, 16)
                elif o.signal:
                    ins.then_inc(esem[k], 1)

        block = st.enter_context(nc.Block())

        @block.sync
        def _(e):
            run("sync", e)

        @block.tensor
        def _(e):
            run("pe", e)

        @block.scalar
        def _(e):
            run("act", e)

        @block.vector
        def _(e):
            run("dve", e)

        @block.gpsimd
        def _(e):
            run("pool", e)

        st.close()


def _esize(dt):
    return 2 if dt == BF16 else 4


class Arena:
    def __init__(self, P, words):
        self.P = P
        self.words = words
        self.t = P.stack.enter_context(P.nc.sbuf_tensor("arena", [128, words], F32))
        self.off = 0

    def reset(self):
        self.off = 0

    def alloc(self, name, shape, dtype=F32):
        n = 1
        for s in shape[1:]:
            n *= s
        words = (n * _esize(dtype) + 3) // 4
        assert self.off + words <= self.words, (name, self.off, words, self.words)
        ap = self.t[0:shape[0], self.off:self.off + words]
        self.off += words
        if dtype == BF16:
            ap = ap.bitcast(dtype)[:, 0:n]
        elif dtype != F32:
            ap = ap.bitcast(dtype)
        if len(shape) > 2:
            names = ["a%d" % i for i in range(len(shape) - 1)]
            pat = "p (" + " ".join(names) + ") -> p " + " ".join(names)
            ap = ap.rearrange(pat, **{nm: s for nm, s in zip(names, shape[1:])})
        return Buf(name, ap)


def _barrier(self):
    evs = []
    for k in ("pe", "act", "dve", "pool"):
        ops = self.ops[k]
        for i in range(len(ops) - 1, -1, -1):
            if ops[i].dma_buf is None and ops[i].emit is not None:
                evs.append(("e", k, i))
                break
    for b in self.dbufs:
        evs.append(("d", b, b.dcount))
    self.pending = {k: list(evs) for k in self.ENG}
    for b in self.live:
        if b is not self.dummy:
            self.free_slots.append(b.dsem)
            b.dsem = None
    self.live = [b for b in self.live if b is self.dummy]


Prog.barrier = _barrier
_orig_apply = Prog._apply_waits


def _apply2(self, eng, op, deps):
    pend = getattr(self, "pending", None)
    if pend and pend.get(eng):
        deps = list(deps) + pend[eng]
        pend[eng] = []
    _orig_apply(self, eng, op, deps)


Prog._apply_waits = _apply2


def MM(P, out_ap, pairs, reads, writes):
    pairs = list(pairs)

    def emit(e):
        n = len(pairs)
        ins = None
        for i, (l, r) in enumerate(pairs):
            ins = e.matmul(out_ap, l, r, start=(i == 0), stop=(i == n - 1))
        return ins
    return P.op("pe", emit, reads, writes)


def ACT(P, out, in_, func, reads, writes, bias=None, scale=None):
    kw = {}
    if bias is not None:
        kw["bias"] = bias
    if scale is not None:
        kw["scale"] = scale
    return P.op("act", lambda e: e.activation(out=out, in_=in_, func=func, **kw), reads, writes)


def TT(P, out, in0, in1, op, reads, writes, eng="dve"):
    return P.op(eng, lambda e: e.tensor_tensor(out=out, in0=in0, in1=in1, op=op), reads, writes)


def TS(P, out, in0, s1, s2, op0, op1, reads, writes, eng="dve"):
    if op1 is None:
        return P.op(eng, lambda e: e.tensor_scalar(out=out, in0=in0, scalar1=s1, scalar2=None, op0=op0), reads, writes)
    return P.op(eng, lambda e: e.tensor_scalar(out=out, in0=in0, scalar1=s1, scalar2=s2, op0=op0, op1=op1), reads, writes)


def STT(P, out, in0, scalar, in1, op0, op1, reads, writes):
    return P.op("dve", lambda e: e.scalar_tensor_tensor(out=out, in0=in0, scalar=scalar, in1=in1, op0=op0, op1=op1), reads, writes)


def CP(P, out, in_, reads, writes, eng="dve"):
    if eng == "act":
        return P.op("act", lambda e: e.copy(out=out, in_=in_), reads, writes)
    return P.op(eng, lambda e: e.tensor_copy(out=out, in_=in_), reads, writes)


def layer_norm_rows(P, t1, out_ap, out_buf, g_bc, b_bc, sm, eps_ap):
    st6 = sm["st6"]
    mv = sm["mv"]
    sd = sm["sd"]
    P.op("dve", lambda e: e.bn_stats(out=st6.ap[:, 0:6], in_=t1.ap[:, 0:512]), [t1], [st6])
    P.op("dve", lambda e: e.bn_stats(out=st6.ap[:, 6:12], in_=t1.ap[:, 512:1024]), [t1, st6], [st6])
    P.op("dve", lambda e: e.bn_aggr(out=mv.ap[:, 0:2], in_=st6.ap[:, 0:12]), [st6], [mv])
    ACT(P, sd.ap[:, 0:1], mv.ap[:, 1:2], AF.Sqrt, [mv, eps_ap], [sd], bias=eps_ap.ap[:, 0:1], scale=1.0)
    P.op("dve", lambda e: e.reciprocal(out=sd.ap[:, 1:2], in_=sd.ap[:, 0:1]), [sd], [sd])
    TS(P, t1.ap, t1.ap, mv.ap[:, 0:1], sd.ap[:, 1:2], ALU.subtract, ALU.mult, [t1, mv, sd], [t1])
    TT(P, t1.ap, t1.ap, g_bc.ap, ALU.mult, [t1, g_bc], [t1])
    TT(P, out_ap, t1.ap, b_bc.ap, ALU.add, [t1, b_bc], [out_buf] if out_buf is not t1 else [t1])


S = 4096
D = 1024
NT = S // 128
ALPHA = 2.0 ** 0.25
LN_EPS = 1e-5
QSCALE = 128.0 ** -0.5
NEG = -1e30
ARENA_WORDS = 45 * 1024
STRIP_LEN = 1792
R_LEN = 1920


def _r3(ap, p=128):
    return ap.rearrange("(c p) t -> p c t", p=p)


def build_program(phases="ACDEF", debug=False):
    nc = bass.Bass("TRN2", target_bir_lowering=False)

    def din(name, shape):
        return nc.dram_tensor(name, list(shape), F32, kind="ExternalInput").ap()

    def dscr(name, shape, dt):
        kind = "ExternalOutput" if debug else "Internal"
        return nc.dram_tensor(name, list(shape), dt, kind=kind).ap()

    T = type("T", (), {})()
    T.xT = din("xT", [D, S]); T.x = din("x", [S, D]); T.pT = din("pT", [256, S])
    T.w_in = din("w_in", [D, 7168]); T.b_in_c = din("b_in_c", [128, 56]); T.b_in = din("b_in", [7168])
    T.gln_g = din("gmlp_ln_g", [D]); T.gln_b = din("gmlp_ln_b", [D])
    T.wsT = din("wsT", [128, 8, 128]); T.bs = din("bs", [1, 8, 128]); T.tri = din("tri", [128, 128])
    T.w_pa = din("w_proj_a", [D, D]); T.w_pb = din("w_proj_b", [D, D]); T.w_out = din("w_out", [D, D])
    T.ln1_g = din("ln1_g", [D]); T.ln1_b = din("ln1_b", [D])
    T.rb_ext = din("rb_ext", [33, 8]); T.Emat = din("Emat", [33, R_LEN])
    T.sel_near = din("sel_near", [33, 16, 128]); T.sel_far = din("sel_far", [33, 16, 128])
    T.ident = din("ident", [128, 128]); T.antiid = din("antiid", [128, 128])
    T.w_q = din("peer_w_q", [D, 2048]); T.skT = din("skT", [128, 2, 128])
    T.uT = din("uT", [D, 16384]); T.vtab = din("vtab", [16384, D])
    T.w_ple = din("ple_w_proj", [256, D]); T.w_pg = din("ple_w_gate", [D, D]); T.b_pg = din("ple_b_gate", [D])
    T.ln2_g = din("ln2_g", [D]); T.ln2_b = din("ln2_b", [D])
    T.out = nc.dram_tensor("out", [S, D], F32, kind="ExternalOutput").ap()
    T.guT = dscr("s_guT", [D, S], BF16); T.qTs = dscr("s_qT", [D, S], BF16); T.kTs = dscr("s_kT", [D, S], BF16)
    T.sgaT = dscr("s_sgaT", [D, S], BF16); T.sgbT = dscr("s_sgbT", [D, S], BF16)
    T.vtok = dscr("s_vtok", [S, D], BF16); T.aT = dscr("s_aT", [D, S], BF16); T.oT = dscr("s_oT", [D, S], BF16)
    T.rrow = dscr("s_rrow", [8, R_LEN], BF16)
    T.uTb = nc.dram_tensor("s_uTb", [D, 16384], BF16, kind="Internal").ap()
    T.vtb = nc.dram_tensor("s_vtb", [16384, D], BF16, kind="Internal").ap()
    T.qpT = dscr("s_qpT", [2048, S], BF16)
    T.Cs = nc.dram_tensor("s_Cs", [NT, 128, 128, 128], BF16, kind="Internal").ap()
    T.h32 = dscr("s_h32", [S, D], F32); T.hT = dscr("s_hT", [D, S], BF16); T.f32 = dscr("s_f32", [S, D], F32)

    P = Prog(nc)
    A = Arena(P, ARENA_WORDS)
    pst = P.stack.enter_context(nc.psum_tensor("psum_all", [128, 4096], F32))
    bank = [Buf("bank%d" % i, pst[:, i * 512:(i + 1) * 512]) for i in range(8)]
    OUT = Buf("OUT")

    def bc(ap1d):
        return ap1d.partition_broadcast(128)

    def load_xT():
        xTb = [A.alloc("xTb%d" % c, [128, S], BF16) for c in range(8)]
        return xTb

    def phase_A1():
        P.barrier(); A.reset()
        xTb = load_xT()
        for c in range(8):
            P.dma("pool", xTb[c].ap, T.xT[c * 128:(c + 1) * 128, :], writes=[xTb[c]])
        wb = [A.alloc("wb%d" % i, [128, 8, 512], BF16) for i in range(2)]
        stg = [[A.alloc("stg%d_%d" % (i, j), [128, 512], BF16) for j in range(8)] for i in range(2)]
        bcol = A.alloc("bcol", [128, 56], F32)
        bqs = A.alloc("bqs", [128, 8], F32)
        P.dma("sync", bcol.ap, T.b_in_c, writes=[bcol])
        TS(P, bqs.ap, bcol.ap[:, 16:24], QSCALE, None, ALU.mult, None, [bcol], [bqs])
        groups = [("gu", 0, AF.Gelu, T.guT, None), ("q", 2048, AF.Identity, T.qTs, QSCALE),
                  ("k", 3072, AF.Identity, T.kTs, None), ("ga", 5120, AF.Sigmoid, T.sgaT, None),
                  ("gb", 6144, AF.Sigmoid, T.sgbT, None)]
        wi = si = bi = 0
        for (nm, c0, fn, dst, sc) in groups:
            for half in range(2):
                w = wb[wi % 2]; wi += 1
                P.dma("pool", w.ap, _r3(T.w_in[:, c0 + half * 512:c0 + half * 512 + 512]), writes=[w])
                for f4 in range(4):
                    fl = half * 4 + f4
                    fc = c0 // 128 + fl
                    st = stg[si % 2]; si += 1
                    if nm == "q":
                        bias_ap, bias_buf = bqs.ap[:, fl:fl + 1], bqs
                    else:
                        bias_ap, bias_buf = bcol.ap[:, fc:fc + 1], bcol
                    for tb in range(8):
                        ps = bank[bi % 4]; bi += 1
                        MM(P, ps.ap, [(w.ap[:, dc, f4 * 128:(f4 + 1) * 128], xTb[dc].ap[:, tb * 512:(tb + 1) * 512])
                                      for dc in range(8)], [w] + xTb, [ps])
                        ACT(P, st[tb].ap, ps.ap, fn, [ps, bias_buf], [st[tb]], bias=bias_ap,
                            scale=(sc if sc is not None else 1.0))
                    for tb in range(8):
                        P.dma("sync", dst[fl * 128:(fl + 1) * 128, tb * 512:(tb + 1) * 512], st[tb].ap, reads=[st[tb]])

    def phase_A2():
        P.barrier(); A.reset()
        xTb = load_xT()
        Wv = A.alloc("Wv", [128, 8, 1024], BF16)
        Wg = A.alloc("Wg", [128, 8, 1024], BF16)
        P.dma("pool", Wv.ap, _r3(T.w_in[:, 4096:5120]), writes=[Wv])
        P.dma("pool", Wg.ap, _r3(T.w_in[:, 1024:2048]), writes=[Wg])
        bvb = A.alloc("bvb", [128, 1024]); bvg = A.alloc("bvg", [128, 1024])
        lg = A.alloc("lg", [128, 1024]); lb = A.alloc("lb", [128, 1024])
        P.dma("sync", bvb.ap, bc(T.b_in[4096:5120]), writes=[bvb])
        P.dma("sync", bvg.ap, bc(T.b_in[1024:2048]), writes=[bvg])
        P.dma("sync", lg.ap, bc(T.gln_g), writes=[lg])
        P.dma("sync", lb.ap, bc(T.gln_b), writes=[lb])
        ws32 = A.alloc("ws32", [128, 8, 128]); tri = A.alloc("tri", [128, 128])
        wsb = A.alloc("wsb", [128, 8, 128], BF16)
        bs32 = A.alloc("bs32", [1, 8, 128]); ones32 = A.alloc("ones32", [1, 128])
        epsb = A.alloc("eps", [128, 1])
        P.dma("sync", ws32.ap, T.wsT, writes=[ws32])
        P.dma("sync", tri.ap, T.tri, writes=[tri])
        P.dma("sync", bs32.ap, T.bs, writes=[bs32])
        P.op("dve", lambda e: e.memset(ones32.ap, 1.0), [], [ones32])
        P.op("dve", lambda e: e.memset(epsb.ap, LN_EPS), [], [epsb])
        TT(P, wsb.ap, ws32.ap, tri.ap.unsqueeze(1).broadcast_to([128, 8, 128]), ALU.mult, [ws32, tri], [wsb])
        t1 = [A.alloc("t1_%d" % i, [128, 1024]) for i in range(2)]
        vn = [A.alloc("vn%d" % i, [128, 1024], BF16) for i in range(2)]
        vst = [A.alloc("vst%d" % i, [128, 1024], BF16) for i in range(2)]
        gub = [A.alloc("gub%d" % i, [128, 8, 512], BF16) for i in range(2)]
        ast = [[A.alloc("ast%d_%d" % (i, j), [128, 8, 128], BF16) for j in range(4)] for i in range(2)]
        sm = [{"st6": A.alloc("st6_%d" % i, [128, 12]), "mv": A.alloc("mv%d" % i, [128, 2]),
               "sd": A.alloc("sd%d" % i, [128, 2])} for i in range(2)]
        for tt in range(NT):
            tsl = slice(tt * 128, (tt + 1) * 128)
            tb, t4 = tt // 4, tt % 4
            b0 = (tt % 2) * 2
            for half in range(2):
                MM(P, bank[b0 + half].ap, [(xTb[dc].ap[:, tsl], Wv.ap[:, dc, half * 512:(half + 1) * 512]) for dc in range(8)],
                   xTb + [Wv], [bank[b0 + half]])
            vs = vst[tt % 2]
            TT(P, vs.ap, pst[:, b0 * 512:(b0 + 2) * 512], bvb.ap, ALU.add, [bank[b0], bank[b0 + 1], bvb], [vs])
            P.dma("sync", T.vtok[tsl, :], vs.ap, reads=[vs])
            for half in range(2):
                MM(P, bank[4 + half].ap, [(xTb[dc].ap[:, tsl], Wg.ap[:, dc, half * 512:(half + 1) * 512]) for dc in range(8)],
                   xTb + [Wg], [bank[4 + half]])
            t = t1[tt % 2]; v = vn[tt % 2]
            TT(P, t.ap, pst[:, 4 * 512:6 * 512], bvg.ap, ALU.add, [bank[4], bank[5], bvg], [t])
            ACT(P, t.ap, t.ap, AF.Gelu, [t], [t])
            layer_norm_rows(P, t, v.ap, v, lg, lb, sm[tt % 2], epsb)
            for g in range(8):
                bk = bank[6 + g // 4]
                oap = bk.ap[:, (g % 4) * 128:(g % 4 + 1) * 128]

                def emit(e, oap=oap, g=g, v=v):
                    e.matmul(oap, v.ap[:, g * 128:(g + 1) * 128], wsb.ap[:, g, :], start=True, stop=False)
                    return e.matmul(oap, ones32.ap[0:1, :], bs32.ap[0:1, g, :], start=False, stop=True)
                P.op("pe", emit, [v, wsb, ones32, bs32], [bk])
            if t4 == 0:
                gu = gub[tb % 2]
                P.dma("sync", gu.ap, _r3(T.guT[:, tb * 512:(tb + 1) * 512]), writes=[gu])
            gu = gub[tb % 2]
            a_s = ast[tb % 2][t4]
            TT(P, a_s.ap, pst[:, 6 * 512:8 * 512].rearrange("p (g t) -> p g t", g=8), gu.ap[:, :, t4 * 128:(t4 + 1) * 128],
               ALU.mult, [bank[6], bank[7], gu], [a_s])
            P.dma("sync", _r3(T.aT[:, tsl]), a_s.ap, reads=[a_s])

    def phase_C():
        P.barrier(); A.reset()
        RR = Buf("RR")
        rb = A.alloc("rb", [33, 8]); Em = A.alloc("Em", [33, R_LEN]); rsb = A.alloc("rsb", [8, R_LEN], BF16)
        P.dma("sync", rb.ap, T.rb_ext, writes=[rb]); P.dma("sync", Em.ap, T.Emat, writes=[Em])
        for j in range(4):
            MM(P, bank[0].ap[0:8, 0:480], [(rb.ap[0:33, :], Em.ap[0:33, j * 480:(j + 1) * 480])], [rb, Em], [bank[0]])
            CP(P, rsb.ap[:, j * 480:(j + 1) * 480], bank[0].ap[0:8, 0:480], [bank[0]], [rsb])
        P.dma("sync", T.rrow, rsb.ap, reads=[rsb], writes=[RR])
        strips = [A.alloc("strip%d" % h, [128, STRIP_LEN], BF16) for h in range(8)]
        for h in range(8):
            src = bass.AP(tensor=T.rrow.tensor, offset=h * R_LEN, ap=[[1, 128], [1, STRIP_LEN]])
            P.dma("sync", strips[h].ap, src, reads=[RR], writes=[strips[h]])
        identb = A.alloc("identb", [128, 128], BF16); ident32 = A.alloc("ident32", [128, 128])
        onesb = A.alloc("onesb", [128, 128], BF16)
        selN = A.alloc("selN", [33, 16, 128], BF16); selF = A.alloc("selF", [33, 16, 128], BF16)
        P.dma("pool", identb.ap, T.antiid, writes=[identb])
        P.dma("sync", ident32.ap, T.ident, writes=[ident32])
        P.dma("pool", selN.ap, T.sel_near, writes=[selN])
        P.dma("pool", selF.ap, T.sel_far, writes=[selF])
        P.op("dve", lambda e: e.memset(onesb.ap, 1.0), [], [onesb])
        kT = [A.alloc("kT%d" % i, [128, S], BF16) for i in range(2)]
        qT = [A.alloc("qT%d" % i, [128, S], BF16) for i in range(2)]
        vh = [A.alloc("vh%d" % i, [128, NT, 128], BF16) for i in range(2)]
        negT = [A.alloc("negT%d" % i, [33, S], BF16) for i in range(2)]
        ost = [[A.alloc("ost%d_%d" % (i, j), [128, 512], BF16) for j in range(8)] for i in range(2)]
        PT = [A.alloc("PT%d" % i, [128, 512], BF16) for i in range(3)]
        rden = [A.alloc("rden%d" % i, [128, 512]) for i in range(2)]
        kmT = [A.alloc("kmT%d" % i, [128, 16], BF16) for i in range(2)]
        km32 = [A.alloc("km32_%d" % i, [128, 16]) for i in range(2)]
        g16 = [A.alloc("g16_%d" % i, [128, 16]) for i in range(2)]
        top8 = [A.alloc("top8_%d" % i, [128, 8]) for i in range(2)]
        ngt = [A.alloc("ngt%d" % i, [128, 16]) for i in range(2)]
        for i in range(2):
            P.op("dve", lambda e, i=i: e.memset(negT[i].ap[0:33, :], 0.0), [], [negT[i]])

        def load_head(h):
            i = h % 2
            P.dma("sync", kT[i].ap, T.kTs[h * 128:(h + 1) * 128, :], writes=[kT[i]])
            P.dma("sync", qT[i].ap, T.qTs[h * 128:(h + 1) * 128, :], writes=[qT[i]])
            P.dma("sync", vh[i].ap, T.vtok[:, h * 128:(h + 1) * 128].rearrange("(n p) c -> p n c", p=128), writes=[vh[i]])

        b7g = Buf("b7g", bank[7].ap[:, 256:272])
        b7t = Buf("b7t", bank[7].ap[0:16, 0:128])

        def gate_first(h):
            i = h % 2
            P.op("dve", lambda e: e.tensor_reduce(out=km32[i].ap, in_=kT[i].ap.rearrange("p (n k) -> p n k", k=256),
                                                  axis=AX.X, op=ALU.add), [kT[i]], [km32[i]])
            CP(P, kmT[i].ap, km32[i].ap, [km32[i]], [kmT[i]])
            P.op("dve", lambda e: e.tensor_copy(out=negT[i].ap[32:33, :],
                                                in_=strips[h].ap[32:33, STRIP_LEN - 1:STRIP_LEN].broadcast_to([1, S])),
                 [strips[h]], [negT[i]])

        def gate_a(h, tt):
            i = h % 2
            qb = tt // 2
            ng = ngt[tt % 4]
            if qb <= 3:
                P.op("dve", lambda e: e.memset(ng.ap, 0.0), [], [ng])
            else:
                j = tt % 2
                MM(P, b7g.ap, [(qT[i].ap[:, tt * 128:(tt + 1) * 128], kmT[i].ap)], [qT[i], kmT[i]], [b7g])
                P.op("dve", lambda e: e.memset(g16[j].ap, NEG), [], [g16[j]])
                CP(P, g16[j].ap[:, 0:qb], b7g.ap[:, 0:qb], [b7g], [g16[j]])
                P.op("dve", lambda e: e.max(out=top8[j].ap, in_=g16[j].ap), [g16[j]], [top8[j]])
                TS(P, ng.ap, g16[j].ap, top8[j].ap[:, 2:3], None, ALU.is_ge, None, [g16[j], top8[j]], [ng])
                TS(P, ng.ap, ng.ap, 1e30, -1e30, ALU.mult, ALU.add, [ng], [ng])
                P.op("dve", lambda e: e.memset(ng.ap[:, qb:16], 0.0), [ng], [ng])

        def gate_b(h, tt):
            i = h % 2
            ng = ngt[tt % 4]
            P.op("pe", lambda e: e.transpose(out=b7t.ap, in_=ng.ap, identity=ident32.ap), [ng, ident32], [b7t])
            CP(P, negT[i].ap[0:16, tt * 128:(tt + 1) * 128], b7t.ap, [b7t], [negT[i]], eng="act")

        ngt = [A.alloc("ngtx%d" % i, [128, 16]) for i in range(4)]
        load_head(0)
        gate_first(0)
        for tt in range(NT):
            gate_a(0, tt)
            if tt >= 2:
                gate_b(0, tt - 2)
        gate_b(0, NT - 2); gate_b(0, NT - 1)

        tasks = []
        for h in range(8):
            for Q in range(8):
                for kt in range(4 * Q + 4):
                    tasks.append((h, Q, kt))
        st = {}

        def stageS(k):
            h, Q, kt = tasks[k]
            i = h % 2
            qs = slice(Q * 512, (Q + 1) * 512)
            n = kt // 2
            delta = Q * 512 - kt * 128
            far = delta >= 1024
            Sb = bank[k % 3]
            pairs = [(kT[i].ap[:, kt * 128:(kt + 1) * 128], qT[i].ap[:, qs])]
            rds = [kT[i], qT[i], negT[i]]
            if not far:
                pairs.append((identb.ap, strips[h].ap[:, delta + 384:delta + 384 + 512]))
                rds += [identb, strips[h]]
                sel = selN
            else:
                sel = selF
            pairs.append((sel.ap[0:33, n, :], negT[i].ap[0:33, qs]))
            rds.append(sel)
            MM(P, Sb.ap, pairs, rds, [Sb])
            pt = PT[k % 3]
            ACT(P, pt.ap, Sb.ap, AF.Exp, [Sb], [pt])
            st[k] = pt

        qcount = {"o": 0}

        def stageV(k):
            h, Q, kt = tasks[k]
            i = h % 2
            qs = slice(Q * 512, (Q + 1) * 512)
            nk = 4 * Q + 4
            pt = st.pop(k)
            if kt == 0:
                qcount["o"] += 1
            oi = qcount["o"] % 2
            OT = bank[3 + oi]; DEN = bank[5 + oi]
            P.op("pe", lambda e: e.matmul(OT.ap, vh[i].ap[:, kt, :], pt.ap, start=(kt == 0), stop=(kt == nk - 1)), [vh[i], pt], [OT])
            P.op("pe", lambda e: e.matmul(DEN.ap, onesb.ap, pt.ap, start=(kt == 0), stop=(kt == nk - 1)), [onesb, pt], [DEN])
            if kt == nk - 1:
                rd = rden[Q % 2]
                P.op("dve", lambda e: e.reciprocal(out=rd.ap, in_=DEN.ap), [DEN], [rd])
                TT(P, ost[i][Q].ap, OT.ap, rd.ap, ALU.mult, [OT, rd], [ost[i][Q]])
                P.dma("sync", T.oT[h * 128:(h + 1) * 128, qs], ost[i][Q].ap, reads=[ost[i][Q]])

        gate_plan = {}
        kbase = 0
        for h in range(8):
            nmain = 144
            if h + 1 < 8:
                for tt in range(NT):
                    gate_plan.setdefault(kbase + 4 + tt * 4, []).append(("a", h + 1, tt))
                    gate_plan.setdefault(kbase + 4 + tt * 4 + 10, []).append(("b", h + 1, tt))
                gate_plan.setdefault(kbase + 1, []).append(("load", h + 1, 0))
                gate_plan.setdefault(kbase + 2, []).append(("first", h + 1, 0))
            kbase += nmain
        for k in range(len(tasks) + 1):
            if k < len(tasks):
                stageS(k)
            if k >= 1:
                stageV(k - 1)
            for (kind, hh, tt) in gate_plan.get(k, []):
                if kind == "load":
                    load_head(hh)
                elif kind == "first":
                    gate_first(hh)
                elif kind == "a":
                    gate_a(hh, tt)
                else:
                    gate_b(hh, tt)

    def phase_D():
        P.barrier(); A.reset()
        WA = A.alloc("WA", [128, 8, 1024], BF16); WB = A.alloc("WB", [128, 8, 1024], BF16); WO = A.alloc("WO", [128, 8, 1024], BF16)
        P.dma("pool", WA.ap, _r3(T.w_pa), writes=[WA]); P.dma("pool", WB.ap, _r3(T.w_pb), writes=[WB])
        P.dma("pool", WO.ap, _r3(T.w_out), writes=[WO])
        lg = A.alloc("lg", [128, 1024]); lb = A.alloc("lb", [128, 1024])
        P.dma("sync", lg.ap, bc(T.ln1_g), writes=[lg]); P.dma("sync", lb.ap, bc(T.ln1_b), writes=[lb])
        identb = A.alloc("identb", [128, 128], BF16)
        P.dma("pool", identb.ap, T.ident, writes=[identb])
        epsb = A.alloc("eps", [128, 1])
        P.op("dve", lambda e: e.memset(epsb.ap, LN_EPS), [], [epsb])
        blk = {nm: [A.alloc("%s%d" % (nm, i), [128, 8, 512], BF16) for i in range(2)] for nm in ("a", "o", "ga", "gb")}
        src = {"a": T.aT, "o": T.oT, "ga": T.sgaT, "gb": T.sgbT}
        mT = [A.alloc("mT%d" % c, [128, 512], BF16) for c in range(8)]
        tmp = [A.alloc("tmp%d" % i, [128, 512]) for i in range(2)]
        tmp2 = [A.alloc("tmpb%d" % i, [128, 512]) for i in range(2)]
        xt = [A.alloc("xt%d" % i, [128, 1024]) for i in range(2)]
        t1 = [A.alloc("t1_%d" % i, [128, 1024]) for i in range(2)]
        hb = [A.alloc("hb%d" % i, [128, 1024], BF16) for i in range(2)]
        hTs = [[A.alloc("hTs%d_%d" % (i, j), [128, 8, 128], BF16) for j in range(4)] for i in range(2)]
        sm = [{"st6": A.alloc("st6_%d" % i, [128, 12]), "mv": A.alloc("mv%d" % i, [128, 2]),
               "sd": A.alloc("sd%d" % i, [128, 2])} for i in range(2)]
        bi = 0
        for tb in range(8):
            i = tb % 2
            bs_ = slice(tb * 512, (tb + 1) * 512)
            for nm in ("a", "o", "ga", "gb"):
                P.dma("sync", blk[nm][i].ap, _r3(src[nm][:, bs_]), writes=[blk[nm][i]])
            for n in range(8):
                pa = bank[bi % 4]; pb = bank[(bi + 1) % 4]; bi += 2
                MM(P, pa.ap, [(WA.ap[:, wc, n * 128:(n + 1) * 128], blk["a"][i].ap[:, wc, :]) for wc in range(8)], [WA, blk["a"][i]], [pa])
                MM(P, pb.ap, [(WB.ap[:, wc, n * 128:(n + 1) * 128], blk["o"][i].ap[:, wc, :]) for wc in range(8)], [WB, blk["o"][i]], [pb])
                ta = tmp[n % 2]; tb2 = tmp2[n % 2]
                TT(P, ta.ap, pa.ap, blk["ga"][i].ap[:, n, :], ALU.mult, [pa, blk["ga"][i]], [ta])
                TT(P, tb2.ap, pb.ap, blk["gb"][i].ap[:, n, :], ALU.mult, [pb, blk["gb"][i]], [tb2])
                TT(P, mT[n].ap, ta.ap, tb2.ap, ALU.add, [ta, tb2], [mT[n]], eng="pool")
            for t4 in range(4):
                tt = tb * 4 + t4
                j = tt % 2
                tsl = slice(tt * 128, (tt + 1) * 128)
                P.dma("sync", xt[j].ap, T.x[tsl, :], writes=[xt[j]])
                for half in range(2):
                    MM(P, bank[4 + half].ap, [(mT[wc].ap[:, t4 * 128:(t4 + 1) * 128], WO.ap[:, wc, half * 512:(half + 1) * 512]) for wc in range(8)],
                       mT + [WO], [bank[4 + half]])
                STT(P, t1[j].ap, xt[j].ap, ALPHA, pst[:, 4 * 512:6 * 512], ALU.mult, ALU.add, [xt[j], bank[4], bank[5]], [t1[j]])
                layer_norm_rows(P, t1[j], t1[j].ap, t1[j], lg, lb, sm[j], epsb)
                P.dma("sync", T.h32[tsl, :], t1[j].ap, reads=[t1[j]])
                CP(P, hb[j].ap, t1[j].ap, [t1[j]], [hb[j]], eng="act")
                psT = bank[6 + j]
                psT_ap = psT.ap.bitcast(BF16)[:, 0:1024]

                def emit(e, j=j, psT_ap=psT_ap):
                    ins = None
                    for c in range(8):
                        ins = e.transpose(out=psT_ap[:, c * 128:(c + 1) * 128], in_=hb[j].ap[:, c * 128:(c + 1) * 128], identity=identb.ap)
                    return ins
                P.op("pe", emit, [hb[j], identb], [psT])
                hs = hTs[i][t4]
                CP(P, hs.ap, psT_ap.rearrange("p (c t) -> p c t", c=8), [psT], [hs], eng="act")
                P.dma("sync", _r3(T.hT[:, tsl]), hs.ap, reads=[hs])

    def phase_F():
        P.barrier(); A.reset()
        Wp = A.alloc("Wp", [128, 2, 1024], BF16); Wg = A.alloc("Wg", [128, 8, 1024], BF16)
        P.dma("pool", Wp.ap, _r3(T.w_ple), writes=[Wp]); P.dma("pool", Wg.ap, _r3(T.w_pg), writes=[Wg])
        pTb = [A.alloc("pTb%d" % c, [128, S], BF16) for c in range(2)]
        for c in range(2):
            P.dma("pool", pTb[c].ap, T.pT[c * 128:(c + 1) * 128, :], writes=[pTb[c]])
        bg = A.alloc("bg", [128, 1024]); lg = A.alloc("lg", [128, 1024]); lb = A.alloc("lb", [128, 1024])
        P.dma("sync", bg.ap, bc(T.b_pg), writes=[bg])
        P.dma("sync", lg.ap, bc(T.ln2_g), writes=[lg]); P.dma("sync", lb.ap, bc(T.ln2_b), writes=[lb])
        epsb = A.alloc("eps", [128, 1])
        P.op("dve", lambda e: e.memset(epsb.ap, LN_EPS), [], [epsb])
        hTb = [A.alloc("hTb%d" % i, [128, 8, 512], BF16) for i in range(2)]
        ht = [A.alloc("ht%d" % i, [128, 1024]) for i in range(2)]
        ft = [A.alloc("ft%d" % i, [128, 1024]) for i in range(2)]
        t1 = [A.alloc("t1_%d" % i, [128, 1024]) for i in range(2)]
        t2 = [A.alloc("t2_%d" % i, [128, 1024]) for i in range(2)]
        sm = [{"st6": A.alloc("st6_%d" % i, [128, 12]), "mv": A.alloc("mv%d" % i, [128, 2]),
               "sd": A.alloc("sd%d" % i, [128, 2])} for i in range(2)]
        for tt in range(NT):
            tb, t4 = tt // 4, tt % 4
            j = tt % 2
            tsl = slice(tt * 128, (tt + 1) * 128)
            if t4 == 0:
                P.dma("sync", hTb[tb % 2].ap, _r3(T.hT[:, tb * 512:(tb + 1) * 512]), writes=[hTb[tb % 2]])
            hblk = hTb[tb % 2]
            P.dma("sync", ht[j].ap, T.h32[tsl, :], writes=[ht[j]])
            P.dma("sync", ft[j].ap, T.f32[tsl, :], writes=[ft[j]])
            b0 = 4 * j
            for half in range(2):
                MM(P, bank[b0 + half].ap, [(pTb[kc].ap[:, tsl], Wp.ap[:, kc, half * 512:(half + 1) * 512]) for kc in range(2)],
                   pTb + [Wp], [bank[b0 + half]])
                MM(P, bank[b0 + 2 + half].ap, [(hblk.ap[:, dc, t4 * 128:(t4 + 1) * 128], Wg.ap[:, dc, half * 512:(half + 1) * 512]) for dc in range(8)],
                   [hblk, Wg], [bank[b0 + 2 + half]])
            TT(P, t2[j].ap, pst[:, (b0 + 2) * 512:(b0 + 4) * 512], bg.ap, ALU.add, [bank[b0 + 2], bank[b0 + 3], bg], [t2[j]])
            ACT(P, t2[j].ap, t2[j].ap, AF.Sigmoid, [t2[j]], [t2[j]])
            TT(P, t2[j].ap, t2[j].ap, pst[:, b0 * 512:(b0 + 2) * 512], ALU.mult, [t2[j], bank[b0], bank[b0 + 1]], [t2[j]])
            STT(P, t1[j].ap, ht[j].ap, ALPHA, ft[j].ap, ALU.mult, ALU.add, [ht[j], ft[j]], [t1[j]])
            TT(P, t1[j].ap, t1[j].ap, t2[j].ap, ALU.add, [t1[j], t2[j]], [t1[j]], eng="pool")
            layer_norm_rows(P, t1[j], t1[j].ap, t1[j], lg, lb, sm[j], epsb)
            P.dma("sync", T.out[tsl, :], t1[j].ap, reads=[t1[j]])

    def phase_E_stub():
        P.barrier(); A.reset()
        z = A.alloc("z", [128, 1024])
        P.op("dve", lambda e: e.memset(z.ap, 0.0), [], [z])
        for tt in range(NT):
            P.dma("sync", T.f32[tt * 128:(tt + 1) * 128, :], z.ap, reads=[z])

    def phase_E0():
        for r in range(16):
            P.dma("pool", T.uTb[r * 64:(r + 1) * 64, :], T.uT[r * 64:(r + 1) * 64, :])
        for r in range(16):
            P.dma("pool", T.vtb[r * 1024:(r + 1) * 1024, :], T.vtab[r * 1024:(r + 1) * 1024, :])

    def phase_E1():
        P.barrier(); A.reset()
        hTb = [A.alloc("hTb%d" % c, [128, S], BF16) for c in range(8)]
        for c in range(8):
            P.dma("sync", hTb[c].ap, T.hT[c * 128:(c + 1) * 128, :], writes=[hTb[c]])
        wb = [A.alloc("wb%d" % i, [128, 8, 512], BF16) for i in range(2)]
        stg = [[A.alloc("stg%d_%d" % (i, j), [128, 512], BF16) for j in range(8)] for i in range(2)]
        si = bi = 0
        for cg in range(4):
            w = wb[cg % 2]
            P.dma("pool", w.ap, _r3(T.w_q[:, cg * 512:(cg + 1) * 512]), writes=[w])
            for f4 in range(4):
                fc = cg * 4 + f4
                st = stg[si % 2]; si += 1
                for tb in range(8):
                    ps = bank[bi % 4]; bi += 1
                    MM(P, ps.ap, [(w.ap[:, dc, f4 * 128:(f4 + 1) * 128], hTb[dc].ap[:, tb * 512:(tb + 1) * 512]) for dc in range(8)],
                       [w] + hTb, [ps])
                    if tb % 2 == 0:
                        CP(P, st[tb].ap, ps.ap, [ps], [st[tb]], eng="act")
                    else:
                        CP(P, st[tb].ap, ps.ap, [ps], [st[tb]], eng="dve")
                for tb in range(8):
                    P.dma("sync", T.qpT[fc * 128:(fc + 1) * 128, tb * 512:(tb + 1) * 512], st[tb].ap, reads=[st[tb]])

    def phase_E2():
        P.barrier(); A.reset()
        I32 = mybir.dt.int32
        U32 = mybir.dt.uint32
        skb = A.alloc("skb", [128, 2, 128], BF16)
        P.dma("pool", skb.ap, T.skT, writes=[skb])
        identb = A.alloc("identb", [128, 128], BF16)
        P.dma("pool", identb.ap, T.ident, writes=[identb])
        ident32 = A.alloc("ident32", [128, 128])
        P.dma("sync", ident32.ap, T.ident, writes=[ident32])
        iot_i = A.alloc("iot_i", [128, 128], I32)
        iot_f = A.alloc("iot_f", [128, 128])
        P.op("pool", lambda e: e.iota(iot_i.ap, pattern=[[1, 128]], base=0, channel_multiplier=0), [], [iot_i])
        P.op("pool", lambda e: e.tensor_copy(out=iot_f.ap, in_=iot_i.ap), [iot_i], [iot_f])
        qpb = A.alloc("qpb", [128, 16, 128], BF16)
        Ssb = A.alloc("Ssb", [128, 16, 128])
        S2 = [A.alloc("S2_%d" % i, [128, 128]) for i in range(4)]
        top = A.alloc("top", [128, 16, 16])
        idx_u = A.alloc("idx_u", [128, 8, 16], U32)
        topc = [Buf("topc%d" % c, top.ap[:, c, :]) for c in range(16)]
        Ssbc = [Buf("Ssbc%d" % c, Ssb.ap[:, c, :]) for c in range(16)]
        idxc = [Buf("idxc%d" % h, idx_u.ap[:, h, :]) for h in range(8)]
        idx_f = A.alloc("idx_f", [128, 128])
        idxT = A.alloc("idxT", [128, 128])
        cand2 = [A.alloc("cand2_%d" % i, [128, 256]) for i in range(2)]
        best = A.alloc("best", [128, 8, 16])
        eb = A.alloc("eb", [128, 8, 16])
        Z = A.alloc("Z", [128, 8]); lnZ = A.alloc("lnZ", [128, 8])
        wp = A.alloc("wp", [128, 8, 16]); taup = A.alloc("taup", [128, 8])
        X = [A.alloc("X%d" % i, [128, 16, 128]) for i in range(2)]
        cand = X[1]
        cand_ap = X[1].ap.rearrange("p a b -> p (a b)").rearrange("p (h c) -> p h c", h=8)
        Rall = A.alloc("Rall", [128, 128, 128], BF16)
        Rtm = [Buf("Rtm%d" % h, Rall.ap[:, h * 16:(h + 1) * 16, :]) for h in range(8)]
        Lsm = [A.alloc("Lsm%d" % i, [128, 128, 128], BF16) for i in range(2)]
        Rsm = A.alloc("Rsm", [128, 128, 128], BF16)
        Cst = [A.alloc("Cst%d" % i, [128, 16, 128], BF16) for i in range(2)]
        topv = top.ap.rearrange("p (h two) a -> p h two a", two=2)
        Sv = Ssb.ap.rearrange("p (h two) n -> p h two n", two=2)
        b0t = Buf("b0t", bank[0].ap)
        cnt = {"sc": 0, "tr": 0, "cp": 0, "cs": 0}

        def stage_scores(tt):
            tsl = slice(tt * 128, (tt + 1) * 128)
            P.dma("sync", qpb.ap, T.qpT[:, tsl].rearrange("(c k) t -> k c t", k=128), writes=[qpb])
            for cg in range(4):
                bk = bank[cnt["sc"] % 2]; cnt["sc"] += 1

                def emit(e, bk=bk, cg=cg):
                    ins = None
                    for cl in range(4):
                        c = cg * 4 + cl
                        ins = e.matmul(bk.ap[:, cl * 128:(cl + 1) * 128], qpb.ap[:, c, :], skb.ap[:, c % 2, :], start=True, stop=True)
                    return ins
                P.op("pe", emit, [qpb, skb], [bk])
                CP(P, Ssb.ap[:, cg * 4:(cg + 1) * 4, :], bk.ap.rearrange("p (c n) -> p c n", c=4), [bk], Ssbc[cg * 4:(cg + 1) * 4], eng="act")

        def stage_route1(tt):
            for c0 in range(0, 16, 4):
                cs_ = list(range(c0, c0 + 4))
                for c in cs_:
                    P.op("dve", lambda e, c=c: e.max(out=topc[c].ap[:, 0:8], in_=Ssbc[c].ap), [Ssbc[c]], [topc[c]])
                for c in cs_:
                    s2 = S2[c % 4]
                    P.op("dve", lambda e, c=c, s2=s2: e.match_replace(out=s2.ap, in_to_replace=topc[c].ap[:, 0:8], in_values=Ssbc[c].ap, imm_value=NEG),
                         [Ssbc[c], topc[c]], [s2])
                for c in cs_:
                    s2 = S2[c % 4]
                    P.op("dve", lambda e, c=c, s2=s2: e.max(out=topc[c].ap[:, 8:16], in_=s2.ap), [s2, topc[c]], [topc[c]])
                for c in cs_:
                    if c % 2 == 0:
                        h = c // 2
                        P.op("dve", lambda e, c=c, h=h: e.max_index(out=idxc[h].ap[:, 0:8], in_max=topc[c].ap[:, 0:8], in_values=Ssbc[c].ap),
                             [Ssbc[c], topc[c]], [idxc[h]])
                        P.op("dve", lambda e, c=c, h=h: e.max_index(out=idxc[h].ap[:, 8:16], in_max=topc[c].ap[:, 8:16], in_values=Ssbc[c].ap),
                             [Ssbc[c], topc[c], idxc[h]], [idxc[h]])
            TT(P, cand_ap.rearrange("p h (a b) -> p h a b", a=16),
               topv[:, :, 0, :].unsqueeze(3).broadcast_to([128, 8, 16, 16]),
               topv[:, :, 1, :].unsqueeze(2).broadcast_to([128, 8, 16, 16]), ALU.add, topc, [cand])
            for h in range(8):
                c2 = cand2[h % 2]
                P.op("dve", lambda e, h=h: e.max(out=best.ap[:, h, 0:8], in_=cand_ap[:, h, :]), [cand], [best])
                P.op("dve", lambda e, h=h, c2=c2: e.match_replace(out=c2.ap, in_to_replace=best.ap[:, h, 0:8], in_values=cand_ap[:, h, :], imm_value=NEG),
                     [cand, best], [c2])
                P.op("dve", lambda e, h=h, c2=c2: e.max(out=best.ap[:, h, 8:16], in_=c2.ap), [c2], [best])
            CP(P, idx_f.ap, idx_u.ap.rearrange("p h a -> p (h a)"), idxc, [idx_f])

        def stage_route2(tt):
            ACT(P, eb.ap, best.ap, AF.Exp, [best], [eb])
            P.op("dve", lambda e: e.tensor_reduce(out=Z.ap, in_=eb.ap, axis=AX.X, op=ALU.add), [eb], [Z])
            ACT(P, lnZ.ap, Z.ap, AF.Ln, [Z], [lnZ])
            TT(P, wp.ap, topv[:, :, 0, :], lnZ.ap.unsqueeze(2).broadcast_to([128, 8, 16]), ALU.subtract, topc + [lnZ], [wp])
            STT(P, taup.ap, best.ap[:, :, 15], -1e-5, lnZ.ap, ALU.add, ALU.subtract, [best, lnZ], [taup])

        def stage_buildL(tt):
            L = Lsm[tt % 2]
            P.op("pe", lambda e: e.transpose(out=b0t.ap[:, 0:128], in_=idx_f.ap, identity=ident32.ap), [idx_f, ident32], [b0t, bank[0]])
            CP(P, idxT.ap, b0t.ap[:, 0:128], [b0t, bank[0]], [idxT], eng="act")
            TT(P, L.ap, iot_f.ap.unsqueeze(1).broadcast_to([128, 128, 128]),
               idxT.ap.unsqueeze(2).broadcast_to([128, 128, 128]), ALU.is_equal, [iot_f, idxT], [L])

        def stage_buildR(tt):
            for h in range(8):
                x = X[h % 2]
                TT(P, x.ap, Sv[:, h, 1, :].unsqueeze(1).broadcast_to([128, 16, 128]),
                   wp.ap[:, h, :].unsqueeze(2).broadcast_to([128, 16, 128]), ALU.add, [Ssbc[2 * h + 1], wp], [x])
                ACT(P, Rtm[h].ap, x.ap, AF.Exp, [x], [Rtm[h]])
                STT(P, Rtm[h].ap, x.ap, taup.ap[:, h:h + 1], Rtm[h].ap, ALU.is_ge, ALU.mult, [x, taup, Rtm[h]], [Rtm[h]])

        def stage_trR(tt):
            for ig in range(16):
                bk = bank[2 + cnt["tr"] % 2]; cnt["tr"] += 1
                bkv = bk.ap.bitcast(BF16)[:, 0:1024]

                def emit(e, bkv=bkv, ig=ig):
                    ins = None
                    for q in range(8):
                        i_ = ig * 8 + q
                        ins = e.transpose(out=bkv[:, q * 128:(q + 1) * 128], in_=Rall.ap[:, :, i_], identity=identb.ap)
                    return ins
                P.op("pe", emit, Rtm + [identb], [bk])
                CP(P, Rsm.ap[:, ig * 8:(ig + 1) * 8, :], bkv.rearrange("p (i t) -> p i t", i=8), [bk], [Rsm], eng="act")

        def stage_cmm(tt):
            L = Lsm[tt % 2]
            for t16 in range(8):
                cs = Cst[cnt["cs"] % 2]; cnt["cs"] += 1
                for t4 in range(4):
                    bk = bank[4 + cnt["cp"] % 4]; cnt["cp"] += 1
                    tb_ = t16 * 16 + t4 * 4

                    def emit(e, bk=bk, tb_=tb_):
                        ins = None
                        for q in range(4):
                            t = tb_ + q
                            ins = e.matmul(bk.ap[:, q * 128:(q + 1) * 128], Rsm.ap[:, :, t], L.ap[:, t, :], start=True, stop=True)
                        return ins
                    P.op("pe", emit, [Rsm, L], [bk])
                    CP(P, cs.ap[:, t4 * 4:(t4 + 1) * 4, :], bk.ap.rearrange("p (t i) -> p t i", t=4), [bk], [cs], eng="act")
                P.dma("sync", T.Cs[tt][:, t16 * 16:(t16 + 1) * 16, :], cs.ap, reads=[cs])

        stage_scores(0)
        stage_route1(0)
        stage_route2(0)
        stage_buildL(0)
        stage_buildR(0)
        for tt in range(NT):
            if tt + 1 < NT:
                stage_scores(tt + 1)
                stage_route1(tt + 1)
            stage_trR(tt)
            stage_cmm(tt)
            if tt + 1 < NT:
                stage_route2(tt + 1)
                stage_buildL(tt + 1)
                stage_buildR(tt + 1)

    def phase_E3():
        P.barrier(); A.reset()
        TB = 256
        hTb = [A.alloc("hTb%d" % i, [128, 8, TB], BF16) for i in range(2)]
        ysb = [A.alloc("ysb%d" % i, [128, 1024]) for i in range(2)]
        NG = 4
        gl = [A.alloc("glx%d" % i, [128, TB]) for i in range(NG)]
        G = [A.alloc("Gx%d" % i, [128, TB], BF16) for i in range(NG)]
        NB = 3
        Ug = [A.alloc("Ugx%d" % i, [128, 8, 256], BF16) for i in range(NB)]
        Vg = [A.alloc("Vgx%d" % i, [128, 2, 1024], BF16) for i in range(NB)]
        Cp = [A.alloc("Cp%d" % i, [128, 2, 128, 128], BF16) for i in range(2)]
        NBLK = S // TB
        tasks = [(blk, c) for blk in range(NBLK) for c in range(128)]
        state = {}

        def load_group(gi_):
            blk, cg = gi_ // 64, gi_ % 64
            if blk >= NBLK:
                return
            ug = Ug[gi_ % NB]; vg = Vg[gi_ % NB]
            P.dma("sync", ug.ap, _r3(T.uTb[:, cg * 256:(cg + 1) * 256]), writes=[ug])
            P.dma("sync", vg.ap, T.vtb[cg * 256:(cg + 1) * 256, :].rearrange("(c p) d -> p c d", p=128), writes=[vg])

        def load_block(blk):
            if blk >= NBLK:
                return
            hb = hTb[blk % 2]
            P.dma("sync", hb.ap, _r3(T.hT[:, blk * TB:(blk + 1) * TB]), writes=[hb])
            cp = Cp[blk % 2]
            for tl in range(2):
                P.dma("pool", cp.ap[:, tl], T.Cs[blk * 2 + tl], writes=[cp])

        load_block(0)
        load_group(0)
        load_group(1)

        def stage1(k):
            blk, c = tasks[k]
            hb = hTb[blk % 2]
            cp = Cp[blk % 2]
            cg, cl = c // 2, c % 2
            gi_ = blk * 64 + cg
            ug = Ug[gi_ % NB]; vg = Vg[gi_ % NB]
            hp = bank[k % 4]
            MM(P, hp.ap[:, 0:TB], [(ug.ap[:, dc, cl * 128:(cl + 1) * 128], hb.ap[:, dc, :]) for dc in range(8)], [ug, hb], [hp])
            g1 = gl[k % NG]; g2 = G[k % NG]
            ACT(P, g1.ap, hp.ap[:, 0:TB], AF.Gelu, [hp], [g1])
            TT(P, g2.ap.rearrange("p (a t) -> p a t", a=2), g1.ap.rearrange("p (a t) -> p a t", a=2), cp.ap[:, :, :, c],
               ALU.mult, [g1, cp], [g2])
            state[k] = (g2, vg, cl)

        def stage2(k):
            blk, c = tasks[k]
            g2, vg, cl = state.pop(k)
            for tl in range(2):
                for half in range(2):
                    yb = bank[4 + tl * 2 + half]
                    P.op("pe", lambda e, yb=yb, tl=tl, half=half:
                         e.matmul(yb.ap, g2.ap[:, tl * 128:(tl + 1) * 128], vg.ap[:, cl, half * 512:(half + 1) * 512],
                                  start=(c == 0), stop=(c == 127)), [g2, vg], [yb])
            if c == 127:
                for tl in range(2):
                    tt = blk * 2 + tl
                    y = ysb[tl]
                    CP(P, y.ap, pst[:, (4 + tl * 2) * 512:(6 + tl * 2) * 512], [bank[4 + tl * 2], bank[5 + tl * 2]], [y], eng=("act" if tl else "dve"))
                    P.dma("sync", T.f32[tt * 128:(tt + 1) * 128, :], y.ap, reads=[y])

        SK = 2
        for k in range(len(tasks) + SK):
            if k < len(tasks):
                stage1(k)
            if k >= SK:
                stage2(k - SK)
            if k < len(tasks):
                blk, c = tasks[k]
                if c % 2 == 1:
                    load_group(blk * 64 + c // 2 + 2)
                if c == 127:
                    load_block(blk + 1)

    def phase_E():
        phase_E0()
        phase_E1()
        phase_E2()
        phase_E3()

    if "A" in phases:
        phase_A1()
        phase_A2()
    if "C" in phases:
        phase_C()
    if "D" in phases:
        phase_D()
    if "E" in phases:
        phase_E()
    for ch, fn in (("0", phase_E0), ("1", phase_E1), ("2", phase_E2), ("3", phase_E3)):
        if ch in phases:
            fn()
    if "e" in phases:
        phase_E_stub()
    if "F" in phases:
        phase_F()
    P.barrier()
    P.finish([])
    return nc, P


def _t5_bucket_np(d):
    import math
    n = np.maximum(d, 0)
    nf = np.maximum(n, 16).astype(np.float32)
    large = 16 + (np.log(nf / np.float32(16)) / np.float32(math.log(1024 / 16)) * np.float32(16)).astype(np.int32)
    large = np.minimum(large, 31)
    return np.where(n < 16, n, large)


def _constants():
    c = {}
    c["tri"] = (np.arange(128)[:, None] <= np.arange(128)[None, :]).astype(np.float32)
    E = np.zeros((33, R_LEN), np.float32)
    y = np.arange(R_LEN)
    d = y - 511
    bk = _t5_bucket_np(d)
    for i in range(R_LEN):
        if d[i] < 0:
            E[32, i] = 1.0
        else:
            E[bk[i], i] = 1.0
    c["Emat"] = E
    sn = np.zeros((33, 16, 128), np.float32)
    sf = np.zeros((33, 16, 128), np.float32)
    for n in range(16):
        sn[n, n, :] = 1.0
        sf[n, n, :] = 1.0
        sf[32, n, :] = 1.0
    c["sel_near"] = sn
    c["sel_far"] = sf
    c["ident"] = np.eye(128, dtype=np.float32)
    c["antiid"] = np.ascontiguousarray(np.eye(128, dtype=np.float32)[::-1])
    return c


def prep_shared(inp):
    f = lambda a: np.ascontiguousarray(a, dtype=np.float32)
    sh = {}
    sh["w_in"] = f(inp["w_in"][0])
    sh["b_in_c"] = f(inp["b_in"][0].reshape(56, 128).T)
    sh["b_in"] = f(inp["b_in"][0])
    sh["gmlp_ln_g"] = f(inp["gmlp_ln_g"][0]); sh["gmlp_ln_b"] = f(inp["gmlp_ln_b"][0])
    sh["wsT"] = f(inp["gmlp_w_s"][0].transpose(2, 0, 1))
    sh["bs"] = f(inp["gmlp_b_s"][0][None])
    for k in ("w_proj_a", "w_proj_b", "w_out", "ln1_g", "ln1_b", "peer_w_q", "ple_w_proj", "ple_w_gate",
              "ple_b_gate", "ln2_g", "ln2_b"):
        sh[k] = f(inp[k][0])
    sh["rb_ext"] = f(np.concatenate([inp["rel_bias"], np.full((1, 8), NEG, np.float32)], axis=0))
    sh["skT"] = f(inp["peer_sub_keys"][0].transpose(2, 0, 1))
    sh["uT"] = f(inp["peer_u"][0].T)
    sh["vtab"] = f(inp["peer_v"][0])
    sh.update(_constants())
    return sh


def prep_core(inp, b):
    f = lambda a: np.ascontiguousarray(a, dtype=np.float32)
    return {"xT": f(inp["x"][b].T), "x": f(inp["x"][b]), "pT": f(inp["p"][0, b].T)}


PHASES = "ACDEF"


def kernel(**inputs):
    inp = {k: np.asarray(v) for k, v in inputs.items()}
    nc, P = build_program(phases=PHASES, debug=False)
    sh = prep_shared(inp)
    in_maps = [dict(sh, **prep_core(inp, b)) for b in range(8)]
    res = run_bass_kernel_spmd(nc, in_maps, core_ids=list(range(8)))
    out = np.stack([np.asarray(r["out"], dtype=np.float32) for r in res.results], axis=0)
    return out
```

```python
import contextlib
import numpy as np
import concourse.bass as bass
import concourse.mybir as mybir
from concourse.bass_utils import run_bass_kernel_spmd

F32 = mybir.dt.float32
BF16 = mybir.dt.bfloat16
AF = mybir.ActivationFunctionType
ALU = mybir.AluOpType
AX = mybir.AxisListType


class Buf:
    __slots__ = ("name", "ap", "lw", "rd", "dsem", "dcount")

    def __init__(self, name, ap=None):
        self.name = name
        self.ap = ap
        self.lw = None
        self.rd = []
        self.dsem = None
        self.dcount = 0


class Slot:
    __slots__ = ("dsem", "dcount")

    def __init__(self):
        self.dsem = None
        self.dcount = 0


class _Op:
    __slots__ = ("emit", "waits", "signal", "known", "dma_buf", "dma_val")

    def __init__(self, emit):
        self.emit = emit
        self.waits = []
        self.signal = False
        self.known = None
        self.dma_buf = None
        self.dma_val = 0


class Prog:
    ENG = {"pe": "tensor", "act": "scalar", "dve": "vector", "pool": "gpsimd", "sync": "sync"}
    ALIAS = {"gpsimd": "pool", "scalar": "act", "vector": "dve", "tensor": "pe"}

    def __init__(self, nc):
        self.nc = nc
        self.ops = {k: [] for k in self.ENG}
        self.known = {k: {} for k in self.ENG}
        self.stack = contextlib.ExitStack()
        self.dbufs = []
        self.dummy = Buf("dummy")
        self.n_ops = 0
        self.free_slots = []
        self.live = []

    def sbuf(self, name, shape, dtype):
        t = self.stack.enter_context(self.nc.sbuf_tensor(name, shape, dtype))
        return Buf(name, t)

    def psum(self, name, shape, dtype):
        t = self.stack.enter_context(self.nc.psum_tensor(name, shape, dtype))
        return Buf(name, t)

    def dram_buf(self, name):
        return Buf(name)

    def view(self, name, ap):
        return Buf(name, ap)

    def _deps(self, reads, writes):
        deps = []
        for b in reads:
            if b.lw is not None:
                deps.append(b.lw)
        for b in writes:
            if b.lw is not None:
                deps.append(b.lw)
            deps.extend(b.rd)
        return deps

    def _apply_waits(self, eng, op, deps):
        known = self.known[eng]
        changed = False
        for ev in deps:
            if ev[0] == "e":
                _, e2, idx = ev
                if e2 == eng and eng == "pe":
                    continue
                key = e2
                if known.get(key, -1) >= idx:
                    continue
                if not changed:
                    known = dict(known)
                    changed = True
                if e2 != eng or True:
                    op.waits.append(ev)
                    self.ops[e2][idx].signal = True
                known[key] = idx
                k2 = self.ops[e2][idx].known
                if k2:
                    for kk, vv in k2.items():
                        if known.get(kk, -1) < vv:
                            known[kk] = vv
            else:
                _, b, val = ev
                key = ("d", id(b))
                if known.get(key, -1) >= val:
                    continue
                if not changed:
                    known = dict(known)
                    changed = True
                op.waits.append(ev)
                known[key] = val
        self.known[eng] = known
        op.known = known

    def _commit(self, ev, reads, writes):
        for b in reads:
            b.rd.append(ev)
        for b in writes:
            b.lw = ev
            b.rd = []

    def op(self, eng, emit, reads=(), writes=()):
        eng = self.ALIAS.get(eng, eng)
        o = _Op(emit)
        self._apply_waits(eng, o, self._deps(reads, writes))
        idx = len(self.ops[eng])
        self.ops[eng].append(o)
        self._commit(("e", eng, idx), reads, writes)
        self.n_ops += 1
        return o

    def dma(self, q, out_ap, in_ap, reads=(), writes=(), sem_buf=None, **kw):
        q = self.ALIAS.get(q, q)
        if sem_buf is None:
            for b in list(writes) + list(reads):
                if b.ap is not None:
                    sem_buf = b
                    break
            else:
                sem_buf = self.dummy
        o = _Op(lambda e: e.dma_start(out=out_ap, in_=in_ap, **kw))
        self._apply_waits(q, o, self._deps(reads, writes))
        if sem_buf.dsem is None:
            if self.free_slots:
                sem_buf.dsem = self.free_slots.pop()
            else:
                sem_buf.dsem = Slot()
                self.dbufs.append(sem_buf.dsem)
            self.live.append(sem_buf)
        sl = sem_buf.dsem
        sl.dcount += 16
        o.dma_buf = sl
        o.dma_val = sl.dcount
        self.ops[q].append(o)
        self._commit(("d", sl, sl.dcount), reads, writes)
        self.n_ops += 1
        return o

    def finish(self, out_bufs):
        nc = self.nc
        fin = _Op(None)
        self._apply_waits("sync", fin, self._deps(out_bufs, ()))
        self.ops["sync"].append(fin)
        st = self.stack
        esem = {}
        for k in self.ENG:
            if any(o.signal for o in self.ops[k]):
                esem[k] = st.enter_context(nc.semaphore("e_" + k))
        for i, b in enumerate(self.dbufs):
            b.dsem = st.enter_context(nc.semaphore("d_%d" % i))
        sval = {}
        for k, ops in self.ops.items():
            c = 0
            for i, o in enumerate(ops):
                if o.signal:
                    c += 1
                    sval[(k, i)] = c
        self.max_sval = max(sval.values()) if sval else 0

        def run(k, eng):
            for i, o in enumerate(self.ops[k]):
                for ev in o.waits:
                    if ev[0] == "e":
                        eng.wait_ge(esem[ev[1]], sval[(ev[1], ev[2])])
                    else:
                        eng.wait_ge(ev[1].dsem, ev[2])
                if o.emit is None:
                    continue
                ins = o.emit(eng)
                if o.dma_buf is not None:
                    ins.then_inc(o.dma_buf.dsem, 16)
                elif o.signal:
                    ins.then_inc(esem[k], 1)

        block = st.enter_context(nc.Block())

        @block.sync
        def _(e):
            run("sync", e)

        @block.tensor
        def _(e):
            run("pe", e)

        @block.scalar
        def _(e):
            run("act", e)

        @block.vector
        def _(e):
            run("dve", e)

        @block.gpsimd
        def _(e):
            run("pool", e)

        st.close()


def _esize(dt):
    return 2 if dt == BF16 else 4


class Arena:
    def __init__(self, P, words):
        self.P = P
        self.words = words
        self.t = P.stack.enter_context(P.nc.sbuf_tensor("arena", [128, words], F32))
        self.off = 0

    def reset(self):
        self.off = 0

    def alloc(self, name, shape, dtype=F32):
        n = 1
        for s in shape[1:]:
            n *= s
        words = (n * _esize(dtype) + 3) // 4
        assert self.off + words <= self.words, (name, self.off, words, self.words)
        ap = self.t[0:shape[0], self.off:self.off + words]
        self.off += words
        if dtype == BF16:
            ap = ap.bitcast(dtype)[:, 0:n]
        elif dtype != F32:
            ap = ap.bitcast(dtype)
        if len(shape) > 2:
            names = ["a%d" % i for i in range(len(shape) - 1)]
            pat = "p (" + " ".join(names) + ") -> p " + " ".join(names)
            ap = ap.rearrange(pat, **{nm: s for nm, s in zip(names, shape[1:])})
        return Buf(name, ap)


def _barrier(self):
    evs = []
    for k in ("pe", "act", "dve", "pool"):
        ops = self.ops[k]
        for i in range(len(ops) - 1, -1, -1):
            if ops[i].dma_buf is None and ops[i].emit is not None:
                evs.append(("e", k, i))
                break
    for b in self.dbufs:
        evs.append(("d", b, b.dcount))
    self.pending = {k: list(evs) for k in self.ENG}
    for b in self.live:
        if b is not self.dummy:
            self.free_slots.append(b.dsem)
            b.dsem = None
    self.live = [b for b in self.live if b is self.dummy]


Prog.barrier = _barrier
_orig_apply = Prog._apply_waits


def _apply2(self, eng, op, deps):
    pend = getattr(self, "pending", None)
    if pend and pend.get(eng):
        deps = list(deps) + pend[eng]
        pend[eng] = []
    _orig_apply(self, eng, op, deps)


Prog._apply_waits = _apply2


def MM(P, out_ap, pairs, reads, writes):
    pairs = list(pairs)

    def emit(e):
        n = len(pairs)
        ins = None
        for i, (l, r) in enumerate(pairs):
            ins = e.matmul(out_ap, l, r, start=(i == 0), stop=(i == n - 1))
        return ins
    return P.op("pe", emit, reads, writes)


def ACT(P, out, in_, func, reads, writes, bias=None, scale=None):
    kw = {}
    if bias is not None:
        kw["bias"] = bias
    if scale is not None:
        kw["scale"] = scale
    return P.op("act", lambda e: e.activation(out=out, in_=in_, func=func, **kw), reads, writes)


def TT(P, out, in0, in1, op, reads, writes, eng="dve"):
    return P.op(eng, lambda e: e.tensor_tensor(out=out, in0=in0, in1=in1, op=op), reads, writes)


def TS(P, out, in0, s1, s2, op0, op1, reads, writes, eng="dve"):
    if op1 is None:
        return P.op(eng, lambda e: e.tensor_scalar(out=out, in0=in0, scalar1=s1, scalar2=None, op0=op0), reads, writes)
    return P.op(eng, lambda e: e.tensor_scalar(out=out, in0=in0, scalar1=s1, scalar2=s2, op0=op0, op1=op1), reads, writes)


def STT(P, out, in0, scalar, in1, op0, op1, reads, writes):
    return P.op("dve", lambda e: e.scalar_tensor_tensor(out=out, in0=in0, scalar=scalar, in1=in1, op0=op0, op1=op1), reads, writes)


def CP(P, out, in_, reads, writes, eng="dve"):
    if eng == "act":
        return P.op("act", lambda e: e.copy(out=out, in_=in_), reads, writes)
    return P.op(eng, lambda e: e.tensor_copy(out=out, in_=in_), reads, writes)


def layer_norm_rows(P, t1, out_ap, out_buf, g_bc, b_bc, sm, eps_ap):
    st6 = sm["st6"]
    mv = sm["mv"]
    sd = sm["sd"]
    P.op("dve", lambda e: e.bn_stats(out=st6.ap[:, 0:6], in_=t1.ap[:, 0:512]), [t1], [st6])
    P.op("dve", lambda e: e.bn_stats(out=st6.ap[:, 6:12], in_=t1.ap[:, 512:1024]), [t1, st6], [st6])
    P.op("dve", lambda e: e.bn_aggr(out=mv.ap[:, 0:2], in_=st6.ap[:, 0:12]), [st6], [mv])
    ACT(P, sd.ap[:, 0:1], mv.ap[:, 1:2], AF.Sqrt, [mv, eps_ap], [sd], bias=eps_ap.ap[:, 0:1], scale=1.0)
    P.op("dve", lambda e: e.reciprocal(out=sd.ap[:, 1:2], in_=sd.ap[:, 0:1]), [sd], [sd])
    TS(P, t1.ap, t1.ap, mv.ap[:, 0:1], sd.ap[:, 1:2], ALU.subtract, ALU.mult, [t1, mv, sd], [t1])
    TT(P, t1.ap, t1.ap, g_bc.ap, ALU.mult, [t1, g_bc], [t1])
    TT(P, out_ap, t1.ap, b_bc.ap, ALU.add, [t1, b_bc], [out_buf] if out_buf is not t1 else [t1])


S = 4096
D = 1024
NT = S // 128
ALPHA = 2.0 ** 0.25
LN_EPS = 1e-5
QSCALE = 128.0 ** -0.5
NEG = -1e30
ARENA_WORDS = 45 * 1024
STRIP_LEN = 1792
R_LEN = 1920


def _r3(ap, p=128):
    return ap.rearrange("(c p) t -> p c t", p=p)


def build_program(phases="ACDEF", debug=False):
    nc = bass.Bass("TRN2", target_bir_lowering=False)

    def din(name, shape):
        return nc.dram_tensor(name, list(shape), F32, kind="ExternalInput").ap()

    def dscr(name, shape, dt):
        kind = "ExternalOutput" if debug else "Internal"
        return nc.dram_tensor(name, list(shape), dt, kind=kind).ap()

    T = type("T", (), {})()
    T.xT = din("xT", [D, S]); T.x = din("x", [S, D]); T.pT = din("pT", [256, S])
    T.w_in = din("w_in", [D, 7168]); T.b_in_c = din("b_in_c", [128, 56]); T.b_in = din("b_in", [7168])
    T.gln_g = din("gmlp_ln_g", [D]); T.gln_b = din("gmlp_ln_b", [D])
    T.wsT = din("wsT", [128, 8, 128]); T.bs = din("bs", [1, 8, 128]); T.tri = din("tri", [128, 128])
    T.w_pa = din("w_proj_a", [D, D]); T.w_pb = din("w_proj_b", [D, D]); T.w_out = din("w_out", [D, D])
    T.ln1_g = din("ln1_g", [D]); T.ln1_b = din("ln1_b", [D])
    T.rb_ext = din("rb_ext", [33, 8]); T.Emat = din("Emat", [33, R_LEN])
    T.sel_near = din("sel_near", [33, 16, 128]); T.sel_far = din("sel_far", [33, 16, 128])
    T.ident = din("ident", [128, 128]); T.antiid = din("antiid", [128, 128])
    T.w_q = din("peer_w_q", [D, 2048]); T.skT = din("skT", [128, 2, 128])
    T.uT = din("uT", [D, 16384]); T.vtab = din("vtab", [16384, D])
    T.w_ple = din("ple_w_proj", [256, D]); T.w_pg = din("ple_w_gate", [D, D]); T.b_pg = din("ple_b_gate", [D])
    T.ln2_g = din("ln2_g", [D]); T.ln2_b = din("ln2_b", [D])
    T.out = nc.dram_tensor("out", [S, D], F32, kind="ExternalOutput").ap()
    T.guT = dscr("s_guT", [D, S], BF16); T.qTs = dscr("s_qT", [D, S], BF16); T.kTs = dscr("s_kT", [D, S], BF16)
    T.sgaT = dscr("s_sgaT", [D, S], BF16); T.sgbT = dscr("s_sgbT", [D, S], BF16)
    T.vtok = dscr("s_vtok", [S, D], BF16); T.aT = dscr("s_aT", [D, S], BF16); T.oT = dscr("s_oT", [D, S], BF16)
    T.rrow = dscr("s_rrow", [8, R_LEN], BF16)
    T.uTb = nc.dram_tensor("s_uTb", [D, 16384], BF16, kind="Internal").ap()
    T.vtb = nc.dram_tensor("s_vtb", [16384, D], BF16, kind="Internal").ap()
    T.qpT = dscr("s_qpT", [2048, S], BF16)
    T.Cs = nc.dram_tensor("s_Cs", [NT, 128, 128, 128], BF16, kind="Internal").ap()
    T.h32 = dscr("s_h32", [S, D], F32); T.hT = dscr("s_hT", [D, S], BF16); T.f32 = dscr("s_f32", [S, D], F32)

    P = Prog(nc)
    A = Arena(P, ARENA_WORDS)
    pst = P.stack.enter_context(nc.psum_tensor("psum_all", [128, 4096], F32))
    bank = [Buf("bank%d" % i, pst[:, i * 512:(i + 1) * 512]) for i in range(8)]
    OUT = Buf("OUT")

    def bc(ap1d):
        return ap1d.partition_broadcast(128)

    def load_xT():
        xTb = [A.alloc("xTb%d" % c, [128, S], BF16) for c in range(8)]
        return xTb

    def phase_A1():
        P.barrier(); A.reset()
        xTb = load_xT()
        for c in range(8):
            P.dma("pool", xTb[c].ap, T.xT[c * 128:(c + 1) * 128, :], writes=[xTb[c]])
        wb = [A.alloc("wb%d" % i, [128, 8, 512], BF16) for i in range(2)]
        stg = [[A.alloc("stg%d_%d" % (i, j), [128, 512], BF16) for j in range(8)] for i in range(2)]
        bcol = A.alloc("bcol", [128, 56], F32)
        bqs = A.alloc("bqs", [128, 8], F32)
        P.dma("sync", bcol.ap, T.b_in_c, writes=[bcol])
        TS(P, bqs.ap, bcol.ap[:, 16:24], QSCALE, None, ALU.mult, None, [bcol], [bqs])
        e0_state = {"done": False}
        groups = [("gu", 0, AF.Gelu, T.guT, None), ("q", 2048, AF.Identity, T.qTs, QSCALE),
                  ("k", 3072, AF.Identity, T.kTs, None), ("ga", 5120, AF.Sigmoid, T.sgaT, None),
                  ("gb", 6144, AF.Sigmoid, T.sgbT, None)]
        wi = si = bi = 0
        for (nm, c0, fn, dst, sc) in groups:
            for half in range(2):
                w = wb[wi % 2]; wi += 1
                P.dma("pool", w.ap, _r3(T.w_in[:, c0 + half * 512:c0 + half * 512 + 512]), writes=[w])
                if wi == 2 and "E" in phases:
                    phase_E0()
                for f4 in range(4):
                    fl = half * 4 + f4
                    fc = c0 // 128 + fl
                    st = stg[si % 2]; si += 1
                    if nm == "q":
                        bias_ap, bias_buf = bqs.ap[:, fl:fl + 1], bqs
                    else:
                        bias_ap, bias_buf = bcol.ap[:, fc:fc + 1], bcol
                    for tb in range(8):
                        ps = bank[bi % 4]; bi += 1
                        MM(P, ps.ap, [(w.ap[:, dc, f4 * 128:(f4 + 1) * 128], xTb[dc].ap[:, tb * 512:(tb + 1) * 512])
                                      for dc in range(8)], [w] + xTb, [ps])
                        ACT(P, st[tb].ap, ps.ap, fn, [ps, bias_buf], [st[tb]], bias=bias_ap,
                            scale=(sc if sc is not None else 1.0))
                    for tb in range(8):
                        P.dma("sync", dst[fl * 128:(fl + 1) * 128, tb * 512:(tb + 1) * 512], st[tb].ap, reads=[st[tb]])

    def phase_A2():
        P.barrier(); A.reset()
        xTb = load_xT()
        Wv = A.alloc("Wv", [128, 8, 1024], BF16)
        Wg = A.alloc("Wg", [128, 8, 1024], BF16)
        P.dma("pool", Wv.ap, _r3(T.w_in[:, 4096:5120]), writes=[Wv])
        P.dma("pool", Wg.ap, _r3(T.w_in[:, 1024:2048]), writes=[Wg])
        bvb = A.alloc("bvb", [128, 1024]); bvg = A.alloc("bvg", [128, 1024])
        lg = A.alloc("lg", [128, 1024]); lb = A.alloc("lb", [128, 1024])
        P.dma("sync", bvb.ap, bc(T.b_in[4096:5120]), writes=[bvb])
        P.dma("sync", bvg.ap, bc(T.b_in[1024:2048]), writes=[bvg])
        P.dma("sync", lg.ap, bc(T.gln_g), writes=[lg])
        P.dma("sync", lb.ap, bc(T.gln_b), writes=[lb])
        ws32 = A.alloc("ws32", [128, 8, 128]); tri = A.alloc("tri", [128, 128])
        wsb = A.alloc("wsb", [128, 8, 128], BF16)
        bs32 = A.alloc("bs32", [1, 8, 128]); ones32 = A.alloc("ones32", [1, 128])
        epsb = A.alloc("eps", [128, 1])
        P.dma("sync", ws32.ap, T.wsT, writes=[ws32])
        P.dma("sync", tri.ap, T.tri, writes=[tri])
        P.dma("sync", bs32.ap, T.bs, writes=[bs32])
        P.op("dve", lambda e: e.memset(ones32.ap, 1.0), [], [ones32])
        P.op("dve", lambda e: e.memset(epsb.ap, LN_EPS), [], [epsb])
        TT(P, wsb.ap, ws32.ap, tri.ap.unsqueeze(1).broadcast_to([128, 8, 128]), ALU.mult, [ws32, tri], [wsb])
        t1 = [A.alloc("t1_%d" % i, [128, 1024]) for i in range(2)]
        vn = [A.alloc("vn%d" % i, [128, 1024], BF16) for i in range(2)]
        vst = [A.alloc("vst%d" % i, [128, 1024], BF16) for i in range(2)]
        gub = [A.alloc("gub%d" % i, [128, 8, 512], BF16) for i in range(2)]
        ast = [[A.alloc("ast%d_%d" % (i, j), [128, 8, 128], BF16) for j in range(4)] for i in range(2)]
        sm = [{"st6": A.alloc("st6_%d" % i, [128, 12]), "mv": A.alloc("mv%d" % i, [128, 2]),
               "sd": A.alloc("sd%d" % i, [128, 2])} for i in range(2)]
        pend_sp = []

        def spatial(tt, v):
            tsl = slice(tt * 128, (tt + 1) * 128)
            tb, t4 = tt // 4, tt % 4
            for g in range(8):
                bk = bank[6 + g // 4]
                oap = bk.ap[:, (g % 4) * 128:(g % 4 + 1) * 128]

                def emit(e, oap=oap, g=g, v=v):
                    e.matmul(oap, v.ap[:, g * 128:(g + 1) * 128], wsb.ap[:, g, :], start=True, stop=False)
                    return e.matmul(oap, ones32.ap[0:1, :], bs32.ap[0:1, g, :], start=False, stop=True)
                P.op("pe", emit, [v, wsb, ones32, bs32], [bk])
            if t4 == 0:
                gu = gub[tb % 2]
                P.dma("sync", gu.ap, _r3(T.guT[:, tb * 512:(tb + 1) * 512]), writes=[gu])
            gu = gub[tb % 2]
            a_s = ast[tb % 2][t4]
            TT(P, a_s.ap, pst[:, 6 * 512:8 * 512].rearrange("p (g t) -> p g t", g=8), gu.ap[:, :, t4 * 128:(t4 + 1) * 128],
               ALU.mult, [bank[6], bank[7], gu], [a_s])
            P.dma("sync", _r3(T.aT[:, tsl]), a_s.ap, reads=[a_s])
        for tt in range(NT):
            tsl = slice(tt * 128, (tt + 1) * 128)
            tb, t4 = tt // 4, tt % 4
            b0 = (tt % 2) * 2
            for half in range(2):
                MM(P, bank[b0 + half].ap, [(xTb[dc].ap[:, tsl], Wv.ap[:, dc, half * 512:(half + 1) * 512]) for dc in range(8)],
                   xTb + [Wv], [bank[b0 + half]])
            vs = vst[tt % 2]
            TT(P, vs.ap, pst[:, b0 * 512:(b0 + 2) * 512], bvb.ap, ALU.add, [bank[b0], bank[b0 + 1], bvb], [vs])
            P.dma("sync", T.vtok[tsl, :], vs.ap, reads=[vs])
            for half in range(2):
                MM(P, bank[4 + half].ap, [(xTb[dc].ap[:, tsl], Wg.ap[:, dc, half * 512:(half + 1) * 512]) for dc in range(8)],
                   xTb + [Wg], [bank[4 + half]])
            t = t1[tt % 2]; v = vn[tt % 2]
            TT(P, t.ap, pst[:, 4 * 512:6 * 512], bvg.ap, ALU.add, [bank[4], bank[5], bvg], [t])
            ACT(P, t.ap, t.ap, AF.Gelu, [t], [t])
            layer_norm_rows(P, t, v.ap, v, lg, lb, sm[tt % 2], epsb)
            pend_sp.append((tt, v))
            if len(pend_sp) > 1:
                spatial(*pend_sp.pop(0))
        spatial(*pend_sp.pop(0))


    def phase_C():
        P.barrier(); A.reset()
        RR = Buf("RR")
        rb = A.alloc("rb", [33, 8]); Em = A.alloc("Em", [33, R_LEN]); rsb = A.alloc("rsb", [8, R_LEN], BF16)
        P.dma("sync", rb.ap, T.rb_ext, writes=[rb]); P.dma("sync", Em.ap, T.Emat, writes=[Em])
        for j in range(4):
            MM(P, bank[0].ap[0:8, 0:480], [(rb.ap[0:33, :], Em.ap[0:33, j * 480:(j + 1) * 480])], [rb, Em], [bank[0]])
            CP(P, rsb.ap[:, j * 480:(j + 1) * 480], bank[0].ap[0:8, 0:480], [bank[0]], [rsb])
        P.dma("sync", T.rrow, rsb.ap, reads=[rsb], writes=[RR])
        strips = [A.alloc("strip%d" % h, [128, STRIP_LEN], BF16) for h in range(8)]
        for h in range(8):
            src = bass.AP(tensor=T.rrow.tensor, offset=h * R_LEN, ap=[[1, 128], [1, STRIP_LEN]])
            P.dma("sync", strips[h].ap, src, reads=[RR], writes=[strips[h]])
        identb = A.alloc("identb", [128, 128], BF16); ident32 = A.alloc("ident32", [128, 128])
        onesb = A.alloc("onesb", [128, 128], BF16)
        selN = A.alloc("selN", [33, 16, 128], BF16); selF = A.alloc("selF", [33, 16, 128], BF16)
        P.dma("pool", identb.ap, T.antiid, writes=[identb])
        P.dma("sync", ident32.ap, T.ident, writes=[ident32])
        P.dma("pool", selN.ap, T.sel_near, writes=[selN])
        P.dma("pool", selF.ap, T.sel_far, writes=[selF])
        P.op("dve", lambda e: e.memset(onesb.ap, 1.0), [], [onesb])
        kT = [A.alloc("kT%d" % i, [128, S], BF16) for i in range(2)]
        qT = [A.alloc("qT%d" % i, [128, S], BF16) for i in range(2)]
        vh = [A.alloc("vh%d" % i, [128, NT, 128], BF16) for i in range(2)]
        negT = [A.alloc("negT%d" % i, [33, S], BF16) for i in range(2)]
        ost = [[A.alloc("ost%d_%d" % (i, j), [128, 512], BF16) for j in range(8)] for i in range(2)]
        PT = [A.alloc("PT%d" % i, [128, 512], BF16) for i in range(3)]
        rden = [A.alloc("rden%d" % i, [128, 512]) for i in range(2)]
        kmT = [A.alloc("kmT%d" % i, [128, 16], BF16) for i in range(2)]
        km32 = [A.alloc("km32_%d" % i, [128, 16]) for i in range(2)]
        g16 = [A.alloc("g16_%d" % i, [128, 16]) for i in range(2)]
        top8 = [A.alloc("top8_%d" % i, [128, 8]) for i in range(2)]
        ngt = [A.alloc("ngt%d" % i, [128, 16]) for i in range(2)]
        for i in range(2):
            P.op("dve", lambda e, i=i: e.memset(negT[i].ap[0:33, :], 0.0), [], [negT[i]])

        def load_head(h):
            i = h % 2
            P.dma("sync", kT[i].ap, T.kTs[h * 128:(h + 1) * 128, :], writes=[kT[i]])
            P.dma("sync", qT[i].ap, T.qTs[h * 128:(h + 1) * 128, :], writes=[qT[i]])
            P.dma("sync", vh[i].ap, T.vtok[:, h * 128:(h + 1) * 128].rearrange("(n p) c -> p n c", p=128), writes=[vh[i]])

        b7g = Buf("b7g", bank[7].ap[:, 256:272])
        b7t = Buf("b7t", bank[7].ap[0:16, 0:128])

        def gate_first(h):
            i = h % 2
            P.op("dve", lambda e: e.tensor_reduce(out=km32[i].ap, in_=kT[i].ap.rearrange("p (n k) -> p n k", k=256),
                                                  axis=AX.X, op=ALU.add), [kT[i]], [km32[i]])
            CP(P, kmT[i].ap, km32[i].ap, [km32[i]], [kmT[i]])
            P.op("dve", lambda e: e.tensor_copy(out=negT[i].ap[32:33, :],
                                                in_=strips[h].ap[32:33, STRIP_LEN - 1:STRIP_LEN].broadcast_to([1, S])),
                 [strips[h]], [negT[i]])

        def gate_a(h, tt):
            i = h % 2
            qb = tt // 2
            ng = ngt[tt % 4]
            if qb <= 3:
                P.op("dve", lambda e: e.memset(ng.ap, 0.0), [], [ng])
            else:
                j = tt % 2
                MM(P, b7g.ap, [(qT[i].ap[:, tt * 128:(tt + 1) * 128], kmT[i].ap)], [qT[i], kmT[i]], [b7g])
                P.op("dve", lambda e: e.memset(g16[j].ap, NEG), [], [g16[j]])
                CP(P, g16[j].ap[:, 0:qb], b7g.ap[:, 0:qb], [b7g], [g16[j]])
                P.op("dve", lambda e: e.max(out=top8[j].ap, in_=g16[j].ap), [g16[j]], [top8[j]])
                TS(P, ng.ap, g16[j].ap, top8[j].ap[:, 2:3], None, ALU.is_ge, None, [g16[j], top8[j]], [ng])
                TS(P, ng.ap, ng.ap, 1e30, -1e30, ALU.mult, ALU.add, [ng], [ng])
                P.op("dve", lambda e: e.memset(ng.ap[:, qb:16], 0.0), [ng], [ng])

        def gate_b(h, tt):
            i = h % 2
            ng = ngt[tt % 4]
            P.op("pe", lambda e: e.transpose(out=b7t.ap, in_=ng.ap, identity=ident32.ap), [ng, ident32], [b7t])
            CP(P, negT[i].ap[0:16, tt * 128:(tt + 1) * 128], b7t.ap, [b7t], [negT[i]], eng="act")

        ngt = [A.alloc("ngtx%d" % i, [128, 16]) for i in range(4)]
        load_head(0)
        gate_first(0)
        for tt in range(NT):
            gate_a(0, tt)
            if tt >= 2:
                gate_b(0, tt - 2)
        gate_b(0, NT - 2); gate_b(0, NT - 1)

        tasks = []
        for h in range(8):
            for Q in range(8):
                for kt in range(4 * Q + 4):
                    tasks.append((h, Q, kt))
        st = {}

        def stageS(k):
            h, Q, kt = tasks[k]
            i = h % 2
            qs = slice(Q * 512, (Q + 1) * 512)
            n = kt // 2
            delta = Q * 512 - kt * 128
            far = delta >= 1024
            Sb = bank[k % 3]
            pairs = [(kT[i].ap[:, kt * 128:(kt + 1) * 128], qT[i].ap[:, qs])]
            rds = [kT[i], qT[i], negT[i]]
            if not far:
                pairs.append((identb.ap, strips[h].ap[:, delta + 384:delta + 384 + 512]))
                rds += [identb, strips[h]]
                sel = selN
            else:
                sel = selF
            pairs.append((sel.ap[0:33, n, :], negT[i].ap[0:33, qs]))
            rds.append(sel)
            MM(P, Sb.ap, pairs, rds, [Sb])
            pt = PT[k % 3]
            ACT(P, pt.ap, Sb.ap, AF.Exp, [Sb], [pt])
            st[k] = pt

        qcount = {"o": 0}

        def stageV(k):
            h, Q, kt = tasks[k]
            i = h % 2
            qs = slice(Q * 512, (Q + 1) * 512)
            nk = 4 * Q + 4
            pt = st.pop(k)
            if kt == 0:
                qcount["o"] += 1
            oi = qcount["o"] % 2
            OT = bank[3 + oi]; DEN = bank[5 + oi]
            P.op("pe", lambda e: e.matmul(OT.ap, vh[i].ap[:, kt, :], pt.ap, start=(kt == 0), stop=(kt == nk - 1)), [vh[i], pt], [OT])
            P.op("pe", lambda e: e.matmul(DEN.ap, onesb.ap, pt.ap, start=(kt == 0), stop=(kt == nk - 1)), [onesb, pt], [DEN])
            if kt == nk - 1:
                rd = rden[Q % 2]
                P.op("dve", lambda e: e.reciprocal(out=rd.ap, in_=DEN.ap), [DEN], [rd])
                TT(P, ost[i][Q].ap, OT.ap, rd.ap, ALU.mult, [OT, rd], [ost[i][Q]])
                P.dma("sync", T.oT[h * 128:(h + 1) * 128, qs], ost[i][Q].ap, reads=[ost[i][Q]])

        gate_plan = {}
        kbase = 0
        for h in range(8):
            nmain = 144
            if h + 1 < 8:
                for tt in range(NT):
                    gate_plan.setdefault(kbase + 4 + tt * 4, []).append(("a", h + 1, tt))
                    gate_plan.setdefault(kbase + 4 + tt * 4 + 10, []).append(("b", h + 1, tt))
                gate_plan.setdefault(kbase + 1, []).append(("load", h + 1, 0))
                gate_plan.setdefault(kbase + 2, []).append(("first", h + 1, 0))
            kbase += nmain
        for k in range(len(tasks) + 1):
            if k < len(tasks):
                stageS(k)
            if k >= 1:
                stageV(k - 1)
            for (kind, hh, tt) in gate_plan.get(k, []):
                if kind == "load":
                    load_head(hh)
                elif kind == "first":
                    gate_first(hh)
                elif kind == "a":
                    gate_a(hh, tt)
                else:
                    gate_b(hh, tt)

    def phase_D():
        P.barrier(); A.reset()
        WA = A.alloc("WA", [128, 8, 1024], BF16); WB = A.alloc("WB", [128, 8, 1024], BF16); WO = A.alloc("WO", [128, 8, 1024], BF16)
        P.dma("pool", WA.ap, _r3(T.w_pa), writes=[WA]); P.dma("pool", WB.ap, _r3(T.w_pb), writes=[WB])
        P.dma("pool", WO.ap, _r3(T.w_out), writes=[WO])
        lg = A.alloc("lg", [128, 1024]); lb = A.alloc("lb", [128, 1024])
        P.dma("sync", lg.ap, bc(T.ln1_g), writes=[lg]); P.dma("sync", lb.ap, bc(T.ln1_b), writes=[lb])
        identb = A.alloc("identb", [128, 128], BF16)
        P.dma("pool", identb.ap, T.ident, writes=[identb])
        epsb = A.alloc("eps", [128, 1])
        P.op("dve", lambda e: e.memset(epsb.ap, LN_EPS), [], [epsb])
        blk = {nm: [A.alloc("%s%d" % (nm, i), [128, 8, 512], BF16) for i in range(2)] for nm in ("a", "o", "ga", "gb")}
        src = {"a": T.aT, "o": T.oT, "ga": T.sgaT, "gb": T.sgbT}
        mT = [A.alloc("mT%d" % c, [128, 512], BF16) for c in range(8)]
        tmp = [A.alloc("tmp%d" % i, [128, 512]) for i in range(2)]
        tmp2 = [A.alloc("tmpb%d" % i, [128, 512]) for i in range(2)]
        xt = [A.alloc("xt%d" % i, [128, 1024]) for i in range(2)]
        t1 = [A.alloc("t1_%d" % i, [128, 1024]) for i in range(2)]
        hb = [A.alloc("hb%d" % i, [128, 1024], BF16) for i in range(2)]
        hTs = [[A.alloc("hTs%d_%d" % (i, j), [128, 8, 128], BF16) for j in range(4)] for i in range(2)]
        sm = [{"st6": A.alloc("st6_%d" % i, [128, 12]), "mv": A.alloc("mv%d" % i, [128, 2]),
               "sd": A.alloc("sd%d" % i, [128, 2])} for i in range(2)]
        bi = 0
        for tb in range(8):
            i = tb % 2
            bs_ = slice(tb * 512, (tb + 1) * 512)
            for nm in ("a", "o", "ga", "gb"):
                P.dma("sync", blk[nm][i].ap, _r3(src[nm][:, bs_]), writes=[blk[nm][i]])
            for n in range(8):
                pa = bank[bi % 4]; pb = bank[(bi + 1) % 4]; bi += 2
                MM(P, pa.ap, [(WA.ap[:, wc, n * 128:(n + 1) * 128], blk["a"][i].ap[:, wc, :]) for wc in range(8)], [WA, blk["a"][i]], [pa])
                MM(P, pb.ap, [(WB.ap[:, wc, n * 128:(n + 1) * 128], blk["o"][i].ap[:, wc, :]) for wc in range(8)], [WB, blk["o"][i]], [pb])
                ta = tmp[n % 2]; tb2 = tmp2[n % 2]
                TT(P, ta.ap, pa.ap, blk["ga"][i].ap[:, n, :], ALU.mult, [pa, blk["ga"][i]], [ta])
                TT(P, tb2.ap, pb.ap, blk["gb"][i].ap[:, n, :], ALU.mult, [pb, blk["gb"][i]], [tb2])
                TT(P, mT[n].ap, ta.ap, tb2.ap, ALU.add, [ta, tb2], [mT[n]])
            for t4 in range(4):
                tt = tb * 4 + t4
                j = tt % 2
                tsl = slice(tt * 128, (tt + 1) * 128)
                P.dma("sync", xt[j].ap, T.x[tsl, :], writes=[xt[j]])
                for half in range(2):
                    MM(P, bank[4 + half].ap, [(mT[wc].ap[:, t4 * 128:(t4 + 1) * 128], WO.ap[:, wc, half * 512:(half + 1) * 512]) for wc in range(8)],
                       mT + [WO], [bank[4 + half]])
                STT(P, t1[j].ap, xt[j].ap, ALPHA, pst[:, 4 * 512:6 * 512], ALU.mult, ALU.add, [xt[j], bank[4], bank[5]], [t1[j]])
                layer_norm_rows(P, t1[j], t1[j].ap, t1[j], lg, lb, sm[j], epsb)
                P.dma("sync", T.h32[tsl, :], t1[j].ap, reads=[t1[j]])
                CP(P, hb[j].ap, t1[j].ap, [t1[j]], [hb[j]], eng="act")
                psT = bank[6 + j]
                psT_ap = psT.ap.bitcast(BF16)[:, 0:1024]

                def emit(e, j=j, psT_ap=psT_ap):
                    ins = None
                    for c in range(8):
                        ins = e.transpose(out=psT_ap[:, c * 128:(c + 1) * 128], in_=hb[j].ap[:, c * 128:(c + 1) * 128], identity=identb.ap)
                    return ins
                P.op("pe", emit, [hb[j], identb], [psT])
                hs = hTs[i][t4]
                CP(P, hs.ap, psT_ap.rearrange("p (c t) -> p c t", c=8), [psT], [hs], eng="act")
                P.dma("sync", _r3(T.hT[:, tsl]), hs.ap, reads=[hs])

    def phase_F():
        P.barrier(); A.reset()
        Wp = A.alloc("Wp", [128, 2, 1024], BF16); Wg = A.alloc("Wg", [128, 8, 1024], BF16)
        P.dma("pool", Wp.ap, _r3(T.w_ple), writes=[Wp]); P.dma("pool", Wg.ap, _r3(T.w_pg), writes=[Wg])
        pTb = [A.alloc("pTb%d" % c, [128, S], BF16) for c in range(2)]
        for c in range(2):
            P.dma("pool", pTb[c].ap, T.pT[c * 128:(c + 1) * 128, :], writes=[pTb[c]])
        bg = A.alloc("bg", [128, 1024]); lg = A.alloc("lg", [128, 1024]); lb = A.alloc("lb", [128, 1024])
        P.dma("sync", bg.ap, bc(T.b_pg), writes=[bg])
        P.dma("sync", lg.ap, bc(T.ln2_g), writes=[lg]); P.dma("sync", lb.ap, bc(T.ln2_b), writes=[lb])
        epsb = A.alloc("eps", [128, 1])
        P.op("dve", lambda e: e.memset(epsb.ap, LN_EPS), [], [epsb])
        hTb = [A.alloc("hTb%d" % i, [128, 8, 512], BF16) for i in range(2)]
        ht = [A.alloc("ht%d" % i, [128, 1024]) for i in range(2)]
        ft = [A.alloc("ft%d" % i, [128, 1024]) for i in range(2)]
        t1 = [A.alloc("t1_%d" % i, [128, 1024]) for i in range(2)]
        t2 = [A.alloc("t2_%d" % i, [128, 1024]) for i in range(2)]
        sm = [{"st6": A.alloc("st6_%d" % i, [128, 12]), "mv": A.alloc("mv%d" % i, [128, 2]),
               "sd": A.alloc("sd%d" % i, [128, 2])} for i in range(2)]
        for tt in range(NT):
            tb, t4 = tt // 4, tt % 4
            j = tt % 2
            tsl = slice(tt * 128, (tt + 1) * 128)
            if t4 == 0:
                P.dma("sync", hTb[tb % 2].ap, _r3(T.hT[:, tb * 512:(tb + 1) * 512]), writes=[hTb[tb % 2]])
            hblk = hTb[tb % 2]
            P.dma("sync", ht[j].ap, T.h32[tsl, :], writes=[ht[j]])
            P.dma("sync", ft[j].ap, T.f32[tsl, :], writes=[ft[j]])
            b0 = 4 * j
            for half in range(2):
                MM(P, bank[b0 + half].ap, [(pTb[kc].ap[:, tsl], Wp.ap[:, kc, half * 512:(half + 1) * 512]) for kc in range(2)],
                   pTb + [Wp], [bank[b0 + half]])
                MM(P, bank[b0 + 2 + half].ap, [(hblk.ap[:, dc, t4 * 128:(t4 + 1) * 128], Wg.ap[:, dc, half * 512:(half + 1) * 512]) for dc in range(8)],
                   [hblk, Wg], [bank[b0 + 2 + half]])
            TT(P, t2[j].ap, pst[:, (b0 + 2) * 512:(b0 + 4) * 512], bg.ap, ALU.add, [bank[b0 + 2], bank[b0 + 3], bg], [t2[j]])
            ACT(P, t2[j].ap, t2[j].ap, AF.Sigmoid, [t2[j]], [t2[j]])
            TT(P, t2[j].ap, t2[j].ap, pst[:, b0 * 512:(b0 + 2) * 512], ALU.mult, [t2[j], bank[b0], bank[b0 + 1]], [t2[j]])
            STT(P, t1[j].ap, ht[j].ap, ALPHA, ft[j].ap, ALU.mult, ALU.add, [ht[j], ft[j]], [t1[j]])
            TT(P, t1[j].ap, t1[j].ap, t2[j].ap, ALU.add, [t1[j], t2[j]], [t1[j]])
            layer_norm_rows(P, t1[j], t1[j].ap, t1[j], lg, lb, sm[j], epsb)
            P.dma("sync", T.out[tsl, :], t1[j].ap, reads=[t1[j]])

    def phase_E_stub():
        P.barrier(); A.reset()
        z = A.alloc("z", [128, 1024])
        P.op("dve", lambda e: e.memset(z.ap, 0.0), [], [z])
        for tt in range(NT):
            P.dma("sync", T.f32[tt * 128:(tt + 1) * 128, :], z.ap, reads=[z])

    def phase_E0():
        for r in range(16):
            P.dma("pool", T.uTb[r * 64:(r + 1) * 64, :], T.uT[r * 64:(r + 1) * 64, :])
        for r in range(16):
            P.dma("pool", T.vtb[r * 1024:(r + 1) * 1024, :], T.vtab[r * 1024:(r + 1) * 1024, :])

    def phase_E1():
        P.barrier(); A.reset()
        hTb = [A.alloc("hTb%d" % c, [128, S], BF16) for c in range(8)]
        for c in range(8):
            P.dma("sync", hTb[c].ap, T.hT[c * 128:(c + 1) * 128, :], writes=[hTb[c]])
        wb = [A.alloc("wb%d" % i, [128, 8, 512], BF16) for i in range(2)]
        stg = [[A.alloc("stg%d_%d" % (i, j), [128, 512], BF16) for j in range(8)] for i in range(2)]
        si = bi = 0
        for cg in range(4):
            w = wb[cg % 2]
            P.dma("pool", w.ap, _r3(T.w_q[:, cg * 512:(cg + 1) * 512]), writes=[w])
            for f4 in range(4):
                fc = cg * 4 + f4
                st = stg[si % 2]; si += 1
                for tb in range(8):
                    ps = bank[bi % 4]; bi += 1
                    MM(P, ps.ap, [(w.ap[:, dc, f4 * 128:(f4 + 1) * 128], hTb[dc].ap[:, tb * 512:(tb + 1) * 512]) for dc in range(8)],
                       [w] + hTb, [ps])
                    if tb % 2 == 0:
                        CP(P, st[tb].ap, ps.ap, [ps], [st[tb]], eng="act")
                    else:
                        CP(P, st[tb].ap, ps.ap, [ps], [st[tb]], eng="dve")
                for tb in range(8):
                    P.dma("sync", T.qpT[fc * 128:(fc + 1) * 128, tb * 512:(tb + 1) * 512], st[tb].ap, reads=[st[tb]])

    def phase_E2():
        P.barrier(); A.reset()
        I32 = mybir.dt.int32
        U32 = mybir.dt.uint32
        skb = A.alloc("skb", [128, 2, 128], BF16)
        P.dma("pool", skb.ap, T.skT, writes=[skb])
        identb = A.alloc("identb", [128, 128], BF16)
        P.dma("pool", identb.ap, T.ident, writes=[identb])
        ident32 = A.alloc("ident32", [128, 128])
        P.dma("sync", ident32.ap, T.ident, writes=[ident32])
        iot_i = A.alloc("iot_i", [128, 128], I32)
        iot_f = A.alloc("iot_f", [128, 128])
        P.op("pool", lambda e: e.iota(iot_i.ap, pattern=[[1, 128]], base=0, channel_multiplier=0), [], [iot_i])
        P.op("pool", lambda e: e.tensor_copy(out=iot_f.ap, in_=iot_i.ap), [iot_i], [iot_f])
        qpb = A.alloc("qpb", [128, 16, 128], BF16)
        Ssb = A.alloc("Ssb", [128, 16, 128])
        S2 = [A.alloc("S2_%d" % i, [128, 128]) for i in range(4)]
        top = A.alloc("top", [128, 16, 16])
        idx_u = A.alloc("idx_u", [128, 8, 16], U32)
        topc = [Buf("topc%d" % c, top.ap[:, c, :]) for c in range(16)]
        Ssbc = [Buf("Ssbc%d" % c, Ssb.ap[:, c, :]) for c in range(16)]
        idxc = [Buf("idxc%d" % h, idx_u.ap[:, h, :]) for h in range(8)]
        idx_f = A.alloc("idx_f", [128, 128])
        idxT = A.alloc("idxT", [128, 128])
        cand2 = [A.alloc("cand2_%d" % i, [128, 256]) for i in range(2)]
        best = A.alloc("best", [128, 8, 16])
        eb = A.alloc("eb", [128, 8, 16])
        Z = A.alloc("Z", [128, 8]); lnZ = A.alloc("lnZ", [128, 8])
        wp = A.alloc("wp", [128, 8, 16]); taup = A.alloc("taup", [128, 8])
        X = [A.alloc("X%d" % i, [128, 16, 128]) for i in range(2)]
        cand = X[1]
        cand_ap = X[1].ap.rearrange("p a b -> p (a b)").rearrange("p (h c) -> p h c", h=8)
        Rall = A.alloc("Rall", [128, 128, 128], BF16)
        Rtm = [Buf("Rtm%d" % h, Rall.ap[:, h * 16:(h + 1) * 16, :]) for h in range(8)]
        Lsm = [A.alloc("Lsm%d" % i, [128, 128, 128], BF16) for i in range(2)]
        Rsm = A.alloc("Rsm", [128, 128, 128], BF16)
        Cst = [A.alloc("Cst%d" % i, [128, 16, 128], BF16) for i in range(2)]
        topv = top.ap.rearrange("p (h two) a -> p h two a", two=2)
        Sv = Ssb.ap.rearrange("p (h two) n -> p h two n", two=2)
        b0t = Buf("b0t", bank[0].ap)
        cnt = {"sc": 0, "tr": 0, "cp": 0, "cs": 0}

        def stage_scores(tt):
            tsl = slice(tt * 128, (tt + 1) * 128)
            P.dma("sync", qpb.ap, T.qpT[:, tsl].rearrange("(c k) t -> k c t", k=128), writes=[qpb])
            for cg in range(4):
                bk = bank[cnt["sc"] % 2]; cnt["sc"] += 1

                def emit(e, bk=bk, cg=cg):
                    ins = None
                    for cl in range(4):
                        c = cg * 4 + cl
                        ins = e.matmul(bk.ap[:, cl * 128:(cl + 1) * 128], qpb.ap[:, c, :], skb.ap[:, c % 2, :], start=True, stop=True)
                    return ins
                P.op("pe", emit, [qpb, skb], [bk])
                CP(P, Ssb.ap[:, cg * 4:(cg + 1) * 4, :], bk.ap.rearrange("p (c n) -> p c n", c=4), [bk], Ssbc[cg * 4:(cg + 1) * 4], eng="act")

        def stage_route1(tt):
            for c0 in range(0, 16, 4):
                cs_ = list(range(c0, c0 + 4))
                for c in cs_:
                    P.op("dve", lambda e, c=c: e.max(out=topc[c].ap[:, 0:8], in_=Ssbc[c].ap), [Ssbc[c]], [topc[c]])
                for c in cs_:
                    s2 = S2[c % 4]
                    P.op("dve", lambda e, c=c, s2=s2: e.match_replace(out=s2.ap, in_to_replace=topc[c].ap[:, 0:8], in_values=Ssbc[c].ap, imm_value=NEG),
                         [Ssbc[c], topc[c]], [s2])
                for c in cs_:
                    s2 = S2[c % 4]
                    P.op("dve", lambda e, c=c, s2=s2: e.max(out=topc[c].ap[:, 8:16], in_=s2.ap), [s2, topc[c]], [topc[c]])
                for c in cs_:
                    if c % 2 == 0:
                        h = c // 2
                        P.op("dve", lambda e, c=c, h=h: e.max_index(out=idxc[h].ap[:, 0:8], in_max=topc[c].ap[:, 0:8], in_values=Ssbc[c].ap),
                             [Ssbc[c], topc[c]], [idxc[h]])
                        P.op("dve", lambda e, c=c, h=h: e.max_index(out=idxc[h].ap[:, 8:16], in_max=topc[c].ap[:, 8:16], in_values=Ssbc[c].ap),
                             [Ssbc[c], topc[c], idxc[h]], [idxc[h]])
            TT(P, cand_ap.rearrange("p h (a b) -> p h a b", a=16),
               topv[:, :, 0, :].unsqueeze(3).broadcast_to([128, 8, 16, 16]),
               topv[:, :, 1, :].unsqueeze(2).broadcast_to([128, 8, 16, 16]), ALU.add, topc, [cand])
            for h in range(8):
                c2 = cand2[h % 2]
                P.op("dve", lambda e, h=h: e.max(out=best.ap[:, h, 0:8], in_=cand_ap[:, h, :]), [cand], [best])
                P.op("dve", lambda e, h=h, c2=c2: e.match_replace(out=c2.ap, in_to_replace=best.ap[:, h, 0:8], in_values=cand_ap[:, h, :], imm_value=NEG),
                     [cand, best], [c2])
                P.op("dve", lambda e, h=h, c2=c2: e.max(out=best.ap[:, h, 8:16], in_=c2.ap), [c2], [best])
            CP(P, idx_f.ap, idx_u.ap.rearrange("p h a -> p (h a)"), idxc, [idx_f])

        def stage_route2(tt):
            ACT(P, eb.ap, best.ap, AF.Exp, [best], [eb])
            P.op("dve", lambda e: e.tensor_reduce(out=Z.ap, in_=eb.ap, axis=AX.X, op=ALU.add), [eb], [Z])
            ACT(P, lnZ.ap, Z.ap, AF.Ln, [Z], [lnZ])
            TT(P, wp.ap, topv[:, :, 0, :], lnZ.ap.unsqueeze(2).broadcast_to([128, 8, 16]), ALU.subtract, topc + [lnZ], [wp])
            STT(P, taup.ap, best.ap[:, :, 15], -1e-5, lnZ.ap, ALU.add, ALU.subtract, [best, lnZ], [taup])

        def stage_idxT(tt):
            P.op("pe", lambda e: e.transpose(out=b0t.ap[:, 0:128], in_=idx_f.ap, identity=ident32.ap), [idx_f, ident32], [b0t, bank[0]])
            CP(P, idxT.ap, b0t.ap[:, 0:128], [b0t, bank[0]], [idxT], eng="act")

        def stage_buildL(tt):
            L = Lsm[tt % 2]
            TT(P, L.ap, iot_f.ap.unsqueeze(1).broadcast_to([128, 128, 128]),
               idxT.ap.unsqueeze(2).broadcast_to([128, 128, 128]), ALU.is_equal, [iot_f, idxT], [L])

        def stage_buildR(tt):
            def fin(h):
                x = X[h % 2]
                STT(P, Rtm[h].ap, x.ap, taup.ap[:, h:h + 1], Rtm[h].ap, ALU.is_ge, ALU.mult, [x, taup, Rtm[h]], [Rtm[h]])
            for h in range(8):
                x = X[h % 2]
                TT(P, x.ap, Sv[:, h, 1, :].unsqueeze(1).broadcast_to([128, 16, 128]),
                   wp.ap[:, h, :].unsqueeze(2).broadcast_to([128, 16, 128]), ALU.add, [Ssbc[2 * h + 1], wp], [x])
                ACT(P, Rtm[h].ap, x.ap, AF.Exp, [x], [Rtm[h]])
                if h >= 1:
                    fin(h - 1)
            fin(7)

        def stage_trR(tt):
            for ig in range(16):
                bk = bank[2 + cnt["tr"] % 2]; cnt["tr"] += 1
                bkv = bk.ap.bitcast(BF16)[:, 0:1024]

                def emit(e, bkv=bkv, ig=ig):
                    ins = None
                    for q in range(8):
                        i_ = ig * 8 + q
                        ins = e.transpose(out=bkv[:, q * 128:(q + 1) * 128], in_=Rall.ap[:, :, i_], identity=identb.ap)
                    return ins
                P.op("pe", emit, Rtm + [identb], [bk])
                CP(P, Rsm.ap[:, ig * 8:(ig + 1) * 8, :], bkv.rearrange("p (i t) -> p i t", i=8), [bk], [Rsm], eng="act")

        def stage_cmm(tt):
            L = Lsm[tt % 2]
            for t16 in range(8):
                cs = Cst[cnt["cs"] % 2]; cnt["cs"] += 1
                for t4 in range(4):
                    bk = bank[4 + cnt["cp"] % 4]; cnt["cp"] += 1
                    tb_ = t16 * 16 + t4 * 4

                    def emit(e, bk=bk, tb_=tb_):
                        ins = None
                        for q in range(4):
                            t = tb_ + q
                            ins = e.matmul(bk.ap[:, q * 128:(q + 1) * 128], Rsm.ap[:, :, t], L.ap[:, t, :], start=True, stop=True)
                        return ins
                    P.op("pe", emit, [Rsm, L], [bk])
                    CP(P, cs.ap[:, t4 * 4:(t4 + 1) * 4, :], bk.ap.rearrange("p (t i) -> p t i", t=4), [bk], [cs], eng="act")
                P.dma("sync", T.Cs[tt][:, t16 * 16:(t16 + 1) * 16, :], cs.ap, reads=[cs])

        stage_scores(0)
        stage_route1(0)
        stage_route2(0)
        stage_idxT(0)
        stage_buildR(0)
        stage_buildL(0)
        for tt in range(NT):
            if tt + 1 < NT:
                stage_scores(tt + 1)
                stage_route1(tt + 1)
            stage_trR(tt)
            if tt + 1 < NT:
                stage_route2(tt + 1)
                stage_idxT(tt + 1)
            stage_cmm(tt)
            if tt + 1 < NT:
                stage_buildR(tt + 1)
                stage_buildL(tt + 1)

    def phase_E3():
        P.barrier(); A.reset()
        TB = 256
        hTb = [A.alloc("hTb%d" % i, [128, 8, TB], BF16) for i in range(2)]
        ysb = [A.alloc("ysb%d" % i, [128, 1024]) for i in range(2)]
        NG = 4
        gl = [A.alloc("glx%d" % i, [128, TB]) for i in range(NG)]
        G = [A.alloc("Gx%d" % i, [128, TB], BF16) for i in range(NG)]
        NB = 3
        Ug = [A.alloc("Ugx%d" % i, [128, 8, 256], BF16) for i in range(NB)]
        Vg = [A.alloc("Vgx%d" % i, [128, 2, 1024], BF16) for i in range(NB)]
        Cp = [A.alloc("Cp%d" % i, [128, 2, 128, 128], BF16) for i in range(2)]
        NBLK = S // TB
        tasks = [(blk, c) for blk in range(NBLK) for c in range(128)]
        state = {}

        def load_group(gi_):
            blk, cg = gi_ // 64, gi_ % 64
            if blk >= NBLK:
                return
            ug = Ug[gi_ % NB]; vg = Vg[gi_ % NB]
            P.dma("sync", ug.ap, _r3(T.uTb[:, cg * 256:(cg + 1) * 256]), writes=[ug])
            P.dma("sync", vg.ap, T.vtb[cg * 256:(cg + 1) * 256, :].rearrange("(c p) d -> p c d", p=128), writes=[vg])

        def load_block(blk):
            if blk >= NBLK:
                return
            hb = hTb[blk % 2]
            P.dma("sync", hb.ap, _r3(T.hT[:, blk * TB:(blk + 1) * TB]), writes=[hb])
            cp = Cp[blk % 2]
            for tl in range(2):
                P.dma("pool", cp.ap[:, tl], T.Cs[blk * 2 + tl], writes=[cp])

        load_block(0)
        load_group(0)
        load_group(1)

        def stage1(k):
            blk, c = tasks[k]
            hb = hTb[blk % 2]
            cp = Cp[blk % 2]
            cg, cl = c // 2, c % 2
            gi_ = blk * 64 + cg
            ug = Ug[gi_ % NB]; vg = Vg[gi_ % NB]
            hp = bank[k % 4]
            MM(P, hp.ap[:, 0:TB], [(ug.ap[:, dc, cl * 128:(cl + 1) * 128], hb.ap[:, dc, :]) for dc in range(8)], [ug, hb], [hp])
            g1 = gl[k % NG]; g2 = G[k % NG]
            ACT(P, g1.ap, hp.ap[:, 0:TB], AF.Gelu, [hp], [g1])
            TT(P, g2.ap.rearrange("p (a t) -> p a t", a=2), g1.ap.rearrange("p (a t) -> p a t", a=2), cp.ap[:, :, :, c],
               ALU.mult, [g1, cp], [g2])
            state[k] = (g2, vg, cl)

        def stage2(k):
            blk, c = tasks[k]
            g2, vg, cl = state.pop(k)
            for tl in range(2):
                for half in range(2):
                    yb = bank[4 + tl * 2 + half]
                    P.op("pe", lambda e, yb=yb, tl=tl, half=half:
                         e.matmul(yb.ap, g2.ap[:, tl * 128:(tl + 1) * 128], vg.ap[:, cl, half * 512:(half + 1) * 512],
                                  start=(c == 0), stop=(c == 127)), [g2, vg], [yb])
            if c == 127:
                for tl in range(2):
                    tt = blk * 2 + tl
                    y = ysb[tl]
                    CP(P, y.ap, pst[:, (4 + tl * 2) * 512:(6 + tl * 2) * 512], [bank[4 + tl * 2], bank[5 + tl * 2]], [y], eng=("act" if tl else "dve"))
                    P.dma("sync", T.f32[tt * 128:(tt + 1) * 128, :], y.ap, reads=[y])

        SK = 2
        for k in range(len(tasks) + SK):
            if k < len(tasks):
                stage1(k)
            if k >= SK:
                stage2(k - SK)
            if k < len(tasks):
                blk, c = tasks[k]
                if c % 2 == 1:
                    load_group(blk * 64 + c // 2 + 2)
                if c == 127:
                    load_block(blk + 1)

    def phase_E():
        if "A" not in phases:
            phase_E0()
        phase_E1()
        phase_E2()
        phase_E3()

    if "A" in phases:
        phase_A1()
        phase_A2()
    if "C" in phases:
        phase_C()
    if "D" in phases:
        phase_D()
    if "E" in phases:
        phase_E()
    for ch, fn in (("0", phase_E0), ("1", phase_E1), ("2", phase_E2), ("3", phase_E3)):
        if ch in phases:
            fn()
    if "e" in phases:
        phase_E_stub()
    if "F" in phases:
        phase_F()
    P.barrier()
    P.finish([])
    return nc, P


def _t5_bucket_np(d):
    import math
    n = np.maximum(d, 0)
    nf = np.maximum(n, 16).astype(np.float32)
    large = 16 + (np.log(nf / np.float32(16)) / np.float32(math.log(1024 / 16)) * np.float32(16)).astype(np.int32)
    large = np.minimum(large, 31)
    return np.where(n < 16, n, large)


def _constants():
    c = {}
    c["tri"] = (np.arange(128)[:, None] <= np.arange(128)[None, :]).astype(np.float32)
    E = np.zeros((33, R_LEN), np.float32)
    y = np.arange(R_LEN)
    d = y - 511
    bk = _t5_bucket_np(d)
    for i in range(R_LEN):
        if d[i] < 0:
            E[32, i] = 1.0
        else:
            E[bk[i], i] = 1.0
    c["Emat"] = E
    sn = np.zeros((33, 16, 128), np.float32)
    sf = np.zeros((33, 16, 128), np.float32)
    for n in range(16):
        sn[n, n, :] = 1.0
        sf[n, n, :] = 1.0
        sf[32, n, :] = 1.0
    c["sel_near"] = sn
    c["sel_far"] = sf
    c["ident"] = np.eye(128, dtype=np.float32)
    c["antiid"] = np.ascontiguousarray(np.eye(128, dtype=np.float32)[::-1])
    return c


def prep_shared(inp):
    f = lambda a: np.ascontiguousarray(a, dtype=np.float32)
    sh = {}
    sh["w_in"] = f(inp["w_in"][0])
    sh["b_in_c"] = f(inp["b_in"][0].reshape(56, 128).T)
    sh["b_in"] = f(inp["b_in"][0])
    sh["gmlp_ln_g"] = f(inp["gmlp_ln_g"][0]); sh["gmlp_ln_b"] = f(inp["gmlp_ln_b"][0])
    sh["wsT"] = f(inp["gmlp_w_s"][0].transpose(2, 0, 1))
    sh["bs"] = f(inp["gmlp_b_s"][0][None])
    for k in ("w_proj_a", "w_proj_b", "w_out", "ln1_g", "ln1_b", "peer_w_q", "ple_w_proj", "ple_w_gate",
              "ple_b_gate", "ln2_g", "ln2_b"):
        sh[k] = f(inp[k][0])
    sh["rb_ext"] = f(np.concatenate([inp["rel_bias"], np.full((1, 8), NEG, np.float32)], axis=0))
    sh["skT"] = f(inp["peer_sub_keys"][0].transpose(2, 0, 1))
    sh["uT"] = f(inp["peer_u"][0].T)
    sh["vtab"] = f(inp["peer_v"][0])
    sh.update(_constants())
    return sh


def prep_core(inp, b):
    f = lambda a: np.ascontiguousarray(a, dtype=np.float32)
    return {"xT": f(inp["x"][b].T), "x": f(inp["x"][b]), "pT": f(inp["p"][0, b].T)}


PHASES = "ACDEF"


def kernel(**inputs):
    inp = {k: np.asarray(v) for k, v in inputs.items()}
    nc, P = build_program(phases=PHASES, debug=False)
    sh = prep_shared(inp)
    in_maps = [dict(sh, **prep_core(inp, b)) for b in range(8)]
    res = run_bass_kernel_spmd(nc, in_maps, core_ids=list(range(8)))
    out = np.stack([np.asarray(r["out"], dtype=np.float32) for r in res.results], axis=0)
    return out
```

```python
import contextlib
import numpy as np
import concourse.bass as bass
import concourse.mybir as mybir
from concourse.bass_utils import run_bass_kernel_spmd

F32 = mybir.dt.float32
BF16 = mybir.dt.bfloat16
AF = mybir.ActivationFunctionType
ALU = mybir.AluOpType
AX = mybir.AxisListType


class Buf:
    __slots__ = ("name", "ap", "lw", "rd", "dsem", "dcount")

    def __init__(self, name, ap=None):
        self.name = name
        self.ap = ap
        self.lw = None
        self.rd = []
        self.dsem = None
        self.dcount = 0


class Slot:
    __slots__ = ("dsem", "dcount")

    def __init__(self):
        self.dsem = None
        self.dcount = 0


class _Op:
    __slots__ = ("emit", "waits", "signal", "known", "dma_buf", "dma_val")

    def __init__(self, emit):
        self.emit = emit
        self.waits = []
        self.signal = False
        self.known = None
        self.dma_buf = None
        self.dma_val = 0


class Prog:
    ENG = {"pe": "tensor", "act": "scalar", "dve": "vector", "pool": "gpsimd", "sync": "sync"}
    ALIAS = {"gpsimd": "pool", "scalar": "act", "vector": "dve", "tensor": "pe"}

    def __init__(self, nc):
        self.nc = nc
        self.ops = {k: [] for k in self.ENG}
        self.known = {k: {} for k in self.ENG}
        self.stack = contextlib.ExitStack()
        self.dbufs = []
        self.dummy = Buf("dummy")
        self.n_ops = 0
        self.free_slots = []
        self.free_sw = []
        self.hw_slot = {}
        self.sw_slot = {}
        self.live = []

    def sbuf(self, name, shape, dtype):
        t = self.stack.enter_context(self.nc.sbuf_tensor(name, shape, dtype))
        return Buf(name, t)

    def psum(self, name, shape, dtype):
        t = self.stack.enter_context(self.nc.psum_tensor(name, shape, dtype))
        return Buf(name, t)

    def dram_buf(self, name):
        return Buf(name)

    def view(self, name, ap):
        return Buf(name, ap)

    def _deps(self, reads, writes):
        deps = []
        for b in reads:
            if b.lw is not None:
                deps.append(b.lw)
        for b in writes:
            if b.lw is not None:
                deps.append(b.lw)
            deps.extend(b.rd)
        return deps

    def _apply_waits(self, eng, op, deps):
        known = self.known[eng]
        changed = False
        for ev in deps:
            if ev[0] == "e":
                _, e2, idx = ev
                if e2 == eng and eng == "pe":
                    continue
                key = e2
                if known.get(key, -1) >= idx:
                    continue
                if not changed:
                    known = dict(known)
                    changed = True
                if e2 != eng or True:
                    op.waits.append(ev)
                    self.ops[e2][idx].signal = True
                known[key] = idx
                k2 = self.ops[e2][idx].known
                if k2:
                    for kk, vv in k2.items():
                        if known.get(kk, -1) < vv:
                            known[kk] = vv
            else:
                _, b, val = ev
                key = ("d", id(b))
                if known.get(key, -1) >= val:
                    continue
                if not changed:
                    known = dict(known)
                    changed = True
                op.waits.append(ev)
                known[key] = val
        self.known[eng] = known
        op.known = known

    def _commit(self, ev, reads, writes):
        for b in reads:
            b.rd.append(ev)
        for b in writes:
            b.lw = ev
            b.rd = []

    def op(self, eng, emit, reads=(), writes=()):
        eng = self.ALIAS.get(eng, eng)
        o = _Op(emit)
        self._apply_waits(eng, o, self._deps(reads, writes))
        idx = len(self.ops[eng])
        self.ops[eng].append(o)
        self._commit(("e", eng, idx), reads, writes)
        self.n_ops += 1
        return o

    def dma(self, q, out_ap, in_ap, reads=(), writes=(), sem_buf=None, **kw):
        q = self.ALIAS.get(q, q)
        if sem_buf is None:
            for b in list(writes) + list(reads):
                if b.ap is not None:
                    sem_buf = b
                    break
            else:
                sem_buf = self.dummy
        o = _Op(lambda e: e.dma_start(out=out_ap, in_=in_ap, **kw))
        self._apply_waits(q, o, self._deps(reads, writes))
        sw = (q == "pool")
        key = id(sem_buf)
        table = self.sw_slot if sw else self.hw_slot
        if key not in table:
            free = self.free_sw if sw else self.free_slots
            if free:
                table[key] = free.pop()
            else:
                table[key] = Slot()
                self.dbufs.append(table[key])
        sl = table[key]
        sl.dcount += 16
        o.dma_buf = sl
        o.dma_val = sl.dcount
        self.ops[q].append(o)
        self._commit(("d", sl, sl.dcount), reads, writes)
        self.n_ops += 1
        return o

    def finish(self, out_bufs):
        nc = self.nc
        fin = _Op(None)
        self._apply_waits("sync", fin, self._deps(out_bufs, ()))
        self.ops["sync"].append(fin)
        st = self.stack
        esem = {}
        for k in self.ENG:
            if any(o.signal for o in self.ops[k]):
                esem[k] = st.enter_context(nc.semaphore("e_" + k))
        for i, b in enumerate(self.dbufs):
            b.dsem = st.enter_context(nc.semaphore("d_%d" % i))
        sval = {}
        for k, ops in self.ops.items():
            c = 0
            for i, o in enumerate(ops):
                if o.signal:
                    c += 1
                    sval[(k, i)] = c
        self.max_sval = max(sval.values()) if sval else 0

        def run(k, eng):
            for i, o in enumerate(self.ops[k]):
                for ev in o.waits:
                    if ev[0] == "e":
                        eng.wait_ge(esem[ev[1]], sval[(ev[1], ev[2])])
                    else:
                        eng.wait_ge(ev[1].dsem, ev[2])
                if o.emit is None:
                    continue
                ins = o.emit(eng)
                if o.dma_buf is not None:
                    ins.then_inc(o.dma_buf.dsem, 16)
                elif o.signal:
                    ins.then_inc(esem[k], 1)

        block = st.enter_context(nc.Block())

        @block.sync
        def _(e):
            run("sync", e)

        @block.tensor
        def _(e):
            run("pe", e)

        @block.scalar
        def _(e):
            run("act", e)

        @block.vector
        def _(e):
            run("dve", e)

        @block.gpsimd
        def _(e):
            run("pool", e)

        st.close()


def _esize(dt):
    return 2 if dt == BF16 else 4


class Arena:
    def __init__(self, P, words):
        self.P = P
        self.words = words
        self.t = P.stack.enter_context(P.nc.sbuf_tensor("arena", [128, words], F32))
        self.off = 0

    def reset(self):
        self.off = 0

    def alloc(self, name, shape, dtype=F32):
        n = 1
        for s in shape[1:]:
            n *= s
        words = (n * _esize(dtype) + 3) // 4
        assert self.off + words <= self.words, (name, self.off, words, self.words)
        ap = self.t[0:shape[0], self.off:self.off + words]
        self.off += words
        if dtype == BF16:
            ap = ap.bitcast(dtype)[:, 0:n]
        elif dtype != F32:
            ap = ap.bitcast(dtype)
        if len(shape) > 2:
            names = ["a%d" % i for i in range(len(shape) - 1)]
            pat = "p (" + " ".join(names) + ") -> p " + " ".join(names)
            ap = ap.rearrange(pat, **{nm: s for nm, s in zip(names, shape[1:])})
        return Buf(name, ap)


def _barrier(self):
    evs = []
    for k in ("pe", "act", "dve", "pool"):
        ops = self.ops[k]
        for i in range(len(ops) - 1, -1, -1):
            if ops[i].dma_buf is None and ops[i].emit is not None:
                evs.append(("e", k, i))
                break
    for b in self.dbufs:
        evs.append(("d", b, b.dcount))
    self.pending = {k: list(evs) for k in self.ENG}
    self.free_slots.extend(self.hw_slot.values())
    self.free_sw.extend(self.sw_slot.values())
    self.hw_slot = {}
    self.sw_slot = {}


Prog.barrier = _barrier
_orig_apply = Prog._apply_waits


def _apply2(self, eng, op, deps):
    pend = getattr(self, "pending", None)
    if pend and pend.get(eng):
        deps = list(deps) + pend[eng]
        pend[eng] = []
    _orig_apply(self, eng, op, deps)


Prog._apply_waits = _apply2


def MM(P, out_ap, pairs, reads, writes):
    pairs = list(pairs)

    def emit(e):
        n = len(pairs)
        ins = None
        for i, (l, r) in enumerate(pairs):
            ins = e.matmul(out_ap, l, r, start=(i == 0), stop=(i == n - 1))
        return ins
    return P.op("pe", emit, reads, writes)


def ACT(P, out, in_, func, reads, writes, bias=None, scale=None):
    kw = {}
    if bias is not None:
        kw["bias"] = bias
    if scale is not None:
        kw["scale"] = scale
    return P.op("act", lambda e: e.activation(out=out, in_=in_, func=func, **kw), reads, writes)


def TT(P, out, in0, in1, op, reads, writes, eng="dve"):
    return P.op(eng, lambda e: e.tensor_tensor(out=out, in0=in0, in1=in1, op=op), reads, writes)


def TS(P, out, in0, s1, s2, op0, op1, reads, writes, eng="dve"):
    if op1 is None:
        return P.op(eng, lambda e: e.tensor_scalar(out=out, in0=in0, scalar1=s1, scalar2=None, op0=op0), reads, writes)
    return P.op(eng, lambda e: e.tensor_scalar(out=out, in0=in0, scalar1=s1, scalar2=s2, op0=op0, op1=op1), reads, writes)


def STT(P, out, in0, scalar, in1, op0, op1, reads, writes):
    return P.op("dve", lambda e: e.scalar_tensor_tensor(out=out, in0=in0, scalar=scalar, in1=in1, op0=op0, op1=op1), reads, writes)


def CP(P, out, in_, reads, writes, eng="dve"):
    if eng == "act":
        return P.op("act", lambda e: e.copy(out=out, in_=in_), reads, writes)
    return P.op(eng, lambda e: e.tensor_copy(out=out, in_=in_), reads, writes)


def layer_norm_rows(P, t1, out_ap, out_buf, g_bc, b_bc, sm, eps_ap):
    st6 = sm["st6"]
    mv = sm["mv"]
    sd = sm["sd"]
    P.op("dve", lambda e: e.bn_stats(out=st6.ap[:, 0:6], in_=t1.ap[:, 0:512]), [t1], [st6])
    P.op("dve", lambda e: e.bn_stats(out=st6.ap[:, 6:12], in_=t1.ap[:, 512:1024]), [t1, st6], [st6])
    P.op("dve", lambda e: e.bn_aggr(out=mv.ap[:, 0:2], in_=st6.ap[:, 0:12]), [st6], [mv])
    ACT(P, sd.ap[:, 0:1], mv.ap[:, 1:2], AF.Sqrt, [mv, eps_ap], [sd], bias=eps_ap.ap[:, 0:1], scale=1.0)
    P.op("dve", lambda e: e.reciprocal(out=sd.ap[:, 1:2], in_=sd.ap[:, 0:1]), [sd], [sd])
    TS(P, t1.ap, t1.ap, mv.ap[:, 0:1], sd.ap[:, 1:2], ALU.subtract, ALU.mult, [t1, mv, sd], [t1])
    TT(P, t1.ap, t1.ap, g_bc.ap, ALU.mult, [t1, g_bc], [t1])
    TT(P, out_ap, t1.ap, b_bc.ap, ALU.add, [t1, b_bc], [out_buf] if out_buf is not t1 else [t1])


S = 4096
D = 1024
NT = S // 128
ALPHA = 2.0 ** 0.25
LN_EPS = 1e-5
QSCALE = 128.0 ** -0.5
NEG = -1e30
ARENA_WORDS = 45 * 1024
STRIP_LEN = 1792
R_LEN = 1920


def _r3(ap, p=128):
    return ap.rearrange("(c p) t -> p c t", p=p)


def build_program(phases="ACDEF", debug=False):
    nc = bass.Bass("TRN2", target_bir_lowering=False)

    def din(name, shape):
        return nc.dram_tensor(name, list(shape), F32, kind="ExternalInput").ap()

    def dscr(name, shape, dt):
        kind = "ExternalOutput" if debug else "Internal"
        return nc.dram_tensor(name, list(shape), dt, kind=kind).ap()

    T = type("T", (), {})()
    T.xT = din("xT", [D, S]); T.x = din("x", [S, D]); T.pT = din("pT", [256, S])
    T.w_in = din("w_in", [D, 7168]); T.b_in_c = din("b_in_c", [128, 56]); T.b_in = din("b_in", [7168])
    T.gln_g = din("gmlp_ln_g", [D]); T.gln_b = din("gmlp_ln_b", [D])
    T.wsT = din("wsT", [128, 8, 128]); T.bs = din("bs", [1, 8, 128]); T.tri = din("tri", [128, 128])
    T.w_pa = din("w_proj_a", [D, D]); T.w_pb = din("w_proj_b", [D, D]); T.w_out = din("w_out", [D, D])
    T.ln1_g = din("ln1_g", [D]); T.ln1_b = din("ln1_b", [D])
    T.rb_ext = din("rb_ext", [33, 8]); T.Emat = din("Emat", [33, R_LEN])
    T.sel_near = din("sel_near", [33, 16, 128]); T.sel_far = din("sel_far", [33, 16, 128])
    T.ident = din("ident", [128, 128]); T.antiid = din("antiid", [128, 128])
    T.w_q = din("peer_w_q", [D, 2048]); T.skT = din("skT", [128, 2, 128])
    T.uT = din("uT", [D, 16384]); T.vtab = din("vtab", [16384, D])
    T.w_ple = din("ple_w_proj", [256, D]); T.w_pg = din("ple_w_gate", [D, D]); T.b_pg = din("ple_b_gate", [D])
    T.ln2_g = din("ln2_g", [D]); T.ln2_b = din("ln2_b", [D])
    T.out = nc.dram_tensor("out", [S, D], F32, kind="ExternalOutput").ap()
    T.guT = dscr("s_guT", [D, S], BF16); T.qTs = dscr("s_qT", [D, S], BF16); T.kTs = dscr("s_kT", [D, S], BF16)
    T.sgaT = dscr("s_sgaT", [D, S], BF16); T.sgbT = dscr("s_sgbT", [D, S], BF16)
    T.vtok = dscr("s_vtok", [S, D], BF16); T.aT = dscr("s_aT", [D, S], BF16); T.oT = dscr("s_oT", [D, S], BF16)
    T.rrow = dscr("s_rrow", [8, R_LEN], BF16)
    T.uTb = nc.dram_tensor("s_uTb", [D, 16384], BF16, kind="Internal").ap()
    T.vtb = nc.dram_tensor("s_vtb", [16384, D], BF16, kind="Internal").ap()
    T.qpT = dscr("s_qpT", [2048, S], BF16)
    T.Cs = nc.dram_tensor("s_Cs", [NT, 128, 128, 128], BF16, kind="Internal").ap()
    T.h32 = dscr("s_h32", [S, D], F32); T.hT = dscr("s_hT", [D, S], BF16); T.f32 = dscr("s_f32", [S, D], F32)

    P = Prog(nc)
    A = Arena(P, ARENA_WORDS)
    pst = P.stack.enter_context(nc.psum_tensor("psum_all", [128, 4096], F32))
    bank = [Buf("bank%d" % i, pst[:, i * 512:(i + 1) * 512]) for i in range(8)]
    OUT = Buf("OUT")

    def bc(ap1d):
        return ap1d.partition_broadcast(128)

    def load_xT():
        xTb = [A.alloc("xTb%d" % c, [128, S], BF16) for c in range(8)]
        return xTb

    def phase_A1():
        P.barrier(); A.reset()
        xTb = load_xT()
        for c in range(8):
            P.dma("pool", xTb[c].ap, T.xT[c * 128:(c + 1) * 128, :], writes=[xTb[c]])
        wb = [A.alloc("wb%d" % i, [128, 8, 512], BF16) for i in range(2)]
        stg = [[A.alloc("stg%d_%d" % (i, j), [128, 512], BF16) for j in range(8)] for i in range(2)]
        bcol = A.alloc("bcol", [128, 56], F32)
        bqs = A.alloc("bqs", [128, 8], F32)
        P.dma("sync", bcol.ap, T.b_in_c, writes=[bcol])
        TS(P, bqs.ap, bcol.ap[:, 16:24], QSCALE, None, ALU.mult, None, [bcol], [bqs])
        e0_state = {"done": False}
        groups = [("gu", 0, AF.Gelu, T.guT, None), ("q", 2048, AF.Identity, T.qTs, QSCALE),
                  ("k", 3072, AF.Identity, T.kTs, None), ("ga", 5120, AF.Sigmoid, T.sgaT, None),
                  ("gb", 6144, AF.Sigmoid, T.sgbT, None)]
        wi = si = bi = 0
        for (nm, c0, fn, dst, sc) in groups:
            for half in range(2):
                w = wb[wi % 2]; wi += 1
                P.dma("pool", w.ap, _r3(T.w_in[:, c0 + half * 512:c0 + half * 512 + 512]), writes=[w])
                if wi == 2 and "E" in phases:
                    phase_E0()
                for f4 in range(4):
                    fl = half * 4 + f4
                    fc = c0 // 128 + fl
                    st = stg[si % 2]; si += 1
                    if nm == "q":
                        bias_ap, bias_buf = bqs.ap[:, fl:fl + 1], bqs
                    else:
                        bias_ap, bias_buf = bcol.ap[:, fc:fc + 1], bcol
                    for tb in range(8):
                        ps = bank[bi % 4]; bi += 1
                        MM(P, ps.ap, [(w.ap[:, dc, f4 * 128:(f4 + 1) * 128], xTb[dc].ap[:, tb * 512:(tb + 1) * 512])
                                      for dc in range(8)], [w] + xTb, [ps])
                        ACT(P, st[tb].ap, ps.ap, fn, [ps, bias_buf], [st[tb]], bias=bias_ap,
                            scale=(sc if sc is not None else 1.0))
                    for tb in range(8):
                        P.dma("sync", dst[fl * 128:(fl + 1) * 128, tb * 512:(tb + 1) * 512], st[tb].ap, reads=[st[tb]])

    def phase_A2():
        P.barrier(); A.reset()
        xTb = load_xT()
        Wv = A.alloc("Wv", [128, 8, 1024], BF16)
        Wg = A.alloc("Wg", [128, 8, 1024], BF16)
        P.dma("pool", Wv.ap, _r3(T.w_in[:, 4096:5120]), writes=[Wv])
        P.dma("pool", Wg.ap, _r3(T.w_in[:, 1024:2048]), writes=[Wg])
        bvb = A.alloc("bvb", [128, 1024]); bvg = A.alloc("bvg", [128, 1024])
        lg = A.alloc("lg", [128, 1024]); lb = A.alloc("lb", [128, 1024])
        P.dma("sync", bvb.ap, bc(T.b_in[4096:5120]), writes=[bvb])
        P.dma("sync", bvg.ap, bc(T.b_in[1024:2048]), writes=[bvg])
        P.dma("sync", lg.ap, bc(T.gln_g), writes=[lg])
        P.dma("sync", lb.ap, bc(T.gln_b), writes=[lb])
        ws32 = A.alloc("ws32", [128, 8, 128]); tri = A.alloc("tri", [128, 128])
        wsb = A.alloc("wsb", [128, 8, 128], BF16)
        bs32 = A.alloc("bs32", [1, 8, 128]); ones32 = A.alloc("ones32", [1, 128])
        epsb = A.alloc("eps", [128, 1])
        P.dma("sync", ws32.ap, T.wsT, writes=[ws32])
        P.dma("sync", tri.ap, T.tri, writes=[tri])
        P.dma("sync", bs32.ap, T.bs, writes=[bs32])
        P.op("dve", lambda e: e.memset(ones32.ap, 1.0), [], [ones32])
        P.op("dve", lambda e: e.memset(epsb.ap, LN_EPS), [], [epsb])
        TT(P, wsb.ap, ws32.ap, tri.ap.unsqueeze(1).broadcast_to([128, 8, 128]), ALU.mult, [ws32, tri], [wsb])
        t1 = [A.alloc("t1_%d" % i, [128, 1024]) for i in range(2)]
        vn = [A.alloc("vn%d" % i, [128, 1024], BF16) for i in range(2)]
        vst = [A.alloc("vst%d" % i, [128, 1024], BF16) for i in range(2)]
        gub = [A.alloc("gub%d" % i, [128, 8, 512], BF16) for i in range(2)]
        ast = [[A.alloc("ast%d_%d" % (i, j), [128, 8, 128], BF16) for j in range(4)] for i in range(2)]
        sm = [{"st6": A.alloc("st6_%d" % i, [128, 12]), "mv": A.alloc("mv%d" % i, [128, 2]),
               "sd": A.alloc("sd%d" % i, [128, 2])} for i in range(2)]
        pend_sp = []

        def spatial(tt, v):
            tsl = slice(tt * 128, (tt + 1) * 128)
            tb, t4 = tt // 4, tt % 4
            for g in range(8):
                bk = bank[6 + g // 4]
                oap = bk.ap[:, (g % 4) * 128:(g % 4 + 1) * 128]

                def emit(e, oap=oap, g=g, v=v):
                    e.matmul(oap, v.ap[:, g * 128:(g + 1) * 128], wsb.ap[:, g, :], start=True, stop=False)
                    return e.matmul(oap, ones32.ap[0:1, :], bs32.ap[0:1, g, :], start=False, stop=True)
                P.op("pe", emit, [v, wsb, ones32, bs32], [bk])
            if t4 == 0:
                gu = gub[tb % 2]
                P.dma("sync", gu.ap, _r3(T.guT[:, tb * 512:(tb + 1) * 512]), writes=[gu])
            gu = gub[tb % 2]
            a_s = ast[tb % 2][t4]
            TT(P, a_s.ap, pst[:, 6 * 512:8 * 512].rearrange("p (g t) -> p g t", g=8), gu.ap[:, :, t4 * 128:(t4 + 1) * 128],
               ALU.mult, [bank[6], bank[7], gu], [a_s])
            P.dma("sync", _r3(T.aT[:, tsl]), a_s.ap, reads=[a_s])
        for tt in range(NT):
            tsl = slice(tt * 128, (tt + 1) * 128)
            tb, t4 = tt // 4, tt % 4
            b0 = (tt % 2) * 2
            for half in range(2):
                MM(P, bank[b0 + half].ap, [(xTb[dc].ap[:, tsl], Wv.ap[:, dc, half * 512:(half + 1) * 512]) for dc in range(8)],
                   xTb + [Wv], [bank[b0 + half]])
            vs = vst[tt % 2]
            TT(P, vs.ap, pst[:, b0 * 512:(b0 + 2) * 512], bvb.ap, ALU.add, [bank[b0], bank[b0 + 1], bvb], [vs])
            P.dma("sync", T.vtok[tsl, :], vs.ap, reads=[vs])
            for half in range(2):
                MM(P, bank[4 + half].ap, [(xTb[dc].ap[:, tsl], Wg.ap[:, dc, half * 512:(half + 1) * 512]) for dc in range(8)],
                   xTb + [Wg], [bank[4 + half]])
            t = t1[tt % 2]; v = vn[tt % 2]
            TT(P, t.ap, pst[:, 4 * 512:6 * 512], bvg.ap, ALU.add, [bank[4], bank[5], bvg], [t])
            ACT(P, t.ap, t.ap, AF.Gelu, [t], [t])
            layer_norm_rows(P, t, v.ap, v, lg, lb, sm[tt % 2], epsb)
            pend_sp.append((tt, v))
            if len(pend_sp) > 1:
                spatial(*pend_sp.pop(0))
        spatial(*pend_sp.pop(0))


    def phase_C():
        P.barrier(); A.reset()
        RR = Buf("RR")
        rb = A.alloc("rb", [33, 8]); Em = A.alloc("Em", [33, R_LEN]); rsb = A.alloc("rsb", [8, R_LEN], BF16)
        P.dma("sync", rb.ap, T.rb_ext, writes=[rb]); P.dma("sync", Em.ap, T.Emat, writes=[Em])
        for j in range(4):
            MM(P, bank[0].ap[0:8, 0:480], [(rb.ap[0:33, :], Em.ap[0:33, j * 480:(j + 1) * 480])], [rb, Em], [bank[0]])
            CP(P, rsb.ap[:, j * 480:(j + 1) * 480], bank[0].ap[0:8, 0:480], [bank[0]], [rsb])
        P.dma("sync", T.rrow, rsb.ap, reads=[rsb], writes=[RR])
        strips = [A.alloc("strip%d" % h, [128, STRIP_LEN], BF16) for h in range(8)]
        for h in range(8):
            src = bass.AP(tensor=T.rrow.tensor, offset=h * R_LEN, ap=[[1, 128], [1, STRIP_LEN]])
            P.dma("sync", strips[h].ap, src, reads=[RR], writes=[strips[h]])
        identb = A.alloc("identb", [128, 128], BF16); ident32 = A.alloc("ident32", [128, 128])
        onesb = A.alloc("onesb", [128, 128], BF16)
        selN = A.alloc("selN", [33, 16, 128], BF16); selF = A.alloc("selF", [33, 16, 128], BF16)
        P.dma("pool", identb.ap, T.antiid, writes=[identb])
        P.dma("sync", ident32.ap, T.ident, writes=[ident32])
        P.dma("pool", selN.ap, T.sel_near, writes=[selN])
        P.dma("pool", selF.ap, T.sel_far, writes=[selF])
        P.op("dve", lambda e: e.memset(onesb.ap, 1.0), [], [onesb])
        kT = [A.alloc("kT%d" % i, [128, S], BF16) for i in range(2)]
        qT = [A.alloc("qT%d" % i, [128, S], BF16) for i in range(2)]
        vh = [A.alloc("vh%d" % i, [128, NT, 128], BF16) for i in range(2)]
        negT = [A.alloc("negT%d" % i, [33, S], BF16) for i in range(2)]
        ost = [[A.alloc("ost%d_%d" % (i, j), [128, 512], BF16) for j in range(8)] for i in range(2)]
        PT = [A.alloc("PT%d" % i, [128, 512], BF16) for i in range(3)]
        rden = [A.alloc("rden%d" % i, [128, 512]) for i in range(2)]
        kmT = [A.alloc("kmT%d" % i, [128, 16], BF16) for i in range(2)]
        km32 = [A.alloc("km32_%d" % i, [128, 16]) for i in range(2)]
        g16 = [A.alloc("g16_%d" % i, [128, 16]) for i in range(2)]
        top8 = [A.alloc("top8_%d" % i, [128, 8]) for i in range(2)]
        ngt = [A.alloc("ngt%d" % i, [128, 16]) for i in range(2)]
        for i in range(2):
            P.op("dve", lambda e, i=i: e.memset(negT[i].ap[0:33, :], 0.0), [], [negT[i]])

        def load_head(h):
            i = h % 2
            P.dma("sync", kT[i].ap, T.kTs[h * 128:(h + 1) * 128, :], writes=[kT[i]])
            P.dma("sync", qT[i].ap, T.qTs[h * 128:(h + 1) * 128, :], writes=[qT[i]])
            P.dma("sync", vh[i].ap, T.vtok[:, h * 128:(h + 1) * 128].rearrange("(n p) c -> p n c", p=128), writes=[vh[i]])

        b7g = Buf("b7g", bank[7].ap[:, 256:272])
        b7t = Buf("b7t", bank[7].ap[0:16, 0:128])

        def gate_first(h):
            i = h % 2
            P.op("dve", lambda e: e.tensor_reduce(out=km32[i].ap, in_=kT[i].ap.rearrange("p (n k) -> p n k", k=256),
                                                  axis=AX.X, op=ALU.add), [kT[i]], [km32[i]])
            CP(P, kmT[i].ap, km32[i].ap, [km32[i]], [kmT[i]])
            P.op("dve", lambda e: e.tensor_copy(out=negT[i].ap[32:33, :],
                                                in_=strips[h].ap[32:33, STRIP_LEN - 1:STRIP_LEN].broadcast_to([1, S])),
                 [strips[h]], [negT[i]])

        def gate_a(h, tt):
            i = h % 2
            qb = tt // 2
            ng = ngt[tt % 4]
            if qb <= 3:
                P.op("dve", lambda e: e.memset(ng.ap, 0.0), [], [ng])
            else:
                j = tt % 2
                MM(P, b7g.ap, [(qT[i].ap[:, tt * 128:(tt + 1) * 128], kmT[i].ap)], [qT[i], kmT[i]], [b7g])
                P.op("dve", lambda e: e.memset(g16[j].ap, NEG), [], [g16[j]])
                CP(P, g16[j].ap[:, 0:qb], b7g.ap[:, 0:qb], [b7g], [g16[j]])
                P.op("dve", lambda e: e.max(out=top8[j].ap, in_=g16[j].ap), [g16[j]], [top8[j]])
                TS(P, ng.ap, g16[j].ap, top8[j].ap[:, 2:3], None, ALU.is_ge, None, [g16[j], top8[j]], [ng])
                TS(P, ng.ap, ng.ap, 1e30, -1e30, ALU.mult, ALU.add, [ng], [ng])
                P.op("dve", lambda e: e.memset(ng.ap[:, qb:16], 0.0), [ng], [ng])

        def gate_b(h, tt):
            i = h % 2
            ng = ngt[tt % 4]
            P.op("pe", lambda e: e.transpose(out=b7t.ap, in_=ng.ap, identity=ident32.ap), [ng, ident32], [b7t])
            CP(P, negT[i].ap[0:16, tt * 128:(tt + 1) * 128], b7t.ap, [b7t], [negT[i]], eng="act")

        ngt = [A.alloc("ngtx%d" % i, [128, 16]) for i in range(4)]
        load_head(0)
        gate_first(0)
        for tt in range(NT):
            gate_a(0, tt)
            if tt >= 2:
                gate_b(0, tt - 2)
        gate_b(0, NT - 2); gate_b(0, NT - 1)

        tasks = []
        for h in range(8):
            for Q in range(8):
                for kt in range(4 * Q + 4):
                    tasks.append((h, Q, kt))
        st = {}

        def stageS(k):
            h, Q, kt = tasks[k]
            i = h % 2
            qs = slice(Q * 512, (Q + 1) * 512)
            n = kt // 2
            delta = Q * 512 - kt * 128
            far = delta >= 1024
            Sb = bank[k % 3]
            pairs = [(kT[i].ap[:, kt * 128:(kt + 1) * 128], qT[i].ap[:, qs])]
            rds = [kT[i], qT[i], negT[i]]
            if not far:
                pairs.append((identb.ap, strips[h].ap[:, delta + 384:delta + 384 + 512]))
                rds += [identb, strips[h]]
                sel = selN
            else:
                sel = selF
            pairs.append((sel.ap[0:33, n, :], negT[i].ap[0:33, qs]))
            rds.append(sel)
            MM(P, Sb.ap, pairs, rds, [Sb])
            pt = PT[k % 3]
            ACT(P, pt.ap, Sb.ap, AF.Exp, [Sb], [pt])
            st[k] = pt

        qcount = {"o": 0}

        def stageV(k):
            h, Q, kt = tasks[k]
            i = h % 2
            qs = slice(Q * 512, (Q + 1) * 512)
            nk = 4 * Q + 4
            pt = st.pop(k)
            if kt == 0:
                qcount["o"] += 1
            oi = qcount["o"] % 2
            OT = bank[3 + oi]; DEN = bank[5 + oi]
            P.op("pe", lambda e: e.matmul(OT.ap, vh[i].ap[:, kt, :], pt.ap, start=(kt == 0), stop=(kt == nk - 1)), [vh[i], pt], [OT])
            P.op("pe", lambda e: e.matmul(DEN.ap, onesb.ap, pt.ap, start=(kt == 0), stop=(kt == nk - 1)), [onesb, pt], [DEN])
            if kt == nk - 1:
                rd = rden[Q % 2]
                P.op("dve", lambda e: e.reciprocal(out=rd.ap, in_=DEN.ap), [DEN], [rd])
                TT(P, ost[i][Q].ap, OT.ap, rd.ap, ALU.mult, [OT, rd], [ost[i][Q]])
                P.dma("sync", T.oT[h * 128:(h + 1) * 128, qs], ost[i][Q].ap, reads=[ost[i][Q]])

        gate_plan = {}
        kbase = 0
        for h in range(8):
            nmain = 144
            if h + 1 < 8:
                for tt in range(NT):
                    gate_plan.setdefault(kbase + 4 + tt * 4, []).append(("a", h + 1, tt))
                    gate_plan.setdefault(kbase + 4 + tt * 4 + 10, []).append(("b", h + 1, tt))
                gate_plan.setdefault(kbase + 1, []).append(("load", h + 1, 0))
                gate_plan.setdefault(kbase + 2, []).append(("first", h + 1, 0))
            kbase += nmain
        for k in range(len(tasks) + 1):
            if k < len(tasks):
                stageS(k)
            if k >= 1:
                stageV(k - 1)
            for (kind, hh, tt) in gate_plan.get(k, []):
                if kind == "load":
                    load_head(hh)
                elif kind == "first":
                    gate_first(hh)
                elif kind == "a":
                    gate_a(hh, tt)
                else:
                    gate_b(hh, tt)

    def phase_D():
        P.barrier(); A.reset()
        WA = A.alloc("WA", [128, 8, 1024], BF16); WB = A.alloc("WB", [128, 8, 1024], BF16); WO = A.alloc("WO", [128, 8, 1024], BF16)
        P.dma("pool", WA.ap, _r3(T.w_pa), writes=[WA]); P.dma("pool", WB.ap, _r3(T.w_pb), writes=[WB])
        P.dma("pool", WO.ap, _r3(T.w_out), writes=[WO])
        lg = A.alloc("lg", [128, 1024]); lb = A.alloc("lb", [128, 1024])
        P.dma("sync", lg.ap, bc(T.ln1_g), writes=[lg]); P.dma("sync", lb.ap, bc(T.ln1_b), writes=[lb])
        identb = A.alloc("identb", [128, 128], BF16)
        P.dma("pool", identb.ap, T.ident, writes=[identb])
        epsb = A.alloc("eps", [128, 1])
        P.op("dve", lambda e: e.memset(epsb.ap, LN_EPS), [], [epsb])
        blk = {nm: [A.alloc("%s%d" % (nm, i), [128, 8, 512], BF16) for i in range(2)] for nm in ("a", "o", "ga", "gb")}
        src = {"a": T.aT, "o": T.oT, "ga": T.sgaT, "gb": T.sgbT}
        mT = [A.alloc("mT%d" % c, [128, 512], BF16) for c in range(8)]
        tmp = [A.alloc("tmp%d" % i, [128, 512]) for i in range(2)]
        tmp2 = [A.alloc("tmpb%d" % i, [128, 512]) for i in range(2)]
        xt = [A.alloc("xt%d" % i, [128, 1024]) for i in range(2)]
        t1 = [A.alloc("t1_%d" % i, [128, 1024]) for i in range(2)]
        hb = [A.alloc("hb%d" % i, [128, 1024], BF16) for i in range(2)]
        hTs = [[A.alloc("hTs%d_%d" % (i, j), [128, 8, 128], BF16) for j in range(4)] for i in range(2)]
        sm = [{"st6": A.alloc("st6_%d" % i, [128, 12]), "mv": A.alloc("mv%d" % i, [128, 2]),
               "sd": A.alloc("sd%d" % i, [128, 2])} for i in range(2)]
        bi = 0
        for tb in range(8):
            i = tb % 2
            bs_ = slice(tb * 512, (tb + 1) * 512)
            for nm in ("a", "o", "ga", "gb"):
                P.dma("sync", blk[nm][i].ap, _r3(src[nm][:, bs_]), writes=[blk[nm][i]])
            for n in range(8):
                pa = bank[bi % 4]; pb = bank[(bi + 1) % 4]; bi += 2
                MM(P, pa.ap, [(WA.ap[:, wc, n * 128:(n + 1) * 128], blk["a"][i].ap[:, wc, :]) for wc in range(8)], [WA, blk["a"][i]], [pa])
                MM(P, pb.ap, [(WB.ap[:, wc, n * 128:(n + 1) * 128], blk["o"][i].ap[:, wc, :]) for wc in range(8)], [WB, blk["o"][i]], [pb])
                ta = tmp[n % 2]; tb2 = tmp2[n % 2]
                TT(P, ta.ap, pa.ap, blk["ga"][i].ap[:, n, :], ALU.mult, [pa, blk["ga"][i]], [ta])
                TT(P, tb2.ap, pb.ap, blk["gb"][i].ap[:, n, :], ALU.mult, [pb, blk["gb"][i]], [tb2])
                TT(P, mT[n].ap, ta.ap, tb2.ap, ALU.add, [ta, tb2], [mT[n]])
            for t4 in range(4):
                tt = tb * 4 + t4
                j = tt % 2
                tsl = slice(tt * 128, (tt + 1) * 128)
                P.dma("sync", xt[j].ap, T.x[tsl, :], writes=[xt[j]])
                for half in range(2):
                    MM(P, bank[4 + half].ap, [(mT[wc].ap[:, t4 * 128:(t4 + 1) * 128], WO.ap[:, wc, half * 512:(half + 1) * 512]) for wc in range(8)],
                       mT + [WO], [bank[4 + half]])
                STT(P, t1[j].ap, xt[j].ap, ALPHA, pst[:, 4 * 512:6 * 512], ALU.mult, ALU.add, [xt[j], bank[4], bank[5]], [t1[j]])
                layer_norm_rows(P, t1[j], t1[j].ap, t1[j], lg, lb, sm[j], epsb)
                P.dma("sync", T.h32[tsl, :], t1[j].ap, reads=[t1[j]])
                CP(P, hb[j].ap, t1[j].ap, [t1[j]], [hb[j]], eng="act")
                psT = bank[6 + j]
                psT_ap = psT.ap.bitcast(BF16)[:, 0:1024]

                def emit(e, j=j, psT_ap=psT_ap):
                    ins = None
                    for c in range(8):
                        ins = e.transpose(out=psT_ap[:, c * 128:(c + 1) * 128], in_=hb[j].ap[:, c * 128:(c + 1) * 128], identity=identb.ap)
                    return ins
                P.op("pe", emit, [hb[j], identb], [psT])
                hs = hTs[i][t4]
                CP(P, hs.ap, psT_ap.rearrange("p (c t) -> p c t", c=8), [psT], [hs], eng="act")
                P.dma("sync", _r3(T.hT[:, tsl]), hs.ap, reads=[hs])

    def phase_F():
        P.barrier(); A.reset()
        Wp = A.alloc("Wp", [128, 2, 1024], BF16); Wg = A.alloc("Wg", [128, 8, 1024], BF16)
        P.dma("pool", Wp.ap, _r3(T.w_ple), writes=[Wp]); P.dma("pool", Wg.ap, _r3(T.w_pg), writes=[Wg])
        pTb = [A.alloc("pTb%d" % c, [128, S], BF16) for c in range(2)]
        for c in range(2):
            P.dma("pool", pTb[c].ap, T.pT[c * 128:(c + 1) * 128, :], writes=[pTb[c]])
        bg = A.alloc("bg", [128, 1024]); lg = A.alloc("lg", [128, 1024]); lb = A.alloc("lb", [128, 1024])
        P.dma("sync", bg.ap, bc(T.b_pg), writes=[bg])
        P.dma("sync", lg.ap, bc(T.ln2_g), writes=[lg]); P.dma("sync", lb.ap, bc(T.ln2_b), writes=[lb])
        epsb = A.alloc("eps", [128, 1])
        P.op("dve", lambda e: e.memset(epsb.ap, LN_EPS), [], [epsb])
        hTb = [A.alloc("hTb%d" % i, [128, 8, 512], BF16) for i in range(2)]
        ht = [A.alloc("ht%d" % i, [128, 1024]) for i in range(2)]
        ft = [A.alloc("ft%d" % i, [128, 1024]) for i in range(2)]
        t1 = [A.alloc("t1_%d" % i, [128, 1024]) for i in range(2)]
        t2 = [A.alloc("t2_%d" % i, [128, 1024]) for i in range(2)]
        sm = [{"st6": A.alloc("st6_%d" % i, [128, 12]), "mv": A.alloc("mv%d" % i, [128, 2]),
               "sd": A.alloc("sd%d" % i, [128, 2])} for i in range(2)]
        for tt in range(NT):
            tb, t4 = tt // 4, tt % 4
            j = tt % 2
            tsl = slice(tt * 128, (tt + 1) * 128)
            if t4 == 0:
                P.dma("sync", hTb[tb % 2].ap, _r3(T.hT[:, tb * 512:(tb + 1) * 512]), writes=[hTb[tb % 2]])
            hblk = hTb[tb % 2]
            P.dma("sync", ht[j].ap, T.h32[tsl, :], writes=[ht[j]])
            P.dma("sync", ft[j].ap, T.f32[tsl, :], writes=[ft[j]])
            b0 = 4 * j
            for half in range(2):
                MM(P, bank[b0 + half].ap, [(pTb[kc].ap[:, tsl], Wp.ap[:, kc, half * 512:(half + 1) * 512]) for kc in range(2)],
                   pTb + [Wp], [bank[b0 + half]])
                MM(P, bank[b0 + 2 + half].ap, [(hblk.ap[:, dc, t4 * 128:(t4 + 1) * 128], Wg.ap[:, dc, half * 512:(half + 1) * 512]) for dc in range(8)],
                   [hblk, Wg], [bank[b0 + 2 + half]])
            TT(P, t2[j].ap, pst[:, (b0 + 2) * 512:(b0 + 4) * 512], bg.ap, ALU.add, [bank[b0 + 2], bank[b0 + 3], bg], [t2[j]])
            ACT(P, t2[j].ap, t2[j].ap, AF.Sigmoid, [t2[j]], [t2[j]])
            TT(P, t2[j].ap, t2[j].ap, pst[:, b0 * 512:(b0 + 2) * 512], ALU.mult, [t2[j], bank[b0], bank[b0 + 1]], [t2[j]])
            STT(P, t1[j].ap, ht[j].ap, ALPHA, ft[j].ap, ALU.mult, ALU.add, [ht[j], ft[j]], [t1[j]])
            TT(P, t1[j].ap, t1[j].ap, t2[j].ap, ALU.add, [t1[j], t2[j]], [t1[j]])
            layer_norm_rows(P, t1[j], t1[j].ap, t1[j], lg, lb, sm[j], epsb)
            P.dma("sync", T.out[tsl, :], t1[j].ap, reads=[t1[j]])

    def phase_E_stub():
        P.barrier(); A.reset()
        z = A.alloc("z", [128, 1024])
        P.op("dve", lambda e: e.memset(z.ap, 0.0), [], [z])
        for tt in range(NT):
            P.dma("sync", T.f32[tt * 128:(tt + 1) * 128, :], z.ap, reads=[z])

    def phase_E0():
        for r in range(16):
            P.dma("pool", T.uTb[r * 64:(r + 1) * 64, :], T.uT[r * 64:(r + 1) * 64, :])
        for r in range(16):
            P.dma("pool", T.vtb[r * 1024:(r + 1) * 1024, :], T.vtab[r * 1024:(r + 1) * 1024, :])

    def phase_E1():
        P.barrier(); A.reset()
        hTb = [A.alloc("hTb%d" % c, [128, S], BF16) for c in range(8)]
        for c in range(8):
            P.dma("sync", hTb[c].ap, T.hT[c * 128:(c + 1) * 128, :], writes=[hTb[c]])
        wb = [A.alloc("wb%d" % i, [128, 8, 512], BF16) for i in range(2)]
        stg = [[A.alloc("stg%d_%d" % (i, j), [128, 512], BF16) for j in range(8)] for i in range(2)]
        si = bi = 0
        for cg in range(4):
            w = wb[cg % 2]
            P.dma("pool", w.ap, _r3(T.w_q[:, cg * 512:(cg + 1) * 512]), writes=[w])
            for f4 in range(4):
                fc = cg * 4 + f4
                st = stg[si % 2]; si += 1
                for tb in range(8):
                    ps = bank[bi % 4]; bi += 1
                    MM(P, ps.ap, [(w.ap[:, dc, f4 * 128:(f4 + 1) * 128], hTb[dc].ap[:, tb * 512:(tb + 1) * 512]) for dc in range(8)],
                       [w] + hTb, [ps])
                    if tb % 2 == 0:
                        CP(P, st[tb].ap, ps.ap, [ps], [st[tb]], eng="act")
                    else:
                        CP(P, st[tb].ap, ps.ap, [ps], [st[tb]], eng="dve")
                for tb in range(8):
                    P.dma("sync", T.qpT[fc * 128:(fc + 1) * 128, tb * 512:(tb + 1) * 512], st[tb].ap, reads=[st[tb]])

    def phase_E2():
        P.barrier(); A.reset()
        I32 = mybir.dt.int32
        U32 = mybir.dt.uint32
        skb = A.alloc("skb", [128, 2, 128], BF16)
        P.dma("pool", skb.ap, T.skT, writes=[skb])
        identb = A.alloc("identb", [128, 128], BF16)
        P.dma("pool", identb.ap, T.ident, writes=[identb])
        ident32 = A.alloc("ident32", [128, 128])
        P.dma("sync", ident32.ap, T.ident, writes=[ident32])
        iot_i = A.alloc("iot_i", [128, 128], I32)
        iot_f = A.alloc("iot_f", [128, 128])
        P.op("pool", lambda e: e.iota(iot_i.ap, pattern=[[1, 128]], base=0, channel_multiplier=0), [], [iot_i])
        P.op("pool", lambda e: e.tensor_copy(out=iot_f.ap, in_=iot_i.ap), [iot_i], [iot_f])
        qpb = A.alloc("qpb", [128, 16, 128], BF16)
        Ssb = A.alloc("Ssb", [128, 16, 128])
        S2 = [A.alloc("S2_%d" % i, [128, 128]) for i in range(4)]
        top = A.alloc("top", [128, 16, 16])
        idx_u = A.alloc("idx_u", [128, 8, 16], U32)
        topc = [Buf("topc%d" % c, top.ap[:, c, :]) for c in range(16)]
        Ssbc = [Buf("Ssbc%d" % c, Ssb.ap[:, c, :]) for c in range(16)]
        idxc = [Buf("idxc%d" % h, idx_u.ap[:, h, :]) for h in range(8)]
        idx_f = A.alloc("idx_f", [128, 128])
        idxT = A.alloc("idxT", [128, 128])
        cand2 = [A.alloc("cand2_%d" % i, [128, 256]) for i in range(2)]
        best = A.alloc("best", [128, 8, 16])
        eb = A.alloc("eb", [128, 8, 16])
        Z = A.alloc("Z", [128, 8]); lnZ = A.alloc("lnZ", [128, 8])
        wp = A.alloc("wp", [128, 8, 16]); taup = A.alloc("taup", [128, 8])
        X = [A.alloc("X%d" % i, [128, 16, 128]) for i in range(2)]
        cand = X[1]
        cand_ap = X[1].ap.rearrange("p a b -> p (a b)").rearrange("p (h c) -> p h c", h=8)
        Rall = A.alloc("Rall", [128, 128, 128], BF16)
        Rtm = [Buf("Rtm%d" % h, Rall.ap[:, h * 16:(h + 1) * 16, :]) for h in range(8)]
        Lsm = [A.alloc("Lsm%d" % i, [128, 128, 128], BF16) for i in range(2)]
        Rsm = A.alloc("Rsm", [128, 128, 128], BF16)
        Cst = [A.alloc("Cst%d" % i, [128, 16, 128], BF16) for i in range(2)]
        topv = top.ap.rearrange("p (h two) a -> p h two a", two=2)
        Sv = Ssb.ap.rearrange("p (h two) n -> p h two n", two=2)
        b0t = Buf("b0t", bank[0].ap)
        cnt = {"sc": 0, "tr": 0, "cp": 0, "cs": 0}

        def stage_scores(tt):
            tsl = slice(tt * 128, (tt + 1) * 128)
            P.dma("sync", qpb.ap, T.qpT[:, tsl].rearrange("(c k) t -> k c t", k=128), writes=[qpb])
            for cg in range(4):
                bk = bank[cnt["sc"] % 2]; cnt["sc"] += 1

                def emit(e, bk=bk, cg=cg):
                    ins = None
                    for cl in range(4):
                        c = cg * 4 + cl
                        ins = e.matmul(bk.ap[:, cl * 128:(cl + 1) * 128], qpb.ap[:, c, :], skb.ap[:, c % 2, :], start=True, stop=True)
                    return ins
                P.op("pe", emit, [qpb, skb], [bk])
                CP(P, Ssb.ap[:, cg * 4:(cg + 1) * 4, :], bk.ap.rearrange("p (c n) -> p c n", c=4), [bk], Ssbc[cg * 4:(cg + 1) * 4], eng="act")

        def stage_route1(tt):
            for c0 in range(0, 16, 4):
                cs_ = list(range(c0, c0 + 4))
                for c in cs_:
                    P.op("dve", lambda e, c=c: e.max(out=topc[c].ap[:, 0:8], in_=Ssbc[c].ap), [Ssbc[c]], [topc[c]])
                for c in cs_:
                    s2 = S2[c % 4]
                    P.op("dve", lambda e, c=c, s2=s2: e.match_replace(out=s2.ap, in_to_replace=topc[c].ap[:, 0:8], in_values=Ssbc[c].ap, imm_value=NEG),
                         [Ssbc[c], topc[c]], [s2])
                for c in cs_:
                    s2 = S2[c % 4]
                    P.op("dve", lambda e, c=c, s2=s2: e.max(out=topc[c].ap[:, 8:16], in_=s2.ap), [s2, topc[c]], [topc[c]])
                for c in cs_:
                    if c % 2 == 0:
                        h = c // 2
                        P.op("dve", lambda e, c=c, h=h: e.max_index(out=idxc[h].ap[:, 0:8], in_max=topc[c].ap[:, 0:8], in_values=Ssbc[c].ap),
                             [Ssbc[c], topc[c]], [idxc[h]])
                        P.op("dve", lambda e, c=c, h=h: e.max_index(out=idxc[h].ap[:, 8:16], in_max=topc[c].ap[:, 8:16], in_values=Ssbc[c].ap),
                             [Ssbc[c], topc[c], idxc[h]], [idxc[h]])
            TT(P, cand_ap.rearrange("p h (a b) -> p h a b", a=16),
               topv[:, :, 0, :].unsqueeze(3).broadcast_to([128, 8, 16, 16]),
               topv[:, :, 1, :].unsqueeze(2).broadcast_to([128, 8, 16, 16]), ALU.add, topc, [cand])
            for h in range(8):
                c2 = cand2[h % 2]
                P.op("dve", lambda e, h=h: e.max(out=best.ap[:, h, 0:8], in_=cand_ap[:, h, :]), [cand], [best])
                P.op("dve", lambda e, h=h, c2=c2: e.match_replace(out=c2.ap, in_to_replace=best.ap[:, h, 0:8], in_values=cand_ap[:, h, :], imm_value=NEG),
                     [cand, best], [c2])
                P.op("dve", lambda e, h=h, c2=c2: e.max(out=best.ap[:, h, 8:16], in_=c2.ap), [c2], [best])
            CP(P, idx_f.ap, idx_u.ap.rearrange("p h a -> p (h a)"), idxc, [idx_f])

        def stage_route2(tt):
            ACT(P, eb.ap, best.ap, AF.Exp, [best], [eb])
            P.op("dve", lambda e: e.tensor_reduce(out=Z.ap, in_=eb.ap, axis=AX.X, op=ALU.add), [eb], [Z])
            ACT(P, lnZ.ap, Z.ap, AF.Ln, [Z], [lnZ])
            TT(P, wp.ap, topv[:, :, 0, :], lnZ.ap.unsqueeze(2).broadcast_to([128, 8, 16]), ALU.subtract, topc + [lnZ], [wp])
            STT(P, taup.ap, best.ap[:, :, 15], -1e-5, lnZ.ap, ALU.add, ALU.subtract, [best, lnZ], [taup])

        def stage_idxT(tt):
            P.op("pe", lambda e: e.transpose(out=b0t.ap[:, 0:128], in_=idx_f.ap, identity=ident32.ap), [idx_f, ident32], [b0t, bank[0]])
            CP(P, idxT.ap, b0t.ap[:, 0:128], [b0t, bank[0]], [idxT], eng="act")

        def stage_buildL(tt):
            L = Lsm[tt % 2]
            TT(P, L.ap, iot_f.ap.unsqueeze(1).broadcast_to([128, 128, 128]),
               idxT.ap.unsqueeze(2).broadcast_to([128, 128, 128]), ALU.is_equal, [iot_f, idxT], [L])

        def stage_buildR(tt):
            def fin(h):
                x = X[h % 2]
                STT(P, Rtm[h].ap, x.ap, taup.ap[:, h:h + 1], Rtm[h].ap, ALU.is_ge, ALU.mult, [x, taup, Rtm[h]], [Rtm[h]])
            for h in range(8):
                x = X[h % 2]
                TT(P, x.ap, Sv[:, h, 1, :].unsqueeze(1).broadcast_to([128, 16, 128]),
                   wp.ap[:, h, :].unsqueeze(2).broadcast_to([128, 16, 128]), ALU.add, [Ssbc[2 * h + 1], wp], [x])
                ACT(P, Rtm[h].ap, x.ap, AF.Exp, [x], [Rtm[h]])
                if h >= 1:
                    fin(h - 1)
            fin(7)

        def stage_trR(tt):
            for ig in range(16):
                bk = bank[2 + cnt["tr"] % 2]; cnt["tr"] += 1
                bkv = bk.ap.bitcast(BF16)[:, 0:1024]

                def emit(e, bkv=bkv, ig=ig):
                    ins = None
                    for q in range(8):
                        i_ = ig * 8 + q
                        ins = e.transpose(out=bkv[:, q * 128:(q + 1) * 128], in_=Rall.ap[:, :, i_], identity=identb.ap)
                    return ins
                P.op("pe", emit, Rtm + [identb], [bk])
                CP(P, Rsm.ap[:, ig * 8:(ig + 1) * 8, :], bkv.rearrange("p (i t) -> p i t", i=8), [bk], [Rsm], eng="act")

        def stage_cmm(tt):
            L = Lsm[tt % 2]
            for t16 in range(8):
                cs = Cst[cnt["cs"] % 2]; cnt["cs"] += 1
                for t4 in range(4):
                    bk = bank[4 + cnt["cp"] % 4]; cnt["cp"] += 1
                    tb_ = t16 * 16 + t4 * 4

                    def emit(e, bk=bk, tb_=tb_):
                        ins = None
                        for q in range(4):
                            t = tb_ + q
                            ins = e.matmul(bk.ap[:, q * 128:(q + 1) * 128], Rsm.ap[:, :, t], L.ap[:, t, :], start=True, stop=True)
                        return ins
                    P.op("pe", emit, [Rsm, L], [bk])
                    CP(P, cs.ap[:, t4 * 4:(t4 + 1) * 4, :], bk.ap.rearrange("p (t i) -> p t i", t=4), [bk], [cs], eng="act")
                P.dma("sync", T.Cs[tt][:, t16 * 16:(t16 + 1) * 16, :], cs.ap, reads=[cs])

        stage_scores(0)
        stage_route1(0)
        stage_route2(0)
        stage_idxT(0)
        stage_buildR(0)
        stage_buildL(0)
        for tt in range(NT):
            if tt + 1 < NT:
                stage_scores(tt + 1)
                stage_route1(tt + 1)
            stage_trR(tt)
            if tt + 1 < NT:
                stage_route2(tt + 1)
                stage_idxT(tt + 1)
            stage_cmm(tt)
            if tt + 1 < NT:
                stage_buildR(tt + 1)
                stage_buildL(tt + 1)

    def phase_E3():
        P.barrier(); A.reset()
        TB = 256
        hTb = [A.alloc("hTb%d" % i, [128, 8, TB], BF16) for i in range(2)]
        ysb = [A.alloc("ysb0", [128, 1024])] * 2
        NG = 4
        gl = [A.alloc("glx%d" % i, [128, TB]) for i in range(NG)]
        G = [A.alloc("Gx%d" % i, [128, TB], BF16) for i in range(NG)]
        NB = 4
        Ug = [A.alloc("Ugx%d" % i, [128, 8, 256], BF16) for i in range(NB)]
        Vg = [A.alloc("Vgx%d" % i, [128, 2, 1024], BF16) for i in range(NB)]
        Cp = [A.alloc("Cp%d" % i, [128, 2, 128, 128], BF16) for i in range(2)]
        NBLK = S // TB
        tasks = [(blk, c) for blk in range(NBLK) for c in range(128)]
        state = {}

        def load_group(gi_):
            blk, cg = gi_ // 64, gi_ % 64
            if blk >= NBLK:
                return
            ug = Ug[gi_ % NB]; vg = Vg[gi_ % NB]
            P.dma("sync", ug.ap, _r3(T.uTb[:, cg * 256:(cg + 1) * 256]), writes=[ug])
            P.dma("sync", vg.ap, T.vtb[cg * 256:(cg + 1) * 256, :].rearrange("(c p) d -> p c d", p=128), writes=[vg])

        def load_block(blk):
            if blk >= NBLK:
                return
            hb = hTb[blk % 2]
            P.dma("sync", hb.ap, _r3(T.hT[:, blk * TB:(blk + 1) * TB]), writes=[hb])
            cp = Cp[blk % 2]
            for tl in range(2):
                P.dma("pool", cp.ap[:, tl], T.Cs[blk * 2 + tl], writes=[cp])

        load_block(0)
        load_group(0)
        load_group(1)
        load_group(2)

        def stage1(k):
            blk, c = tasks[k]
            hb = hTb[blk % 2]
            cp = Cp[blk % 2]
            cg, cl = c // 2, c % 2
            gi_ = blk * 64 + cg
            ug = Ug[gi_ % NB]; vg = Vg[gi_ % NB]
            hp = bank[k % 4]
            MM(P, hp.ap[:, 0:TB], [(ug.ap[:, dc, cl * 128:(cl + 1) * 128], hb.ap[:, dc, :]) for dc in range(8)], [ug, hb], [hp])
            g1 = gl[k % NG]; g2 = G[k % NG]
            ACT(P, g1.ap, hp.ap[:, 0:TB], AF.Gelu, [hp], [g1])
            TT(P, g2.ap.rearrange("p (a t) -> p a t", a=2), g1.ap.rearrange("p (a t) -> p a t", a=2), cp.ap[:, :, :, c],
               ALU.mult, [g1, cp], [g2])
            state[k] = (g2, vg, cl)

        def stage2(k):
            blk, c = tasks[k]
            g2, vg, cl = state.pop(k)
            for tl in range(2):
                for half in range(2):
                    yb = bank[4 + tl * 2 + half]
                    P.op("pe", lambda e, yb=yb, tl=tl, half=half:
                         e.matmul(yb.ap, g2.ap[:, tl * 128:(tl + 1) * 128], vg.ap[:, cl, half * 512:(half + 1) * 512],
                                  start=(c == 0), stop=(c == 127)), [g2, vg], [yb])
            if c == 127:
                for tl in range(2):
                    tt = blk * 2 + tl
                    y = ysb[tl]
                    CP(P, y.ap, pst[:, (4 + tl * 2) * 512:(6 + tl * 2) * 512], [bank[4 + tl * 2], bank[5 + tl * 2]], [y], eng=("act" if tl else "dve"))
                    P.dma("sync", T.f32[tt * 128:(tt + 1) * 128, :], y.ap, reads=[y])

        SK = 2
        for k in range(len(tasks) + SK):
            if k < len(tasks):
                stage1(k)
            if k >= SK:
                stage2(k - SK)
            if k < len(tasks):
                blk, c = tasks[k]
                if c % 2 == 1:
                    load_group(blk * 64 + c // 2 + 3)
                if c == 4:
                    load_block(blk + 1)

    def phase_E():
        if "A" not in phases:
            phase_E0()
        phase_E1()
        phase_E2()
        phase_E3()

    if "A" in phases:
        phase_A1()
        phase_A2()
    if "C" in phases:
        phase_C()
    if "D" in phases:
        phase_D()
    if "E" in phases:
        phase_E()
    for ch, fn in (("0", phase_E0), ("1", phase_E1), ("2", phase_E2), ("3", phase_E3)):
        if ch in phases:
            fn()
    if "e" in phases:
        phase_E_stub()
    if "F" in phases:
        phase_F()
    P.barrier()
    P.finish([])
    return nc, P


def _t5_bucket_np(d):
    import math
    n = np.maximum(d, 0)
    nf = np.maximum(n, 16).astype(np.float32)
    large = 16 + (np.log(nf / np.float32(16)) / np.float32(math.log(1024 / 16)) * np.float32(16)).astype(np.int32)
    large = np.minimum(large, 31)
    return np.where(n < 16, n, large)


def _constants():
    c = {}
    c["tri"] = (np.arange(128)[:, None] <= np.arange(128)[None, :]).astype(np.float32)
    E = np.zeros((33, R_LEN), np.float32)
    y = np.arange(R_LEN)
    d = y - 511
    bk = _t5_bucket_np(d)
    for i in range(R_LEN):
        if d[i] < 0:
            E[32, i] = 1.0
        else:
            E[bk[i], i] = 1.0
    c["Emat"] = E
    sn = np.zeros((33, 16, 128), np.float32)
    sf = np.zeros((33, 16, 128), np.float32)
    for n in range(16):
        sn[n, n, :] = 1.0
        sf[n, n, :] = 1.0
        sf[32, n, :] = 1.0
    c["sel_near"] = sn
    c["sel_far"] = sf
    c["ident"] = np.eye(128, dtype=np.float32)
    c["antiid"] = np.ascontiguousarray(np.eye(128, dtype=np.float32)[::-1])
    return c


def prep_shared(inp):
    f = lambda a: np.ascontiguousarray(a, dtype=np.float32)
    sh = {}
    sh["w_in"] = f(inp["w_in"][0])
    sh["b_in_c"] = f(inp["b_in"][0].reshape(56, 128).T)
    sh["b_in"] = f(inp["b_in"][0])
    sh["gmlp_ln_g"] = f(inp["gmlp_ln_g"][0]); sh["gmlp_ln_b"] = f(inp["gmlp_ln_b"][0])
    sh["wsT"] = f(inp["gmlp_w_s"][0].transpose(2, 0, 1))
    sh["bs"] = f(inp["gmlp_b_s"][0][None])
    for k in ("w_proj_a", "w_proj_b", "w_out", "ln1_g", "ln1_b", "peer_w_q", "ple_w_proj", "ple_w_gate",
              "ple_b_gate", "ln2_g", "ln2_b"):
        sh[k] = f(inp[k][0])
    sh["rb_ext"] = f(np.concatenate([inp["rel_bias"], np.full((1, 8), NEG, np.float32)], axis=0))
    sh["skT"] = f(inp["peer_sub_keys"][0].transpose(2, 0, 1))
    sh["uT"] = f(inp["peer_u"][0].T)
    sh["vtab"] = f(inp["peer_v"][0])
    sh.update(_constants())
    return sh


def prep_core(inp, b):
    f = lambda a: np.ascontiguousarray(a, dtype=np.float32)
    return {"xT": f(inp["x"][b].T), "x": f(inp["x"][b]), "pT": f(inp["p"][0, b].T)}


PHASES = "ACDEF"


def kernel(**inputs):
    inp = {k: np.asarray(v) for k, v in inputs.items()}
    nc, P = build_program(phases=PHASES, debug=False)
    sh = prep_shared(inp)
    in_maps = [dict(sh, **prep_core(inp, b)) for b in range(8)]
    res = run_bass_kernel_spmd(nc, in_maps, core_ids=list(range(8)))
    out = np.stack([np.asarray(r["out"], dtype=np.float32) for r in res.results], axis=0)
    return out
```

```python
import contextlib
import numpy as np
import concourse.bass as bass
import concourse.mybir as mybir
from concourse.bass_utils import run_bass_kernel_spmd

F32 = mybir.dt.float32
BF16 = mybir.dt.bfloat16
AF = mybir.ActivationFunctionType
ALU = mybir.AluOpType
AX = mybir.AxisListType


class Buf:
    __slots__ = ("name", "ap", "lw", "rd", "dsem", "dcount")

    def __init__(self, name, ap=None):
        self.name = name
        self.ap = ap
        self.lw = None
        self.rd = []
        self.dsem = None
        self.dcount = 0


class Slot:
    __slots__ = ("dsem", "dcount")

    def __init__(self):
        self.dsem = None
        self.dcount = 0


class _Op:
    __slots__ = ("emit", "waits", "signal", "known", "dma_buf", "dma_val")

    def __init__(self, emit):
        self.emit = emit
        self.waits = []
        self.signal = False
        self.known = None
        self.dma_buf = None
        self.dma_val = 0


class Prog:
    ENG = {"pe": "tensor", "act": "scalar", "dve": "vector", "pool": "gpsimd", "sync": "sync"}
    ALIAS = {"gpsimd": "pool", "scalar": "act", "vector": "dve", "tensor": "pe"}

    def __init__(self, nc):
        self.nc = nc
        self.ops = {k: [] for k in self.ENG}
        self.known = {k: {} for k in self.ENG}
        self.stack = contextlib.ExitStack()
        self.dbufs = []
        self.dummy = Buf("dummy")
        self.n_ops = 0
        self.free_slots = []
        self.free_sw = []
        self.hw_slot = {}
        self.sw_slot = {}
        self.live = []

    def sbuf(self, name, shape, dtype):
        t = self.stack.enter_context(self.nc.sbuf_tensor(name, shape, dtype))
        return Buf(name, t)

    def psum(self, name, shape, dtype):
        t = self.stack.enter_context(self.nc.psum_tensor(name, shape, dtype))
        return Buf(name, t)

    def dram_buf(self, name):
        return Buf(name)

    def view(self, name, ap):
        return Buf(name, ap)

    def _deps(self, reads, writes):
        deps = []
        for b in reads:
            if b.lw is not None:
                deps.append(b.lw)
        for b in writes:
            if b.lw is not None:
                deps.append(b.lw)
            deps.extend(b.rd)
        return deps

    def _apply_waits(self, eng, op, deps):
        known = self.known[eng]
        changed = False
        for ev in deps:
            if ev[0] == "e":
                _, e2, idx = ev
                if e2 == eng and eng == "pe":
                    continue
                key = e2
                if known.get(key, -1) >= idx:
                    continue
                if not changed:
                    known = dict(known)
                    changed = True
                if e2 != eng or True:
                    op.waits.append(ev)
                    self.ops[e2][idx].signal = True
                known[key] = idx
                k2 = self.ops[e2][idx].known
                if k2:
                    for kk, vv in k2.items():
                        if known.get(kk, -1) < vv:
                            known[kk] = vv
            else:
                _, b, val = ev
                key = ("d", id(b))
                if known.get(key, -1) >= val:
                    continue
                if not changed:
                    known = dict(known)
                    changed = True
                op.waits.append(ev)
                known[key] = val
        self.known[eng] = known
        op.known = known

    def _commit(self, ev, reads, writes):
        for b in reads:
            b.rd.append(ev)
        for b in writes:
            b.lw = ev
            b.rd = []

    def op(self, eng, emit, reads=(), writes=()):
        eng = self.ALIAS.get(eng, eng)
        o = _Op(emit)
        self._apply_waits(eng, o, self._deps(reads, writes))
        idx = len(self.ops[eng])
        self.ops[eng].append(o)
        self._commit(("e", eng, idx), reads, writes)
        self.n_ops += 1
        return o

    def dma(self, q, out_ap, in_ap, reads=(), writes=(), sem_buf=None, **kw):
        q = self.ALIAS.get(q, q)
        if sem_buf is None:
            for b in list(writes) + list(reads):
                if b.ap is not None:
                    sem_buf = b
                    break
            else:
                sem_buf = self.dummy
        o = _Op(lambda e: e.dma_start(out=out_ap, in_=in_ap, **kw))
        self._apply_waits(q, o, self._deps(reads, writes))
        sw = (q == "pool")
        key = id(sem_buf)
        table = self.sw_slot if sw else self.hw_slot
        if key not in table:
            free = self.free_sw if sw else self.free_slots
            if free:
                table[key] = free.pop()
            else:
                table[key] = Slot()
                self.dbufs.append(table[key])
        sl = table[key]
        sl.dcount += 16
        o.dma_buf = sl
        o.dma_val = sl.dcount
        self.ops[q].append(o)
        self._commit(("d", sl, sl.dcount), reads, writes)
        self.n_ops += 1
        return o

    def finish(self, out_bufs):
        nc = self.nc
        fin = _Op(None)
        self._apply_waits("sync", fin, self._deps(out_bufs, ()))
        self.ops["sync"].append(fin)
        st = self.stack
        esem = {}
        for k in self.ENG:
            if any(o.signal for o in self.ops[k]):
                esem[k] = st.enter_context(nc.semaphore("e_" + k))
        for i, b in enumerate(self.dbufs):
            b.dsem = st.enter_context(nc.semaphore("d_%d" % i))
        sval = {}
        for k, ops in self.ops.items():
            c = 0
            for i, o in enumerate(ops):
                if o.signal:
                    c += 1
                    sval[(k, i)] = c
        self.max_sval = max(sval.values()) if sval else 0

        def run(k, eng):
            for i, o in enumerate(self.ops[k]):
                for ev in o.waits:
                    if ev[0] == "e":
                        eng.wait_ge(esem[ev[1]], sval[(ev[1], ev[2])])
                    else:
                        eng.wait_ge(ev[1].dsem, ev[2])
                if o.emit is None:
                    continue
                ins = o.emit(eng)
                if o.dma_buf is not None:
                    ins.then_inc(o.dma_buf.dsem, 16)
                elif o.signal:
                    ins.then_inc(esem[k], 1)

        block = st.enter_context(nc.Block())

        @block.sync
        def _(e):
            run("sync", e)

        @block.tensor
        def _(e):
            run("pe", e)

        @block.scalar
        def _(e):
            run("act", e)

        @block.vector
        def _(e):
            run("dve", e)

        @block.gpsimd
        def _(e):
            run("pool", e)

        st.close()


def _esize(dt):
    return 2 if dt == BF16 else 4


class Arena:
    def __init__(self, P, words):
        self.P = P
        self.words = words
        self.t = P.stack.enter_context(P.nc.sbuf_tensor("arena", [128, words], F32))
        self.off = 0

    def reset(self):
        self.off = 0

    def alloc(self, name, shape, dtype=F32):
        n = 1
        for s in shape[1:]:
            n *= s
        words = (n * _esize(dtype) + 3) // 4
        assert self.off + words <= self.words, (name, self.off, words, self.words)
        ap = self.t[0:shape[0], self.off:self.off + words]
        self.off += words
        if dtype == BF16:
            ap = ap.bitcast(dtype)[:, 0:n]
        elif dtype != F32:
            ap = ap.bitcast(dtype)
        if len(shape) > 2:
            names = ["a%d" % i for i in range(len(shape) - 1)]
            pat = "p (" + " ".join(names) + ") -> p " + " ".join(names)
            ap = ap.rearrange(pat, **{nm: s for nm, s in zip(names, shape[1:])})
        return Buf(name, ap)


def _barrier(self):
    evs = []
    for k in ("pe", "act", "dve", "pool"):
        ops = self.ops[k]
        for i in range(len(ops) - 1, -1, -1):
            if ops[i].dma_buf is None and ops[i].emit is not None:
                evs.append(("e", k, i))
                break
    for b in self.dbufs:
        evs.append(("d", b, b.dcount))
    self.pending = {k: list(evs) for k in self.ENG}
    self.free_slots.extend(self.hw_slot.values())
    self.free_sw.extend(self.sw_slot.values())
    self.hw_slot = {}
    self.sw_slot = {}


Prog.barrier = _barrier
_orig_apply = Prog._apply_waits


def _apply2(self, eng, op, deps):
    pend = getattr(self, "pending", None)
    if pend and pend.get(eng):
        deps = list(deps) + pend[eng]
        pend[eng] = []
    _orig_apply(self, eng, op, deps)


Prog._apply_waits = _apply2


def MM(P, out_ap, pairs, reads, writes):
    pairs = list(pairs)

    def emit(e):
        n = len(pairs)
        ins = None
        for i, (l, r) in enumerate(pairs):
            ins = e.matmul(out_ap, l, r, start=(i == 0), stop=(i == n - 1))
        return ins
    return P.op("pe", emit, reads, writes)


def ACT(P, out, in_, func, reads, writes, bias=None, scale=None):
    kw = {}
    if bias is not None:
        kw["bias"] = bias
    if scale is not None:
        kw["scale"] = scale
    return P.op("act", lambda e: e.activation(out=out, in_=in_, func=func, **kw), reads, writes)


def TT(P, out, in0, in1, op, reads, writes, eng="dve"):
    return P.op(eng, lambda e: e.tensor_tensor(out=out, in0=in0, in1=in1, op=op), reads, writes)


def TS(P, out, in0, s1, s2, op0, op1, reads, writes, eng="dve"):
    if op1 is None:
        return P.op(eng, lambda e: e.tensor_scalar(out=out, in0=in0, scalar1=s1, scalar2=None, op0=op0), reads, writes)
    return P.op(eng, lambda e: e.tensor_scalar(out=out, in0=in0, scalar1=s1, scalar2=s2, op0=op0, op1=op1), reads, writes)


def STT(P, out, in0, scalar, in1, op0, op1, reads, writes):
    return P.op("dve", lambda e: e.scalar_tensor_tensor(out=out, in0=in0, scalar=scalar, in1=in1, op0=op0, op1=op1), reads, writes)


def CP(P, out, in_, reads, writes, eng="dve"):
    if eng == "act":
        return P.op("act", lambda e: e.copy(out=out, in_=in_), reads, writes)
    return P.op(eng, lambda e: e.tensor_copy(out=out, in_=in_), reads, writes)


def layer_norm_rows(P, t1, out_ap, out_buf, g_bc, b_bc, sm, eps_ap):
    st6 = sm["st6"]
    mv = sm["mv"]
    sd = sm["sd"]
    P.op("dve", lambda e: e.bn_stats(out=st6.ap[:, 0:6], in_=t1.ap[:, 0:512]), [t1], [st6])
    P.op("dve", lambda e: e.bn_stats(out=st6.ap[:, 6:12], in_=t1.ap[:, 512:1024]), [t1, st6], [st6])
    P.op("dve", lambda e: e.bn_aggr(out=mv.ap[:, 0:2], in_=st6.ap[:, 0:12]), [st6], [mv])
    ACT(P, sd.ap[:, 0:1], mv.ap[:, 1:2], AF.Sqrt, [mv, eps_ap], [sd], bias=eps_ap.ap[:, 0:1], scale=1.0)
    P.op("dve", lambda e: e.reciprocal(out=sd.ap[:, 1:2], in_=sd.ap[:, 0:1]), [sd], [sd])
    TS(P, t1.ap, t1.ap, mv.ap[:, 0:1], sd.ap[:, 1:2], ALU.subtract, ALU.mult, [t1, mv, sd], [t1])
    TT(P, t1.ap, t1.ap, g_bc.ap, ALU.mult, [t1, g_bc], [t1])
    TT(P, out_ap, t1.ap, b_bc.ap, ALU.add, [t1, b_bc], [out_buf] if out_buf is not t1 else [t1])


S = 4096
D = 1024
NT = S // 128
ALPHA = 2.0 ** 0.25
LN_EPS = 1e-5
QSCALE = 128.0 ** -0.5
NEG = -1e30
ARENA_WORDS = 45 * 1024
STRIP_LEN = 1792
R_LEN = 1920


def _r3(ap, p=128):
    return ap.rearrange("(c p) t -> p c t", p=p)


def build_program(phases="ACDEF", debug=False):
    nc = bass.Bass("TRN2", target_bir_lowering=False)

    def din(name, shape):
        return nc.dram_tensor(name, list(shape), F32, kind="ExternalInput").ap()

    def dscr(name, shape, dt):
        kind = "ExternalOutput" if debug else "Internal"
        return nc.dram_tensor(name, list(shape), dt, kind=kind).ap()

    T = type("T", (), {})()
    T.xT = din("xT", [D, S]); T.x = din("x", [S, D]); T.pT = din("pT", [256, S])
    T.w_in = din("w_in", [D, 7168]); T.b_in_c = din("b_in_c", [128, 56]); T.b_in = din("b_in", [7168])
    T.gln_g = din("gmlp_ln_g", [D]); T.gln_b = din("gmlp_ln_b", [D])
    T.wsT = din("wsT", [128, 8, 128]); T.bs = din("bs", [1, 8, 128]); T.tri = din("tri", [128, 128])
    T.w_pa = din("w_proj_a", [D, D]); T.w_pb = din("w_proj_b", [D, D]); T.w_out = din("w_out", [D, D])
    T.ln1_g = din("ln1_g", [D]); T.ln1_b = din("ln1_b", [D])
    T.rb_ext = din("rb_ext", [33, 8]); T.Emat = din("Emat", [33, R_LEN])
    T.sel_near = din("sel_near", [33, 16, 128]); T.sel_far = din("sel_far", [33, 16, 128])
    T.ident = din("ident", [128, 128]); T.antiid = din("antiid", [128, 128])
    T.w_q = din("peer_w_q", [D, 2048]); T.skT = din("skT", [128, 2, 128])
    T.uT = din("uT", [D, 16384]); T.vtab = din("vtab", [16384, D])
    T.w_ple = din("ple_w_proj", [256, D]); T.w_pg = din("ple_w_gate", [D, D]); T.b_pg = din("ple_b_gate", [D])
    T.ln2_g = din("ln2_g", [D]); T.ln2_b = din("ln2_b", [D])
    T.out = nc.dram_tensor("out", [S, D], F32, kind="ExternalOutput").ap()
    T.guT = dscr("s_guT", [D, S], BF16); T.qTs = dscr("s_qT", [D, S], BF16); T.kTs = dscr("s_kT", [D, S], BF16)
    T.sgaT = dscr("s_sgaT", [D, S], BF16); T.sgbT = dscr("s_sgbT", [D, S], BF16)
    T.vtok = dscr("s_vtok", [S, D], BF16); T.aT = dscr("s_aT", [D, S], BF16); T.oT = dscr("s_oT", [D, S], BF16)
    T.rrow = dscr("s_rrow", [8, R_LEN], BF16)
    T.uTb = nc.dram_tensor("s_uTb", [D, 16384], BF16, kind="Internal").ap()
    T.vtb = nc.dram_tensor("s_vtb", [16384, D], BF16, kind="Internal").ap()
    T.qpT = dscr("s_qpT", [2048, S], BF16)
    T.Cs = nc.dram_tensor("s_Cs", [NT, 128, 128, 128], BF16, kind="Internal").ap()
    T.h32 = dscr("s_h32", [S, D], F32); T.hT = dscr("s_hT", [D, S], BF16); T.f32 = dscr("s_f32", [S, D], F32)

    P = Prog(nc)
    A = Arena(P, ARENA_WORDS)
    pst = P.stack.enter_context(nc.psum_tensor("psum_all", [128, 4096], F32))
    bank = [Buf("bank%d" % i, pst[:, i * 512:(i + 1) * 512]) for i in range(8)]
    OUT = Buf("OUT")

    def bc(ap1d):
        return ap1d.partition_broadcast(128)

    def load_xT():
        xTb = [A.alloc("xTb%d" % c, [128, S], BF16) for c in range(8)]
        return xTb

    def phase_A1():
        P.barrier(); A.reset()
        xTb = load_xT()
        for c in range(8):
            P.dma("pool", xTb[c].ap, T.xT[c * 128:(c + 1) * 128, :], writes=[xTb[c]])
        wb = [A.alloc("wb%d" % i, [128, 8, 512], BF16) for i in range(2)]
        stg = [[A.alloc("stg%d_%d" % (i, j), [128, 512], BF16) for j in range(8)] for i in range(2)]
        bcol = A.alloc("bcol", [128, 56], F32)
        bqs = A.alloc("bqs", [128, 8], F32)
        P.dma("sync", bcol.ap, T.b_in_c, writes=[bcol])
        TS(P, bqs.ap, bcol.ap[:, 16:24], QSCALE, None, ALU.mult, None, [bcol], [bqs])
        e0_state = {"done": False}
        groups = [("gu", 0, AF.Gelu, T.guT, None), ("q", 2048, AF.Identity, T.qTs, QSCALE),
                  ("k", 3072, AF.Identity, T.kTs, None), ("ga", 5120, AF.Sigmoid, T.sgaT, None),
                  ("gb", 6144, AF.Sigmoid, T.sgbT, None)]
        wi = si = bi = 0
        for (nm, c0, fn, dst, sc) in groups:
            for half in range(2):
                w = wb[wi % 2]; wi += 1
                P.dma("pool", w.ap, _r3(T.w_in[:, c0 + half * 512:c0 + half * 512 + 512]), writes=[w])
                if wi == 2 and "E" in phases:
                    phase_E0()
                for f4 in range(4):
                    fl = half * 4 + f4
                    fc = c0 // 128 + fl
                    st = stg[si % 2]; si += 1
                    if nm == "q":
                        bias_ap, bias_buf = bqs.ap[:, fl:fl + 1], bqs
                    else:
                        bias_ap, bias_buf = bcol.ap[:, fc:fc + 1], bcol
                    for tb in range(8):
                        ps = bank[bi % 4]; bi += 1
                        MM(P, ps.ap, [(w.ap[:, dc, f4 * 128:(f4 + 1) * 128], xTb[dc].ap[:, tb * 512:(tb + 1) * 512])
                                      for dc in range(8)], [w] + xTb, [ps])
                        ACT(P, st[tb].ap, ps.ap, fn, [ps, bias_buf], [st[tb]], bias=bias_ap,
                            scale=(sc if sc is not None else 1.0))
                    for tb in range(8):
                        P.dma("sync", dst[fl * 128:(fl + 1) * 128, tb * 512:(tb + 1) * 512], st[tb].ap, reads=[st[tb]])

    def phase_A2():
        P.barrier(); A.reset()
        xTb = load_xT()
        Wv = A.alloc("Wv", [128, 8, 1024], BF16)
        Wg = A.alloc("Wg", [128, 8, 1024], BF16)
        P.dma("pool", Wv.ap, _r3(T.w_in[:, 4096:5120]), writes=[Wv])
        P.dma("pool", Wg.ap, _r3(T.w_in[:, 1024:2048]), writes=[Wg])
        bvb = A.alloc("bvb", [128, 1024]); bvg = A.alloc("bvg", [128, 1024])
        lg = A.alloc("lg", [128, 1024]); lb = A.alloc("lb", [128, 1024])
        P.dma("sync", bvb.ap, bc(T.b_in[4096:5120]), writes=[bvb])
        P.dma("sync", bvg.ap, bc(T.b_in[1024:2048]), writes=[bvg])
        P.dma("sync", lg.ap, bc(T.gln_g), writes=[lg])
        P.dma("sync", lb.ap, bc(T.gln_b), writes=[lb])
        ws32 = A.alloc("ws32", [128, 8, 128]); tri = A.alloc("tri", [128, 128])
        wsb = A.alloc("wsb", [128, 8, 128], BF16)
        bs32 = A.alloc("bs32", [1, 8, 128]); ones32 = A.alloc("ones32", [1, 128])
        epsb = A.alloc("eps", [128, 1])
        P.dma("sync", ws32.ap, T.wsT, writes=[ws32])
        P.dma("sync", tri.ap, T.tri, writes=[tri])
        P.dma("sync", bs32.ap, T.bs, writes=[bs32])
        P.op("dve", lambda e: e.memset(ones32.ap, 1.0), [], [ones32])
        P.op("dve", lambda e: e.memset(epsb.ap, LN_EPS), [], [epsb])
        TT(P, wsb.ap, ws32.ap, tri.ap.unsqueeze(1).broadcast_to([128, 8, 128]), ALU.mult, [ws32, tri], [wsb])
        t1 = [A.alloc("t1_%d" % i, [128, 1024]) for i in range(2)]
        vn = [A.alloc("vn%d" % i, [128, 1024], BF16) for i in range(2)]
        vst = [A.alloc("vst%d" % i, [128, 1024], BF16) for i in range(2)]
        gub = [A.alloc("gub%d" % i, [128, 8, 512], BF16) for i in range(2)]
        ast = [[A.alloc("ast%d_%d" % (i, j), [128, 8, 128], BF16) for j in range(4)] for i in range(2)]
        sm = [{"st6": A.alloc("st6_%d" % i, [128, 12]), "mv": A.alloc("mv%d" % i, [128, 2]),
               "sd": A.alloc("sd%d" % i, [128, 2])} for i in range(2)]
        pend_sp = []

        def spatial(tt, v):
            tsl = slice(tt * 128, (tt + 1) * 128)
            tb, t4 = tt // 4, tt % 4
            for g in range(8):
                bk = bank[6 + g // 4]
                oap = bk.ap[:, (g % 4) * 128:(g % 4 + 1) * 128]

                def emit(e, oap=oap, g=g, v=v):
                    e.matmul(oap, v.ap[:, g * 128:(g + 1) * 128], wsb.ap[:, g, :], start=True, stop=False)
                    return e.matmul(oap, ones32.ap[0:1, :], bs32.ap[0:1, g, :], start=False, stop=True)
                P.op("pe", emit, [v, wsb, ones32, bs32], [bk])
            if t4 == 0:
                gu = gub[tb % 2]
                P.dma("sync", gu.ap, _r3(T.guT[:, tb * 512:(tb + 1) * 512]), writes=[gu])
            gu = gub[tb % 2]
            a_s = ast[tb % 2][t4]
            TT(P, a_s.ap, pst[:, 6 * 512:8 * 512].rearrange("p (g t) -> p g t", g=8), gu.ap[:, :, t4 * 128:(t4 + 1) * 128],
               ALU.mult, [bank[6], bank[7], gu], [a_s])
            P.dma("sync", _r3(T.aT[:, tsl]), a_s.ap, reads=[a_s])
        for tt in range(NT):
            tsl = slice(tt * 128, (tt + 1) * 128)
            tb, t4 = tt // 4, tt % 4
            b0 = (tt % 2) * 2
            for half in range(2):
                MM(P, bank[b0 + half].ap, [(xTb[dc].ap[:, tsl], Wv.ap[:, dc, half * 512:(half + 1) * 512]) for dc in range(8)],
                   xTb + [Wv], [bank[b0 + half]])
            vs = vst[tt % 2]
            TT(P, vs.ap, pst[:, b0 * 512:(b0 + 2) * 512], bvb.ap, ALU.add, [bank[b0], bank[b0 + 1], bvb], [vs])
            P.dma("sync", T.vtok[tsl, :], vs.ap, reads=[vs])
            for half in range(2):
                MM(P, bank[4 + half].ap, [(xTb[dc].ap[:, tsl], Wg.ap[:, dc, half * 512:(half + 1) * 512]) for dc in range(8)],
                   xTb + [Wg], [bank[4 + half]])
            t = t1[tt % 2]; v = vn[tt % 2]
            TT(P, t.ap, pst[:, 4 * 512:6 * 512], bvg.ap, ALU.add, [bank[4], bank[5], bvg], [t])
            ACT(P, t.ap, t.ap, AF.Gelu, [t], [t])
            layer_norm_rows(P, t, v.ap, v, lg, lb, sm[tt % 2], epsb)
            pend_sp.append((tt, v))
            if len(pend_sp) > 1:
                spatial(*pend_sp.pop(0))
        spatial(*pend_sp.pop(0))


    def phase_C():
        P.barrier(); A.reset()
        RR = Buf("RR")
        rb = A.alloc("rb", [33, 8]); Em = A.alloc("Em", [33, R_LEN]); rsb = A.alloc("rsb", [8, R_LEN], BF16)
        P.dma("sync", rb.ap, T.rb_ext, writes=[rb]); P.dma("sync", Em.ap, T.Emat, writes=[Em])
        for j in range(4):
            MM(P, bank[0].ap[0:8, 0:480], [(rb.ap[0:33, :], Em.ap[0:33, j * 480:(j + 1) * 480])], [rb, Em], [bank[0]])
            CP(P, rsb.ap[:, j * 480:(j + 1) * 480], bank[0].ap[0:8, 0:480], [bank[0]], [rsb])
        P.dma("sync", T.rrow, rsb.ap, reads=[rsb], writes=[RR])
        strips = [A.alloc("strip%d" % h, [128, STRIP_LEN], BF16) for h in range(8)]
        for h in range(8):
            src = bass.AP(tensor=T.rrow.tensor, offset=h * R_LEN, ap=[[1, 128], [1, STRIP_LEN]])
            P.dma("sync", strips[h].ap, src, reads=[RR], writes=[strips[h]])
        identb = A.alloc("identb", [128, 128], BF16); ident32 = A.alloc("ident32", [128, 128])
        onesb = A.alloc("onesb", [128, 128], BF16)
        selN = A.alloc("selN", [33, 16, 128], BF16); selF = A.alloc("selF", [33, 16, 128], BF16)
        P.dma("pool", identb.ap, T.antiid, writes=[identb])
        P.dma("sync", ident32.ap, T.ident, writes=[ident32])
        P.dma("pool", selN.ap, T.sel_near, writes=[selN])
        P.dma("pool", selF.ap, T.sel_far, writes=[selF])
        P.op("dve", lambda e: e.memset(onesb.ap, 1.0), [], [onesb])
        kT = [A.alloc("kT%d" % i, [128, S], BF16) for i in range(2)]
        qT = [A.alloc("qT%d" % i, [128, S], BF16) for i in range(2)]
        vh = [A.alloc("vh%d" % i, [128, NT, 128], BF16) for i in range(2)]
        negT = [A.alloc("negT%d" % i, [33, S], BF16) for i in range(2)]
        ost = [[A.alloc("ost%d_%d" % (i, j), [128, 512], BF16) for j in range(8)] for i in range(2)]
        PT = [A.alloc("PT%d" % i, [128, 512], BF16) for i in range(3)]
        rden = [A.alloc("rden%d" % i, [128, 512]) for i in range(2)]
        kmT = [A.alloc("kmT%d" % i, [128, 16], BF16) for i in range(2)]
        km32 = [A.alloc("km32_%d" % i, [128, 16]) for i in range(2)]
        g16 = [A.alloc("g16_%d" % i, [128, 16]) for i in range(2)]
        top8 = [A.alloc("top8_%d" % i, [128, 8]) for i in range(2)]
        ngt = [A.alloc("ngt%d" % i, [128, 16]) for i in range(2)]
        for i in range(2):
            P.op("dve", lambda e, i=i: e.memset(negT[i].ap[0:33, :], 0.0), [], [negT[i]])

        def load_head(h):
            i = h % 2
            P.dma("sync", kT[i].ap, T.kTs[h * 128:(h + 1) * 128, :], writes=[kT[i]])
            P.dma("sync", qT[i].ap, T.qTs[h * 128:(h + 1) * 128, :], writes=[qT[i]])
            P.dma("sync", vh[i].ap, T.vtok[:, h * 128:(h + 1) * 128].rearrange("(n p) c -> p n c", p=128), writes=[vh[i]])

        b7g = Buf("b7g", bank[7].ap[:, 256:272])
        b7t = Buf("b7t", bank[7].ap[0:16, 0:128])

        def gate_first(h):
            i = h % 2
            P.op("dve", lambda e: e.tensor_reduce(out=km32[i].ap, in_=kT[i].ap.rearrange("p (n k) -> p n k", k=256),
                                                  axis=AX.X, op=ALU.add), [kT[i]], [km32[i]])
            CP(P, kmT[i].ap, km32[i].ap, [km32[i]], [kmT[i]])
            P.op("dve", lambda e: e.tensor_copy(out=negT[i].ap[32:33, :],
                                                in_=strips[h].ap[32:33, STRIP_LEN - 1:STRIP_LEN].broadcast_to([1, S])),
                 [strips[h]], [negT[i]])

        def gate_a(h, tt):
            i = h % 2
            qb = tt // 2
            ng = ngt[tt % 4]
            if qb <= 3:
                P.op("dve", lambda e: e.memset(ng.ap, 0.0), [], [ng])
            else:
                j = tt % 2
                MM(P, b7g.ap, [(qT[i].ap[:, tt * 128:(tt + 1) * 128], kmT[i].ap)], [qT[i], kmT[i]], [b7g])
                P.op("dve", lambda e: e.memset(g16[j].ap, NEG), [], [g16[j]])
                CP(P, g16[j].ap[:, 0:qb], b7g.ap[:, 0:qb], [b7g], [g16[j]])
                P.op("dve", lambda e: e.max(out=top8[j].ap, in_=g16[j].ap), [g16[j]], [top8[j]])
                TS(P, ng.ap, g16[j].ap, top8[j].ap[:, 2:3], None, ALU.is_ge, None, [g16[j], top8[j]], [ng])
                TS(P, ng.ap, ng.ap, 1e30, -1e30, ALU.mult, ALU.add, [ng], [ng])
                P.op("dve", lambda e: e.memset(ng.ap[:, qb:16], 0.0), [ng], [ng])

        def gate_b(h, tt):
            i = h % 2
            ng = ngt[tt % 4]
            P.op("pe", lambda e: e.transpose(out=b7t.ap, in_=ng.ap, identity=ident32.ap), [ng, ident32], [b7t])
            CP(P, negT[i].ap[0:16, tt * 128:(tt + 1) * 128], b7t.ap, [b7t], [negT[i]], eng="act")

        ngt = [A.alloc("ngtx%d" % i, [128, 16]) for i in range(4)]
        load_head(0)
        gate_first(0)
        for tt in range(NT):
            gate_a(0, tt)
            if tt >= 2:
                gate_b(0, tt - 2)
        gate_b(0, NT - 2); gate_b(0, NT - 1)

        tasks = []
        for h in range(8):
            for Q in range(8):
                for kt in range(4 * Q + 4):
                    tasks.append((h, Q, kt))
        st = {}

        def stageS(k):
            h, Q, kt = tasks[k]
            i = h % 2
            qs = slice(Q * 512, (Q + 1) * 512)
            n = kt // 2
            delta = Q * 512 - kt * 128
            far = delta >= 1024
            Sb = bank[k % 3]
            pairs = [(kT[i].ap[:, kt * 128:(kt + 1) * 128], qT[i].ap[:, qs])]
            rds = [kT[i], qT[i], negT[i]]
            if not far:
                pairs.append((identb.ap, strips[h].ap[:, delta + 384:delta + 384 + 512]))
                rds += [identb, strips[h]]
                sel = selN
            else:
                sel = selF
            pairs.append((sel.ap[0:33, n, :], negT[i].ap[0:33, qs]))
            rds.append(sel)
            MM(P, Sb.ap, pairs, rds, [Sb])
            pt = PT[k % 3]
            ACT(P, pt.ap, Sb.ap, AF.Exp, [Sb], [pt])
            st[k] = pt

        qcount = {"o": 0}

        def stageV(k):
            h, Q, kt = tasks[k]
            i = h % 2
            qs = slice(Q * 512, (Q + 1) * 512)
            nk = 4 * Q + 4
            pt = st.pop(k)
            if kt == 0:
                qcount["o"] += 1
            oi = qcount["o"] % 2
            OT = bank[3 + oi]; DEN = bank[5 + oi]
            P.op("pe", lambda e: e.matmul(OT.ap, vh[i].ap[:, kt, :], pt.ap, start=(kt == 0), stop=(kt == nk - 1)), [vh[i], pt], [OT])
            P.op("pe", lambda e: e.matmul(DEN.ap, onesb.ap, pt.ap, start=(kt == 0), stop=(kt == nk - 1)), [onesb, pt], [DEN])
            if kt == nk - 1:
                rd = rden[Q % 2]
                P.op("dve", lambda e: e.reciprocal(out=rd.ap, in_=DEN.ap), [DEN], [rd])
                TT(P, ost[i][Q].ap, OT.ap, rd.ap, ALU.mult, [OT, rd], [ost[i][Q]])
                P.dma("sync", T.oT[h * 128:(h + 1) * 128, qs], ost[i][Q].ap, reads=[ost[i][Q]])

        gate_plan = {}
        kbase = 0
        for h in range(8):
            nmain = 144
            if h + 1 < 8:
                for tt in range(NT):
                    gate_plan.setdefault(kbase + 4 + tt * 4, []).append(("a", h + 1, tt))
                    gate_plan.setdefault(kbase + 4 + tt * 4 + 10, []).append(("b", h + 1, tt))
                gate_plan.setdefault(kbase + 1, []).append(("load", h + 1, 0))
                gate_plan.setdefault(kbase + 2, []).append(("first", h + 1, 0))
            kbase += nmain
        for k in range(len(tasks) + 1):
            if k < len(tasks):
                stageS(k)
            if k >= 1:
                stageV(k - 1)
            for (kind, hh, tt) in gate_plan.get(k, []):
                if kind == "load":
                    load_head(hh)
                elif kind == "first":
                    gate_first(hh)
                elif kind == "a":
                    gate_a(hh, tt)
                else:
                    gate_b(hh, tt)

    def phase_D():
        P.barrier(); A.reset()
        WA = A.alloc("WA", [128, 8, 1024], BF16); WB = A.alloc("WB", [128, 8, 1024], BF16); WO = A.alloc("WO", [128, 8, 1024], BF16)
        P.dma("pool", WA.ap, _r3(T.w_pa), writes=[WA]); P.dma("pool", WB.ap, _r3(T.w_pb), writes=[WB])
        P.dma("pool", WO.ap, _r3(T.w_out), writes=[WO])
        lg = A.alloc("lg", [128, 1024]); lb = A.alloc("lb", [128, 1024])
        P.dma("sync", lg.ap, bc(T.ln1_g), writes=[lg]); P.dma("sync", lb.ap, bc(T.ln1_b), writes=[lb])
        identb = A.alloc("identb", [128, 128], BF16)
        P.dma("pool", identb.ap, T.ident, writes=[identb])
        epsb = A.alloc("eps", [128, 1])
        P.op("dve", lambda e: e.memset(epsb.ap, LN_EPS), [], [epsb])
        blk = {nm: [A.alloc("%s%d" % (nm, i), [128, 8, 512], BF16) for i in range(2)] for nm in ("a", "o", "ga", "gb")}
        src = {"a": T.aT, "o": T.oT, "ga": T.sgaT, "gb": T.sgbT}
        mT = [A.alloc("mT%d" % c, [128, 512], BF16) for c in range(8)]
        tmp = [A.alloc("tmp%d" % i, [128, 512]) for i in range(2)]
        tmp2 = [A.alloc("tmpb%d" % i, [128, 512]) for i in range(2)]
        xt = [A.alloc("xt%d" % i, [128, 1024]) for i in range(2)]
        t1 = [A.alloc("t1_%d" % i, [128, 1024]) for i in range(2)]
        hb = [A.alloc("hb%d" % i, [128, 1024], BF16) for i in range(2)]
        hTs = [[A.alloc("hTs%d_%d" % (i, j), [128, 8, 128], BF16) for j in range(4)] for i in range(2)]
        sm = [{"st6": A.alloc("st6_%d" % i, [128, 12]), "mv": A.alloc("mv%d" % i, [128, 2]),
               "sd": A.alloc("sd%d" % i, [128, 2])} for i in range(2)]
        bi = 0
        pend_tr = []

        def do_tr(tt, j, i, t4):
            tsl = slice(tt * 128, (tt + 1) * 128)
            psT = bank[6 + j]
            psT_ap = psT.ap.bitcast(BF16)[:, 0:1024]

            def emit(e):
                ins = None
                for c in range(8):
                    ins = e.transpose(out=psT_ap[:, c * 128:(c + 1) * 128], in_=hb[j].ap[:, c * 128:(c + 1) * 128], identity=identb.ap)
                return ins
            P.op("pe", emit, [hb[j], identb], [psT])
            hs = hTs[i][t4]
            CP(P, hs.ap, psT_ap.rearrange("p (c t) -> p c t", c=8), [psT], [hs], eng="act")
            P.dma("sync", _r3(T.hT[:, tsl]), hs.ap, reads=[hs])

        for tb in range(8):
            i = tb % 2
            bs_ = slice(tb * 512, (tb + 1) * 512)
            for nm in ("a", "o", "ga", "gb"):
                P.dma("sync", blk[nm][i].ap, _r3(src[nm][:, bs_]), writes=[blk[nm][i]])
            for n in range(8):
                pa = bank[bi % 4]; pb = bank[(bi + 1) % 4]; bi += 2
                MM(P, pa.ap, [(WA.ap[:, wc, n * 128:(n + 1) * 128], blk["a"][i].ap[:, wc, :]) for wc in range(8)], [WA, blk["a"][i]], [pa])
                MM(P, pb.ap, [(WB.ap[:, wc, n * 128:(n + 1) * 128], blk["o"][i].ap[:, wc, :]) for wc in range(8)], [WB, blk["o"][i]], [pb])
                ta = tmp[n % 2]; tb2 = tmp2[n % 2]
                TT(P, ta.ap, pa.ap, blk["ga"][i].ap[:, n, :], ALU.mult, [pa, blk["ga"][i]], [ta])
                TT(P, tb2.ap, pb.ap, blk["gb"][i].ap[:, n, :], ALU.mult, [pb, blk["gb"][i]], [tb2])
                TT(P, mT[n].ap, ta.ap, tb2.ap, ALU.add, [ta, tb2], [mT[n]])
            for t4 in range(4):
                tt = tb * 4 + t4
                j = tt % 2
                tsl = slice(tt * 128, (tt + 1) * 128)
                P.dma("sync", xt[j].ap, T.x[tsl, :], writes=[xt[j]])
                for half in range(2):
                    MM(P, bank[4 + half].ap, [(mT[wc].ap[:, t4 * 128:(t4 + 1) * 128], WO.ap[:, wc, half * 512:(half + 1) * 512]) for wc in range(8)],
                       mT + [WO], [bank[4 + half]])
                STT(P, t1[j].ap, xt[j].ap, ALPHA, pst[:, 4 * 512:6 * 512], ALU.mult, ALU.add, [xt[j], bank[4], bank[5]], [t1[j]])
                layer_norm_rows(P, t1[j], t1[j].ap, t1[j], lg, lb, sm[j], epsb)
                P.dma("sync", T.h32[tsl, :], t1[j].ap, reads=[t1[j]])
                CP(P, hb[j].ap, t1[j].ap, [t1[j]], [hb[j]], eng="act")
                pend_tr.append((tt, j, i, t4))
                if len(pend_tr) > 1:
                    do_tr(*pend_tr.pop(0))
        do_tr(*pend_tr.pop(0))

    def phase_F():
        P.barrier(); A.reset()
        Wp = A.alloc("Wp", [128, 2, 1024], BF16); Wg = A.alloc("Wg", [128, 8, 1024], BF16)
        P.dma("pool", Wp.ap, _r3(T.w_ple), writes=[Wp]); P.dma("pool", Wg.ap, _r3(T.w_pg), writes=[Wg])
        pTb = [A.alloc("pTb%d" % c, [128, S], BF16) for c in range(2)]
        for c in range(2):
            P.dma("pool", pTb[c].ap, T.pT[c * 128:(c + 1) * 128, :], writes=[pTb[c]])
        bg = A.alloc("bg", [128, 1024]); lg = A.alloc("lg", [128, 1024]); lb = A.alloc("lb", [128, 1024])
        P.dma("sync", bg.ap, bc(T.b_pg), writes=[bg])
        P.dma("sync", lg.ap, bc(T.ln2_g), writes=[lg]); P.dma("sync", lb.ap, bc(T.ln2_b), writes=[lb])
        epsb = A.alloc("eps", [128, 1])
        P.op("dve", lambda e: e.memset(epsb.ap, LN_EPS), [], [epsb])
        hTb = [A.alloc("hTb%d" % i, [128, 8, 512], BF16) for i in range(2)]
        ht = [A.alloc("ht%d" % i, [128, 1024]) for i in range(2)]
        ft = [A.alloc("ft%d" % i, [128, 1024]) for i in range(2)]
        t1 = [A.alloc("t1_%d" % i, [128, 1024]) for i in range(2)]
        t2 = [A.alloc("t2_%d" % i, [128, 1024]) for i in range(2)]
        sm = [{"st6": A.alloc("st6_%d" % i, [128, 12]), "mv": A.alloc("mv%d" % i, [128, 2]),
               "sd": A.alloc("sd%d" % i, [128, 2])} for i in range(2)]
        for tt in range(NT):
            tb, t4 = tt // 4, tt % 4
            j = tt % 2
            tsl = slice(tt * 128, (tt + 1) * 128)
            if t4 == 0:
                P.dma("sync", hTb[tb % 2].ap, _r3(T.hT[:, tb * 512:(tb + 1) * 512]), writes=[hTb[tb % 2]])
            hblk = hTb[tb % 2]
            P.dma("sync", ht[j].ap, T.h32[tsl, :], writes=[ht[j]])
            P.dma("sync", ft[j].ap, T.f32[tsl, :], writes=[ft[j]])
            b0 = 4 * j
            for half in range(2):
                MM(P, bank[b0 + half].ap, [(pTb[kc].ap[:, tsl], Wp.ap[:, kc, half * 512:(half + 1) * 512]) for kc in range(2)],
                   pTb + [Wp], [bank[b0 + half]])
                MM(P, bank[b0 + 2 + half].ap, [(hblk.ap[:, dc, t4 * 128:(t4 + 1) * 128], Wg.ap[:, dc, half * 512:(half + 1) * 512]) for dc in range(8)],
                   [hblk, Wg], [bank[b0 + 2 + half]])
            TT(P, t2[j].ap, pst[:, (b0 + 2) * 512:(b0 + 4) * 512], bg.ap, ALU.add, [bank[b0 + 2], bank[b0 + 3], bg], [t2[j]])
            ACT(P, t2[j].ap, t2[j].ap, AF.Sigmoid, [t2[j]], [t2[j]])
            TT(P, t2[j].ap, t2[j].ap, pst[:, b0 * 512:(b0 + 2) * 512], ALU.mult, [t2[j], bank[b0], bank[b0 + 1]], [t2[j]])
            STT(P, t1[j].ap, ht[j].ap, ALPHA, ft[j].ap, ALU.mult, ALU.add, [ht[j], ft[j]], [t1[j]])
            TT(P, t1[j].ap, t1[j].ap, t2[j].ap, ALU.add, [t1[j], t2[j]], [t1[j]])
            layer_norm_rows(P, t1[j], t1[j].ap, t1[j], lg, lb, sm[j], epsb)
            P.dma("sync", T.out[tsl, :], t1[j].ap, reads=[t1[j]])

    def phase_E_stub():
        P.barrier(); A.reset()
        z = A.alloc("z", [128, 1024])
        P.op("dve", lambda e: e.memset(z.ap, 0.0), [], [z])
        for tt in range(NT):
            P.dma("sync", T.f32[tt * 128:(tt + 1) * 128, :], z.ap, reads=[z])

    def phase_E0():
        for r in range(16):
            P.dma("pool", T.uTb[r * 64:(r + 1) * 64, :], T.uT[r * 64:(r + 1) * 64, :])
        for r in range(16):
            P.dma("pool", T.vtb[r * 1024:(r + 1) * 1024, :], T.vtab[r * 1024:(r + 1) * 1024, :])

    def phase_E1():
        P.barrier(); A.reset()
        hTb = [A.alloc("hTb%d" % c, [128, S], BF16) for c in range(8)]
        for c in range(8):
            P.dma("sync", hTb[c].ap, T.hT[c * 128:(c + 1) * 128, :], writes=[hTb[c]])
        wb = [A.alloc("wb%d" % i, [128, 8, 512], BF16) for i in range(2)]
        stg = [[A.alloc("stg%d_%d" % (i, j), [128, 512], BF16) for j in range(8)] for i in range(2)]
        si = bi = 0
        for cg in range(4):
            w = wb[cg % 2]
            P.dma("pool", w.ap, _r3(T.w_q[:, cg * 512:(cg + 1) * 512]), writes=[w])
            for f4 in range(4):
                fc = cg * 4 + f4
                st = stg[si % 2]; si += 1
                for tb in range(8):
                    ps = bank[bi % 4]; bi += 1
                    MM(P, ps.ap, [(w.ap[:, dc, f4 * 128:(f4 + 1) * 128], hTb[dc].ap[:, tb * 512:(tb + 1) * 512]) for dc in range(8)],
                       [w] + hTb, [ps])
                    if tb % 2 == 0:
                        CP(P, st[tb].ap, ps.ap, [ps], [st[tb]], eng="act")
                    else:
                        CP(P, st[tb].ap, ps.ap, [ps], [st[tb]], eng="dve")
                for tb in range(8):
                    P.dma("sync", T.qpT[fc * 128:(fc + 1) * 128, tb * 512:(tb + 1) * 512], st[tb].ap, reads=[st[tb]])

    def phase_E2():
        P.barrier(); A.reset()
        I32 = mybir.dt.int32
        U32 = mybir.dt.uint32
        skb = A.alloc("skb", [128, 2, 128], BF16)
        P.dma("pool", skb.ap, T.skT, writes=[skb])
        identb = A.alloc("identb", [128, 128], BF16)
        P.dma("pool", identb.ap, T.ident, writes=[identb])
        ident32 = A.alloc("ident32", [128, 128])
        P.dma("sync", ident32.ap, T.ident, writes=[ident32])
        iot_i = A.alloc("iot_i", [128, 128], I32)
        iot_f = A.alloc("iot_f", [128, 128])
        P.op("pool", lambda e: e.iota(iot_i.ap, pattern=[[1, 128]], base=0, channel_multiplier=0), [], [iot_i])
        P.op("pool", lambda e: e.tensor_copy(out=iot_f.ap, in_=iot_i.ap), [iot_i], [iot_f])
        qpb = A.alloc("qpb", [128, 16, 128], BF16)
        Ssb = A.alloc("Ssb", [128, 16, 128])
        S2 = [A.alloc("S2_%d" % i, [128, 128]) for i in range(4)]
        top = A.alloc("top", [128, 16, 16])
        idx_u = A.alloc("idx_u", [128, 8, 16], U32)
        topc = [Buf("topc%d" % c, top.ap[:, c, :]) for c in range(16)]
        Ssbc = [Buf("Ssbc%d" % c, Ssb.ap[:, c, :]) for c in range(16)]
        idxc = [Buf("idxc%d" % h, idx_u.ap[:, h, :]) for h in range(8)]
        idx_f = A.alloc("idx_f", [128, 128])
        idxT = A.alloc("idxT", [128, 128])
        cand2 = [A.alloc("cand2_%d" % i, [128, 256]) for i in range(2)]
        best = A.alloc("best", [128, 8, 16])
        eb = A.alloc("eb", [128, 8, 16])
        Z = A.alloc("Z", [128, 8]); lnZ = A.alloc("lnZ", [128, 8])
        wp = A.alloc("wp", [128, 8, 16]); taup = A.alloc("taup", [128, 8])
        X = [A.alloc("X%d" % i, [128, 16, 128]) for i in range(2)]
        cand = X[1]
        cand_ap = X[1].ap.rearrange("p a b -> p (a b)").rearrange("p (h c) -> p h c", h=8)
        Rall = A.alloc("Rall", [128, 128, 128], BF16)
        Rtm = [Buf("Rtm%d" % h, Rall.ap[:, h * 16:(h + 1) * 16, :]) for h in range(8)]
        Lsm = [A.alloc("Lsm%d" % i, [128, 128, 128], BF16) for i in range(2)]
        Rsm = A.alloc("Rsm", [128, 128, 128], BF16)
        Cst = [A.alloc("Cst%d" % i, [128, 16, 128], BF16) for i in range(2)]
        topv = top.ap.rearrange("p (h two) a -> p h two a", two=2)
        Sv = Ssb.ap.rearrange("p (h two) n -> p h two n", two=2)
        b0t = Buf("b0t", bank[0].ap)
        cnt = {"sc": 0, "tr": 0, "cp": 0, "cs": 0}

        def stage_scores(tt):
            tsl = slice(tt * 128, (tt + 1) * 128)
            P.dma("sync", qpb.ap, T.qpT[:, tsl].rearrange("(c k) t -> k c t", k=128), writes=[qpb])
            for cg in range(4):
                bk = bank[cnt["sc"] % 2]; cnt["sc"] += 1

                def emit(e, bk=bk, cg=cg):
                    ins = None
                    for cl in range(4):
                        c = cg * 4 + cl
                        ins = e.matmul(bk.ap[:, cl * 128:(cl + 1) * 128], qpb.ap[:, c, :], skb.ap[:, c % 2, :], start=True, stop=True)
                    return ins
                P.op("pe", emit, [qpb, skb], [bk])
                CP(P, Ssb.ap[:, cg * 4:(cg + 1) * 4, :], bk.ap.rearrange("p (c n) -> p c n", c=4), [bk], Ssbc[cg * 4:(cg + 1) * 4], eng="act")

        def stage_route1(tt):
            for c0 in range(0, 16, 4):
                cs_ = list(range(c0, c0 + 4))
                for c in cs_:
                    P.op("dve", lambda e, c=c: e.max(out=topc[c].ap[:, 0:8], in_=Ssbc[c].ap), [Ssbc[c]], [topc[c]])
                for c in cs_:
                    s2 = S2[c % 4]
                    P.op("dve", lambda e, c=c, s2=s2: e.match_replace(out=s2.ap, in_to_replace=topc[c].ap[:, 0:8], in_values=Ssbc[c].ap, imm_value=NEG),
                         [Ssbc[c], topc[c]], [s2])
                for c in cs_:
                    s2 = S2[c % 4]
                    P.op("dve", lambda e, c=c, s2=s2: e.max(out=topc[c].ap[:, 8:16], in_=s2.ap), [s2, topc[c]], [topc[c]])
                for c in cs_:
                    if c % 2 == 0:
                        h = c // 2
                        P.op("dve", lambda e, c=c, h=h: e.max_index(out=idxc[h].ap[:, 0:8], in_max=topc[c].ap[:, 0:8], in_values=Ssbc[c].ap),
                             [Ssbc[c], topc[c]], [idxc[h]])
                        P.op("dve", lambda e, c=c, h=h: e.max_index(out=idxc[h].ap[:, 8:16], in_max=topc[c].ap[:, 8:16], in_values=Ssbc[c].ap),
                             [Ssbc[c], topc[c], idxc[h]], [idxc[h]])
            TT(P, cand_ap.rearrange("p h (a b) -> p h a b", a=16),
               topv[:, :, 0, :].unsqueeze(3).broadcast_to([128, 8, 16, 16]),
               topv[:, :, 1, :].unsqueeze(2).broadcast_to([128, 8, 16, 16]), ALU.add, topc, [cand])
            for h in range(8):
                c2 = cand2[h % 2]
                P.op("dve", lambda e, h=h: e.max(out=best.ap[:, h, 0:8], in_=cand_ap[:, h, :]), [cand], [best])
                P.op("dve", lambda e, h=h, c2=c2: e.match_replace(out=c2.ap, in_to_replace=best.ap[:, h, 0:8], in_values=cand_ap[:, h, :], imm_value=NEG),
                     [cand, best], [c2])
                P.op("dve", lambda e, h=h, c2=c2: e.max(out=best.ap[:, h, 8:16], in_=c2.ap), [c2], [best])
            CP(P, idx_f.ap, idx_u.ap.rearrange("p h a -> p (h a)"), idxc, [idx_f])

        def stage_route2(tt):
            ACT(P, eb.ap, best.ap, AF.Exp, [best], [eb])
            P.op("dve", lambda e: e.tensor_reduce(out=Z.ap, in_=eb.ap, axis=AX.X, op=ALU.add), [eb], [Z])
            ACT(P, lnZ.ap, Z.ap, AF.Ln, [Z], [lnZ])
            TT(P, wp.ap, topv[:, :, 0, :], lnZ.ap.unsqueeze(2).broadcast_to([128, 8, 16]), ALU.subtract, topc + [lnZ], [wp])
            STT(P, taup.ap, best.ap[:, :, 15], -1e-5, lnZ.ap, ALU.add, ALU.subtract, [best, lnZ], [taup])

        def stage_idxT(tt):
            P.op("pe", lambda e: e.transpose(out=b0t.ap[:, 0:128], in_=idx_f.ap, identity=ident32.ap), [idx_f, ident32], [b0t, bank[0]])
            CP(P, idxT.ap, b0t.ap[:, 0:128], [b0t, bank[0]], [idxT], eng="act")

        def stage_buildL(tt):
            L = Lsm[tt % 2]
            TT(P, L.ap, iot_f.ap.unsqueeze(1).broadcast_to([128, 128, 128]),
               idxT.ap.unsqueeze(2).broadcast_to([128, 128, 128]), ALU.is_equal, [iot_f, idxT], [L])

        def stage_buildR(tt):
            def fin(h):
                x = X[h % 2]
                STT(P, Rtm[h].ap, x.ap, taup.ap[:, h:h + 1], Rtm[h].ap, ALU.is_ge, ALU.mult, [x, taup, Rtm[h]], [Rtm[h]])
            for h in range(8):
                x = X[h % 2]
                TT(P, x.ap, Sv[:, h, 1, :].unsqueeze(1).broadcast_to([128, 16, 128]),
                   wp.ap[:, h, :].unsqueeze(2).broadcast_to([128, 16, 128]), ALU.add, [Ssbc[2 * h + 1], wp], [x])
                ACT(P, Rtm[h].ap, x.ap, AF.Exp, [x], [Rtm[h]])
                if h >= 1:
                    fin(h - 1)
            fin(7)

        def stage_trR(tt):
            for ig in range(16):
                bk = bank[2 + cnt["tr"] % 2]; cnt["tr"] += 1
                bkv = bk.ap.bitcast(BF16)[:, 0:1024]

                def emit(e, bkv=bkv, ig=ig):
                    ins = None
                    for q in range(8):
                        i_ = ig * 8 + q
                        ins = e.transpose(out=bkv[:, q * 128:(q + 1) * 128], in_=Rall.ap[:, :, i_], identity=identb.ap)
                    return ins
                P.op("pe", emit, Rtm + [identb], [bk])
                CP(P, Rsm.ap[:, ig * 8:(ig + 1) * 8, :], bkv.rearrange("p (i t) -> p i t", i=8), [bk], [Rsm], eng="act")

        def stage_cmm(tt):
            L = Lsm[tt % 2]
            for t16 in range(8):
                cs = Cst[cnt["cs"] % 2]; cnt["cs"] += 1
                for t4 in range(4):
                    bk = bank[4 + cnt["cp"] % 4]; cnt["cp"] += 1
                    tb_ = t16 * 16 + t4 * 4

                    def emit(e, bk=bk, tb_=tb_):
                        ins = None
                        for q in range(4):
                            t = tb_ + q
                            ins = e.matmul(bk.ap[:, q * 128:(q + 1) * 128], Rsm.ap[:, :, t], L.ap[:, t, :], start=True, stop=True)
                        return ins
                    P.op("pe", emit, [Rsm, L], [bk])
                    CP(P, cs.ap[:, t4 * 4:(t4 + 1) * 4, :], bk.ap.rearrange("p (t i) -> p t i", t=4), [bk], [cs], eng="act")
                P.dma("sync", T.Cs[tt][:, t16 * 16:(t16 + 1) * 16, :], cs.ap, reads=[cs])

        stage_scores(0)
        stage_route1(0)
        stage_route2(0)
        stage_idxT(0)
        stage_buildR(0)
        stage_buildL(0)
        for tt in range(NT):
            if tt + 1 < NT:
                stage_scores(tt + 1)
                stage_route1(tt + 1)
            stage_trR(tt)
            if tt + 1 < NT:
                stage_route2(tt + 1)
                stage_idxT(tt + 1)
            stage_cmm(tt)
            if tt + 1 < NT:
                stage_buildR(tt + 1)
                stage_buildL(tt + 1)

    def phase_E3():
        P.barrier(); A.reset()
        TB = 256
        hTb = [A.alloc("hTb%d" % i, [128, 8, TB], BF16) for i in range(2)]
        ysb = [A.alloc("ysb0", [128, 1024])] * 2
        NG = 4
        gl = [A.alloc("glx%d" % i, [128, TB]) for i in range(NG)]
        G = [A.alloc("Gx%d" % i, [128, TB], BF16) for i in range(NG)]
        NB = 4
        Ug = [A.alloc("Ugx%d" % i, [128, 8, 256], BF16) for i in range(NB)]
        Vg = [A.alloc("Vgx%d" % i, [128, 2, 1024], BF16) for i in range(NB)]
        Cp = [A.alloc("Cp%d" % i, [128, 2, 128, 128], BF16) for i in range(2)]
        NBLK = S // TB
        tasks = [(blk, c) for blk in range(NBLK) for c in range(128)]
        state = {}

        def load_group(gi_):
            blk, cg = gi_ // 64, gi_ % 64
            if blk >= NBLK:
                return
            ug = Ug[gi_ % NB]; vg = Vg[gi_ % NB]
            P.dma("sync", ug.ap, _r3(T.uTb[:, cg * 256:(cg + 1) * 256]), writes=[ug])
            P.dma("sync", vg.ap, T.vtb[cg * 256:(cg + 1) * 256, :].rearrange("(c p) d -> p c d", p=128), writes=[vg])

        def load_block(blk):
            if blk >= NBLK:
                return
            hb = hTb[blk % 2]
            P.dma("sync", hb.ap, _r3(T.hT[:, blk * TB:(blk + 1) * TB]), writes=[hb])
            cp = Cp[blk % 2]
            for tl in range(2):
                P.dma("pool", cp.ap[:, tl], T.Cs[blk * 2 + tl], writes=[cp])

        load_block(0)
        load_group(0)
        load_group(1)
        load_group(2)

        def stage1(k):
            blk, c = tasks[k]
            hb = hTb[blk % 2]
            cp = Cp[blk % 2]
            cg, cl = c // 2, c % 2
            gi_ = blk * 64 + cg
            ug = Ug[gi_ % NB]; vg = Vg[gi_ % NB]
            hp = bank[k % 4]
            MM(P, hp.ap[:, 0:TB], [(ug.ap[:, dc, cl * 128:(cl + 1) * 128], hb.ap[:, dc, :]) for dc in range(8)], [ug, hb], [hp])
            g1 = gl[k % NG]; g2 = G[k % NG]
            ACT(P, g1.ap, hp.ap[:, 0:TB], AF.Gelu, [hp], [g1])
            TT(P, g2.ap.rearrange("p (a t) -> p a t", a=2), g1.ap.rearrange("p (a t) -> p a t", a=2), cp.ap[:, :, :, c],
               ALU.mult, [g1, cp], [g2])
            state[k] = (g2, vg, cl)

        def stage2(k):
            blk, c = tasks[k]
            g2, vg, cl = state.pop(k)
            for tl in range(2):
                for half in range(2):
                    yb = bank[4 + tl * 2 + half]
                    P.op("pe", lambda e, yb=yb, tl=tl, half=half:
                         e.matmul(yb.ap, g2.ap[:, tl * 128:(tl + 1) * 128], vg.ap[:, cl, half * 512:(half + 1) * 512],
                                  start=(c == 0), stop=(c == 127)), [g2, vg], [yb])
            if c == 127:
                for tl in range(2):
                    tt = blk * 2 + tl
                    y = ysb[tl]
                    CP(P, y.ap, pst[:, (4 + tl * 2) * 512:(6 + tl * 2) * 512], [bank[4 + tl * 2], bank[5 + tl * 2]], [y], eng=("act" if tl else "dve"))
                    P.dma("sync", T.f32[tt * 128:(tt + 1) * 128, :], y.ap, reads=[y])

        SK = 2
        for k in range(len(tasks) + SK):
            if k < len(tasks):
                stage1(k)
            if k >= SK:
                stage2(k - SK)
            if k < len(tasks):
                blk, c = tasks[k]
                if c % 2 == 1:
                    load_group(blk * 64 + c // 2 + 3)
                if c == 4:
                    load_block(blk + 1)

    def phase_E():
        if "A" not in phases:
            phase_E0()
        phase_E1()
        phase_E2()
        phase_E3()

    if "A" in phases:
        phase_A1()
        phase_A2()
    if "C" in phases:
        phase_C()
    if "D" in phases:
        phase_D()
    if "E" in phases:
        phase_E()
    for ch, fn in (("0", phase_E0), ("1", phase_E1), ("2", phase_E2), ("3", phase_E3)):
        if ch in phases:
            fn()
    if "e" in phases:
        phase_E_stub()
    if "F" in phases:
        phase_F()
    P.barrier()
    P.finish([])
    return nc, P


def _t5_bucket_np(d):
    import math
    n = np.maximum(d, 0)
    nf = np.maximum(n, 16).astype(np.float32)
    large = 16 + (np.log(nf / np.float32(16)) / np.float32(math.log(1024 / 16)) * np.float32(16)).astype(np.int32)
    large = np.minimum(large, 31)
    return np.where(n < 16, n, large)


def _constants():
    c = {}
    c["tri"] = (np.arange(128)[:, None] <= np.arange(128)[None, :]).astype(np.float32)
    E = np.zeros((33, R_LEN), np.float32)
    y = np.arange(R_LEN)
    d = y - 511
    bk = _t5_bucket_np(d)
    for i in range(R_LEN):
        if d[i] < 0:
            E[32, i] = 1.0
        else:
            E[bk[i], i] = 1.0
    c["Emat"] = E
    sn = np.zeros((33, 16, 128), np.float32)
    sf = np.zeros((33, 16, 128), np.float32)
    for n in range(16):
        sn[n, n, :] = 1.0
        sf[n, n, :] = 1.0
        sf[32, n, :] = 1.0
    c["sel_near"] = sn
    c["sel_far"] = sf
    c["ident"] = np.eye(128, dtype=np.float32)
    c["antiid"] = np.ascontiguousarray(np.eye(128, dtype=np.float32)[::-1])
    return c


def prep_shared(inp):
    f = lambda a: np.ascontiguousarray(a, dtype=np.float32)
    sh = {}
    sh["w_in"] = f(inp["w_in"][0])
    sh["b_in_c"] = f(inp["b_in"][0].reshape(56, 128).T)
    sh["b_in"] = f(inp["b_in"][0])
    sh["gmlp_ln_g"] = f(inp["gmlp_ln_g"][0]); sh["gmlp_ln_b"] = f(inp["gmlp_ln_b"][0])
    sh["wsT"] = f(inp["gmlp_w_s"][0].transpose(2, 0, 1))
    sh["bs"] = f(inp["gmlp_b_s"][0][None])
    for k in ("w_proj_a", "w_proj_b", "w_out", "ln1_g", "ln1_b", "peer_w_q", "ple_w_proj", "ple_w_gate",
              "ple_b_gate", "ln2_g", "ln2_b"):
        sh[k] = f(inp[k][0])
    sh["rb_ext"] = f(np.concatenate([inp["rel_bias"], np.full((1, 8), NEG, np.float32)], axis=0))
    sh["skT"] = f(inp["peer_sub_keys"][0].transpose(2, 0, 1))
    sh["uT"] = f(inp["peer_u"][0].T)
    sh["vtab"] = f(inp["peer_v"][0])
    sh.update(_constants())
    return sh


def prep_core(inp, b):
    f = lambda a: np.ascontiguousarray(a, dtype=np.float32)
    return {"xT": f(inp["x"][b].T), "x": f(inp["x"][b]), "pT": f(inp["p"][0, b].T)}


PHASES = "ACDEF"


def kernel(**inputs):
    inp = {k: np.asarray(v) for k, v in inputs.items()}
    nc, P = build_program(phases=PHASES, debug=False)
    sh = prep_shared(inp)
    in_maps = [dict(sh, **prep_core(inp, b)) for b in range(8)]
    res = run_bass_kernel_spmd(nc, in_maps, core_ids=list(range(8)))
    out = np.stack([np.asarray(r["out"], dtype=np.float32) for r in res.results], axis=0)
    return out
```
